# Optimizing a Trainium2 kernel written in Bass

```python
import math
import numpy as np
import jax
import jax.numpy as jnp
from jax import lax

D_MODEL = 2048
BATCH = 4
SEQ = 2048
DEPTH = 2

EPS = 1e-6
GDN_HEADS = 8
GDN_DK = 128
GDN_DV = 128
GDN_QK = GDN_HEADS * GDN_DK
GDN_V = GDN_HEADS * GDN_DV
GDN_CONV = 4
GDN_CHUNK = 64
M2_HEADS = 16
M2_HEADDIM = 64
M2_DINNER = M2_HEADS * M2_HEADDIM
M2_GROUPS = 2
M2_DSTATE = 128
M2_CONV = 4
M2_CHUNK = 256
HG_HEADS = 8
HG_DK = 128
HG_DV = 128
HG_QK = HG_HEADS * HG_DK
HG_V = HG_HEADS * HG_DV
HG_CHUNK = 32
MB_HEADS = 8
MB_DH = 128
MB_WIDTH = MB_HEADS * MB_DH
MB_BLOCK = 256
MB_TOPK = 3
MB_QBLOCK = 16
ROPE_THETA = 500000.0
ROPE_DIM = MB_DH // 4
MEM_LEN = 256
XA_HEADS = 4
XA_DH = 128
XA_WIDTH = XA_HEADS * XA_DH
D_FF = 5632
FFN_CONV = 3
AB_IN = 2 * GDN_QK + 2 * GDN_V + 2 * GDN_HEADS + 2 * M2_DINNER + 2 * M2_GROUPS * M2_DSTATE + M2_HEADS
AB_OUT = GDN_V + M2_DINNER
CD_IN = 2 * HG_QK + 2 * HG_V + 3 * MB_WIDTH
CD_OUT = HG_V + MB_WIDTH

kernel_name = "hybrid_gdn_ssd_hgrn2_moba_block"

F32 = jnp.float32


def _split(a, sizes):
    return jnp.split(a, [int(s) for s in np.cumsum(sizes)[:-1]], axis=-1)


def _rmsnorm(x, g):
    xf = x.astype(F32)
    y = xf * lax.rsqrt(jnp.mean(xf * xf, axis=-1, keepdims=True) + EPS)
    return (y * g.astype(F32)).astype(x.dtype)


def _l2norm(x):
    xf = x.astype(F32)
    return (xf * lax.rsqrt(jnp.sum(xf * xf, axis=-1, keepdims=True) + EPS)).astype(x.dtype)


def _causal_dwconv(x, w, b=None):
    width, ch = w.shape
    y = lax.conv_general_dilated(
        x, w[:, None, :].astype(x.dtype), window_strides=(1,), padding=[(width - 1, 0)],
        dimension_numbers=('NWC', 'WIO', 'NWC'), feature_group_count=ch)
    return y if b is None else y + b.astype(x.dtype)


def _pad_seq(a, mult, axis=1):
    pad = (-a.shape[axis]) % mult
    if pad == 0:
        return a
    widths = [(0, 0)] * a.ndim
    widths[axis] = (0, pad)
    return jnp.pad(a, widths)


def _to_chunks(t, c):
    bsz, sp, h = t.shape[:3]
    return jnp.moveaxis(t.reshape((bsz, sp // c, c, h) + t.shape[3:]), 3, 1)


def _from_chunks(o):
    n, bsz, h, c, d = o.shape
    return jnp.transpose(o, (1, 0, 3, 2, 4)).reshape(bsz, n * c, h, d)


def _rope_tables(positions):
    inv = jnp.exp(-math.log(ROPE_THETA) * jnp.arange(0, ROPE_DIM, 2, dtype=F32) / ROPE_DIM)
    ang = positions.astype(F32)[..., None] * inv
    return jnp.cos(ang)[:, :, None, :], jnp.sin(ang)[:, :, None, :]


def _rotary(x, cos, sin):
    half = ROPE_DIM // 2
    xf = x.astype(F32)
    x1, x2, rest = xf[..., :half], xf[..., half:ROPE_DIM], xf[..., ROPE_DIM:]
    return jnp.concatenate([x1 * cos - x2 * sin, x2 * cos + x1 * sin, rest], axis=-1).astype(x.dtype)


def _gated_delta_rule(q, k, v, g, beta):
    bsz, seq, h, dk = q.shape
    dv = v.shape[-1]
    c = GDN_CHUNK
    q, k, v, g, beta = (_to_chunks(_pad_seq(t.astype(F32), c), c) for t in (q, k, v, g, beta))
    q = q * (dk ** -0.5)
    gc = jnp.cumsum(g, axis=-1)
    incl = jnp.tril(jnp.ones((c, c), bool))
    strict = jnp.tril(jnp.ones((c, c), bool), -1)
    diff = gc[..., :, None] - gc[..., None, :]
    decay = jnp.where(incl, jnp.exp(jnp.where(incl, diff, 0.0)), 0.0)
    kb = k * beta[..., None]
    vb = v * beta[..., None]
    lmat = jnp.where(strict, jnp.einsum('bhnid,bhnjd->bhnij', kb, k) * decay, 0.0)
    eye = jnp.eye(c, dtype=F32)
    tmat = lax.linalg.triangular_solve(eye + lmat, jnp.broadcast_to(eye, lmat.shape),
                                       left_side=True, lower=True, unit_diagonal=True)
    u = tmat @ vb
    w = tmat @ (kb * jnp.exp(gc)[..., None])
    attn = jnp.einsum('bhnid,bhnjd->bhnij', q, k) * decay
    qg = q * jnp.exp(gc)[..., None]
    gl = gc[..., -1]
    kd = k * jnp.exp(gl[..., None] - gc)[..., None]

    def step(state, xs):
        u_n, w_n, a_n, qg_n, kd_n, gl_n = xs
        v_new = u_n - jnp.einsum('bhck,bhkv->bhcv', w_n, state)
        o = jnp.einsum('bhck,bhkv->bhcv', qg_n, state) + jnp.einsum('bhij,bhjv->bhiv', a_n, v_new)
        state = state * jnp.exp(gl_n)[..., None, None] + jnp.einsum('bhck,bhcv->bhkv', kd_n, v_new)
        return state, o

    xs = tuple(jnp.moveaxis(t, 2, 0) for t in (u, w, attn, qg, kd, gl))
    _, o = lax.scan(step, jnp.zeros((bsz, h, dk, dv), F32), xs)
    return _from_chunks(o)[:, :seq]


def _ssd(xh, dt, a_head, bm, cm):
    bsz, seq, h, p = xh.shape
    grp, n = bm.shape[2], bm.shape[3]
    hg = h // grp
    ln = M2_CHUNK
    xh, dt, bm, cm = (_pad_seq(t.astype(F32), ln) for t in (xh, dt, bm, cm))
    nc = xh.shape[1] // ln
    x = xh.reshape(bsz, nc, ln, grp, hg, p)
    dtc = dt.reshape(bsz, nc, ln, grp, hg)
    bc = bm.reshape(bsz, nc, ln, grp, n)
    cc = cm.reshape(bsz, nc, ln, grp, n)
    a = dtc * a_head.astype(F32).reshape(grp, hg)
    xdt = x * dtc[..., None]
    acs = jnp.cumsum(a, axis=2)
    acs_t = jnp.moveaxis(acs, 2, -1)
    incl = jnp.tril(jnp.ones((ln, ln), bool))
    diff = acs_t[..., :, None] - acs_t[..., None, :]
    ldec = jnp.where(incl, jnp.exp(jnp.where(incl, diff, 0.0)), 0.0)
    cb = jnp.einsum('bclgn,bcsgn->bcgls', cc, bc)
    y_diag = jnp.einsum('bcghls,bcsghp->bclghp', cb[:, :, :, None] * ldec, xdt)
    alast = acs[:, :, -1]
    states = jnp.einsum('bclgn,bclgh,bclghp->bcghpn', bc, jnp.exp(alast[:, :, None] - acs), xdt)

    def step(hst, xs):
        st, al = xs
        return hst * jnp.exp(al)[..., None, None] + st, hst

    _, h_prev = lax.scan(step, jnp.zeros((bsz, grp, hg, p, n), F32),
                         (jnp.moveaxis(states, 1, 0), jnp.moveaxis(alast, 1, 0)))
    h_prev = jnp.moveaxis(h_prev, 0, 1)
    y_off = jnp.einsum('bclgn,bcghpn,bclgh->bclghp', cc, h_prev, jnp.exp(acs))
    return (y_diag + y_off).reshape(bsz, nc * ln, h, p)[:, :seq]


def _hgrn2(q, k, v, logf):
    bsz, seq, h, dk = q.shape
    dv = v.shape[-1]
    c = HG_CHUNK
    q, k, v, logf = (_to_chunks(_pad_seq(t.astype(F32), c), c) for t in (q, k, v, logf))
    b = jnp.cumsum(logf, axis=3)
    qe = q * jnp.exp(b)
    ke = k * jnp.exp(-b)
    incl = jnp.tril(jnp.ones((c, c), bool))
    amat = jnp.where(incl, jnp.einsum('bhnid,bhnjd->bhnij', qe, ke), 0.0)
    o_intra = jnp.einsum('bhnij,bhnjv->bhniv', amat, v)
    bl = b[:, :, :, -1]
    upd = jnp.einsum('bhnck,bhncv->bhnkv', k * jnp.exp(bl[:, :, :, None] - b), v)

    def step(state, xs):
        u_n, bl_n = xs
        return state * jnp.exp(bl_n)[..., None] + u_n, state

    _, s_prev = lax.scan(step, jnp.zeros((bsz, h, dk, dv), F32),
                         (jnp.moveaxis(upd, 2, 0), jnp.moveaxis(bl, 2, 0)))
    s_prev = jnp.moveaxis(s_prev, 0, 2)
    o = o_intra + jnp.einsum('bhnck,bhnkv->bhncv', qe, s_prev)
    return _from_chunks(jnp.moveaxis(o, 2, 0))[:, :seq]


def _moba(q, k, v):
    bsz, seq, h, dh = q.shape
    q, k, v = (_pad_seq(t, MB_BLOCK) for t in (q, k, v))
    sp = q.shape[1]
    nb = sp // MB_BLOCK
    qh = jnp.transpose(q, (0, 2, 1, 3))
    kblk = jnp.transpose(k, (0, 2, 1, 3)).reshape(bsz, h, nb, MB_BLOCK, dh)
    vblk = jnp.transpose(v, (0, 2, 1, 3)).reshape(bsz, h, nb, MB_BLOCK, dh)
    kmean = jnp.mean(kblk.astype(F32), axis=3)
    gate = jnp.einsum('bhtd,bhnd->bhtn', qh.astype(F32), kmean)
    qblk = jnp.arange(sp) // MB_BLOCK
    past = jnp.arange(nb)[None, :] < qblk[:, None]
    gate = jnp.where(past, gate, -jnp.inf)
    ksel = max(1, min(MB_TOPK, nb))
    _, idx = lax.top_k(gate, ksel)
    valid = jnp.arange(ksel)[None, :] < jnp.minimum(MB_TOPK, qblk)[:, None]
    scale = dh ** -0.5
    nq = sp // MB_QBLOCK
    q_x = jnp.moveaxis(qh.reshape(bsz, h, nq, MB_QBLOCK, dh), 2, 0)
    idx_x = jnp.moveaxis(idx.reshape(bsz, h, nq, MB_QBLOCK, ksel), 2, 0)
    valid_x = valid.reshape(nq, MB_QBLOCK, ksel)
    starts = jnp.arange(nq, dtype=jnp.int32) * MB_QBLOCK
    bi = jnp.arange(bsz)[:, None, None, None]
    hi = jnp.arange(h)[None, :, None, None]

    def attend(xs):
        qc, ic, vc, t0 = xs
        k_sel = kblk[bi, hi, ic]
        v_sel = vblk[bi, hi, ic]
        j = t0 // MB_BLOCK
        k_own = lax.dynamic_index_in_dim(kblk, j, axis=2, keepdims=False)
        v_own = lax.dynamic_index_in_dim(vblk, j, axis=2, keepdims=False)
        s_sel = jnp.einsum('bhqd,bhqksd->bhqks', qc, k_sel).astype(F32) * scale
        s_sel = jnp.where(vc[None, None, :, :, None], s_sel, -jnp.inf)
        s_sel = s_sel.reshape(bsz, h, MB_QBLOCK, ksel * MB_BLOCK)
        s_own = jnp.einsum('bhqd,bhsd->bhqs', qc, k_own).astype(F32) * scale
        qpos = t0 + jnp.arange(MB_QBLOCK)
        kpos = j * MB_BLOCK + jnp.arange(MB_BLOCK)
        s_own = jnp.where(kpos[None, :] <= qpos[:, None], s_own, -jnp.inf)
        p = jax.nn.softmax(jnp.concatenate([s_sel, s_own], axis=-1), axis=-1)
        p_sel = p[..., :ksel * MB_BLOCK].reshape(bsz, h, MB_QBLOCK, ksel, MB_BLOCK).astype(v.dtype)
        p_own = p[..., ksel * MB_BLOCK:].astype(v.dtype)
        return (jnp.einsum('bhqks,bhqksd->bhqd', p_sel, v_sel)
                + jnp.einsum('bhqs,bhsd->bhqd', p_own, v_own))

    o = lax.map(attend, (q_x, idx_x, valid_x, starts))
    o = jnp.moveaxis(o, 0, 2).reshape(bsz, h, sp, dh)
    return jnp.transpose(o, (0, 2, 1, 3))[:, :seq]


def _mixer_ab(h, w_in, w_out, gdn_conv_w, gdn_a_log, gdn_dt_bias, gdn_norm,
              m2_conv_w, m2_conv_b, m2_dt_bias, m2_a_log, m2_d, m2_norm):
    bsz, seq, _ = h.shape
    proj = h @ w_in
    g_qkv, g_z, g_b, g_a, m_z, m_xbc, m_dt = _split(
        proj, [2 * GDN_QK + GDN_V, GDN_V, GDN_HEADS, GDN_HEADS, M2_DINNER,
               M2_DINNER + 2 * M2_GROUPS * M2_DSTATE, M2_HEADS])
    qkv = jax.nn.silu(_causal_dwconv(g_qkv, gdn_conv_w))
    q, k, v = _split(qkv, [GDN_QK, GDN_QK, GDN_V])
    q = _l2norm(q.reshape(bsz, seq, GDN_HEADS, GDN_DK))
    k = _l2norm(k.reshape(bsz, seq, GDN_HEADS, GDN_DK))
    v = v.reshape(bsz, seq, GDN_HEADS, GDN_DV)
    beta = jax.nn.sigmoid(g_b.astype(F32))
    decay = -jnp.exp(gdn_a_log.astype(F32)) * jax.nn.softplus(g_a.astype(F32) + gdn_dt_bias.astype(F32))
    o_a = _gated_delta_rule(q, k, v, decay, beta).astype(h.dtype)
    o_a = _rmsnorm(o_a, gdn_norm) * jax.nn.silu(g_z.reshape(bsz, seq, GDN_HEADS, GDN_DV))
    xbc = jax.nn.silu(_causal_dwconv(m_xbc, m2_conv_w, m2_conv_b))
    mx, mb, mc = _split(xbc, [M2_DINNER, M2_GROUPS * M2_DSTATE, M2_GROUPS * M2_DSTATE])
    mx = mx.reshape(bsz, seq, M2_HEADS, M2_HEADDIM)
    mb = mb.reshape(bsz, seq, M2_GROUPS, M2_DSTATE)
    mc = mc.reshape(bsz, seq, M2_GROUPS, M2_DSTATE)
    dt = jax.nn.softplus(m_dt.astype(F32) + m2_dt_bias.astype(F32))
    a_head = -jnp.exp(m2_a_log.astype(F32))
    y = _ssd(mx, dt, a_head, mb, mc) + m2_d.astype(F32)[:, None] * mx.astype(F32)
    y = y.astype(h.dtype).reshape(bsz, seq, M2_DINNER) * jax.nn.silu(m_z)
    y = _rmsnorm(y.reshape(bsz, seq, M2_GROUPS, M2_DINNER // M2_GROUPS),
                 m2_norm.reshape(M2_GROUPS, M2_DINNER // M2_GROUPS)).reshape(bsz, seq, M2_DINNER)
    o = jnp.concatenate([o_a.reshape(bsz, seq, GDN_V), y], axis=-1)
    return o @ w_out


def _mixer_cd(h, cos, sin, lb, w_in, w_out, hgrn_norm, moba_qnorm, moba_knorm):
    bsz, seq, _ = h.shape
    proj = h @ w_in
    hq, hf, hi, hg, mq, mk, mv = _split(proj, [HG_QK, HG_QK, HG_V, HG_V, MB_WIDTH, MB_WIDTH, MB_WIDTH])
    q = jax.nn.silu(hq).reshape(bsz, seq, HG_HEADS, HG_DK)
    lbh = lb.astype(F32).reshape(HG_HEADS, HG_DK)
    fgate = lbh + (1.0 - lbh) * jax.nn.sigmoid(hf.astype(F32).reshape(bsz, seq, HG_HEADS, HG_DK))
    o_c = _hgrn2(q, 1.0 - fgate, hi.reshape(bsz, seq, HG_HEADS, HG_DV), jnp.log(fgate)).astype(h.dtype)
    o_c = _rmsnorm(o_c, hgrn_norm) * jax.nn.silu(hg.reshape(bsz, seq, HG_HEADS, HG_DV))
    q_d = _rotary(_rmsnorm(mq.reshape(bsz, seq, MB_HEADS, MB_DH), moba_qnorm), cos, sin)
    k_d = _rotary(_rmsnorm(mk.reshape(bsz, seq, MB_HEADS, MB_DH), moba_knorm), cos, sin)
    v_d = mv.reshape(bsz, seq, MB_HEADS, MB_DH)
    o_d = _moba(q_d, k_d, v_d)
    o = jnp.concatenate([o_c.reshape(bsz, seq, HG_V), o_d.reshape(bsz, seq, MB_WIDTH)], axis=-1)
    return o @ w_out


def _mem_attention(h, mem_n, wq, wk, wv, wo, qn, kn):
    bsz, seq, _ = h.shape
    mlen = mem_n.shape[1]
    q = _rmsnorm((h @ wq).reshape(bsz, seq, XA_HEADS, XA_DH), qn)
    k = _rmsnorm((mem_n @ wk).reshape(bsz, mlen, XA_HEADS, XA_DH), kn)
    v = (mem_n @ wv).reshape(bsz, mlen, XA_HEADS, XA_DH)
    s = jnp.einsum('bshd,bmhd->bhsm', q, k).astype(F32) * (XA_DH ** -0.5)
    p = jax.nn.softmax(s, axis=-1).astype(v.dtype)
    o = jnp.einsum('bhsm,bmhd->bshd', p, v).reshape(bsz, seq, XA_WIDTH)
    return o @ wo


def _conv_ffn(h, w_in, conv_w, conv_b, w_out):
    u = _causal_dwconv(h @ w_in, conv_w, conv_b)
    gate, val = _split(u, [D_FF, D_FF])
    return (jax.nn.silu(gate) * val) @ w_out


def setup_inputs(seed: int = 0) -> dict:
    key = jax.random.key(seed)
    ks = iter(jax.random.split(key, 48))
    ne = (DEPTH + 1) // 2
    no = DEPTH // 2

    def nrm(shape, scale):
        return jax.random.normal(next(ks), shape, F32) * scale

    def gain(shape):
        return 1.0 + 0.02 * jax.random.normal(next(ks), shape, F32)

    def dt_bias(shape):
        u = jax.random.uniform(next(ks), shape, F32)
        dt = jnp.exp(math.log(1e-3) + u * (math.log(1e-1) - math.log(1e-3)))
        return dt + jnp.log(-jnp.expm1(-dt))

    def a_log(shape):
        return jnp.log(jax.random.uniform(next(ks), shape, F32, 1.0, 16.0))

    positions = (jax.random.randint(next(ks), (BATCH, 1), 0, 1024, jnp.int32)
                 + jnp.arange(SEQ, dtype=jnp.int32)[None, :])
    return {
        'x': nrm((BATCH, SEQ, D_MODEL), 1.0),
        'mem': nrm((BATCH, MEM_LEN, D_MODEL), 1.0),
        'positions': positions,
        'norm_mix': gain((DEPTH, D_MODEL)),
        'norm_mem': gain((DEPTH, D_MODEL)),
        'norm_ffn': gain((DEPTH, D_MODEL)),
        'mem_norm': gain((D_MODEL,)),
        'xa_wq': nrm((DEPTH, D_MODEL, XA_WIDTH), D_MODEL ** -0.5),
        'xa_wk': nrm((DEPTH, D_MODEL, XA_WIDTH), D_MODEL ** -0.5),
        'xa_wv': nrm((DEPTH, D_MODEL, XA_WIDTH), D_MODEL ** -0.5),
        'xa_wo': nrm((DEPTH, XA_WIDTH, D_MODEL), XA_WIDTH ** -0.5),
        'xa_qnorm': gain((DEPTH, XA_DH)),
        'xa_knorm': gain((DEPTH, XA_DH)),
        'ffn_w_in': nrm((DEPTH, D_MODEL, 2 * D_FF), D_MODEL ** -0.5),
        'ffn_conv_w': nrm((DEPTH, FFN_CONV, 2 * D_FF), FFN_CONV ** -0.5),
        'ffn_conv_b': nrm((DEPTH, 2 * D_FF), 0.01),
        'ffn_w_out': nrm((DEPTH, D_FF, D_MODEL), D_FF ** -0.5),
        'ab_w_in': nrm((ne, D_MODEL, AB_IN), D_MODEL ** -0.5),
        'ab_w_out': nrm((ne, AB_OUT, D_MODEL), AB_OUT ** -0.5),
        'gdn_conv_w': nrm((ne, GDN_CONV, 2 * GDN_QK + GDN_V), GDN_CONV ** -0.5),
        'gdn_a_log': a_log((ne, GDN_HEADS)),
        'gdn_dt_bias': dt_bias((ne, GDN_HEADS)),
        'gdn_norm': gain((ne, GDN_DV)),
        'm2_conv_w': nrm((ne, M2_CONV, M2_DINNER + 2 * M2_GROUPS * M2_DSTATE), M2_CONV ** -0.5),
        'm2_conv_b': nrm((ne, M2_DINNER + 2 * M2_GROUPS * M2_DSTATE), 0.01),
        'm2_dt_bias': dt_bias((ne, M2_HEADS)),
        'm2_a_log': a_log((ne, M2_HEADS)),
        'm2_d': gain((ne, M2_HEADS)),
        'm2_norm': gain((ne, M2_DINNER)),
        'cd_w_in': nrm((no, D_MODEL, CD_IN), D_MODEL ** -0.5),
        'cd_w_out': nrm((no, CD_OUT, D_MODEL), CD_OUT ** -0.5),
        'hgrn_lb': nrm((DEPTH, HG_QK), 0.1),
        'hgrn_norm': gain((no, HG_DV)),
        'moba_qnorm': gain((no, MB_DH)),
        'moba_knorm': gain((no, MB_DH)),
    }


def reference(x, mem, positions, norm_mix, norm_mem, norm_ffn, mem_norm,
              xa_wq, xa_wk, xa_wv, xa_wo, xa_qnorm, xa_knorm,
              ffn_w_in, ffn_conv_w, ffn_conv_b, ffn_w_out,
              ab_w_in, ab_w_out, gdn_conv_w, gdn_a_log, gdn_dt_bias, gdn_norm,
              m2_conv_w, m2_conv_b, m2_dt_bias, m2_a_log, m2_d, m2_norm,
              cd_w_in, cd_w_out, hgrn_lb, hgrn_norm, moba_qnorm, moba_knorm):
    cos, sin = _rope_tables(positions)
    mem_n = _rmsnorm(mem, mem_norm)
    lbs = jax.nn.softmax(hgrn_lb.astype(F32), axis=0)
    lbs = jnp.cumsum(lbs, axis=0) - lbs[0]
    h = x
    for layer in range(DEPTH):
        e = layer // 2
        hn = _rmsnorm(h, norm_mix[layer])
        if layer % 2 == 0:
            mix = _mixer_ab(hn, ab_w_in[e], ab_w_out[e], gdn_conv_w[e], gdn_a_log[e], gdn_dt_bias[e],
                            gdn_norm[e], m2_conv_w[e], m2_conv_b[e], m2_dt_bias[e], m2_a_log[e],
                            m2_d[e], m2_norm[e])
        else:
            mix = _mixer_cd(hn, cos, sin, lbs[layer], cd_w_in[e], cd_w_out[e], hgrn_norm[e],
                            moba_qnorm[e], moba_knorm[e])
        h = h + mix.astype(h.dtype)
        h = h + _mem_attention(_rmsnorm(h, norm_mem[layer]), mem_n, xa_wq[layer], xa_wk[layer],
                               xa_wv[layer], xa_wo[layer], xa_qnorm[layer], xa_knorm[layer]).astype(h.dtype)
        h = h + _conv_ffn(_rmsnorm(h, norm_ffn[layer]), ffn_w_in[layer], ffn_conv_w[layer],
                          ffn_conv_b[layer], ffn_w_out[layer]).astype(h.dtype)
    return h
```

```python
import os
import numpy as np
from contextlib import ExitStack
import ml_dtypes
import concourse.bass as bass
import concourse.mybir as mybir
from concourse.bass_utils import run_bass_kernel_spmd

F32 = mybir.dt.float32
BF16 = mybir.dt.bfloat16
I32 = mybir.dt.int32
AF = mybir.ActivationFunctionType
ALU = mybir.AluOpType
AX = mybir.AxisListType

D = 2048
S = 2048
TM = 1024
TH = TM + 2
KC = 16
EPS = 1e-6
NEG = -30000.0


class Buf:
    __slots__ = ("name", "w", "r")

    def __init__(self, name):
        self.name = name
        self.w = None
        self.r = []


class T:
    def __init__(self, name, h):
        self.name = name
        self.h = h
        self.whole = Buf(name)
        self.parts = {}
        self.psum = False

    def __getitem__(self, key):
        return self.h[key]

    def p(self, key):
        if key not in self.parts:
            self.parts[key] = Buf("%s/%s" % (self.name, key))
        return (self, key)


class Mine:
    def __init__(self, slot):
        self.slot = slot

    def __getitem__(self, key):
        slot = self.slot
        return lambda e: slot.h[slot.P.hh(e)][key]


class DynSlot(T):
    def __init__(self, P, name, ap):
        T.__init__(self, name, ap)
        self.P = P
        self.mine = Mine(self)


ENGS = ("pe", "act", "dve", "pool", "sp")
SKIP_SELF = set()


class Prog:
    def __init__(self, nc):
        self.nc = nc
        self.ops = {e: [] for e in ENGS}
        self.cnt = {e: 0 for e in ENGS}
        self.waited = {e: {} for e in ENGS}
        self.dma_cnt = {}
        self.semkeys = set(ENGS)
        self.sem = {}
        self.stack = ExitStack()
        self.scopes = []
        self.n_alloc = 0
        self.rings = {}
        self._hh = None

    def sb(self, name, shape, dtype):
        self.n_alloc += 1
        nm = "%s_%d" % (name, self.n_alloc)
        st = self.scopes[-1] if self.scopes else self.stack
        return T(nm, st.enter_context(self.nc.sbuf_tensor(nm, list(shape), dtype)))

    def ps(self, name, shape, dtype):
        self.n_alloc += 1
        nm = "%s_%d" % (name, self.n_alloc)
        st = self.scopes[-1] if self.scopes else self.stack
        t = T(nm, st.enter_context(self.nc.psum_tensor(nm, list(shape), dtype)))
        t.psum = True
        return t

    def dram(self, name, shape, dtype, kind="Internal"):
        return T(name, self.nc.dram_tensor(name, list(shape), dtype, kind=kind).ap())

    def push(self):
        self.scopes.append(ExitStack())

    def pop(self):
        self.barrier()
        self.scopes.pop().close()

    def _bufs(self, item):
        if isinstance(item, T):
            return [item.whole], [item.whole] + list(item.parts.values())
        t, key = item
        t.p(key)
        return [t.parts[key]], [t.whole, t.parts[key]]

    def _deps(self, reads, writes):
        deps = {}

        def add(rec):
            if rec is not None and deps.get(rec[0], 0) < rec[1]:
                deps[rec[0]] = rec[1]
        for it in reads:
            for b in self._bufs(it)[1]:
                add(b.w)
        for it in writes:
            for b in self._bufs(it)[1]:
                add(b.w)
                for rr in b.r:
                    add(rr)
        return deps

    def _commit(self, reads, writes, rec):
        for it in reads:
            for b in self._bufs(it)[0]:
                b.r.append(rec)
                if len(b.r) > 48:
                    mx = {}
                    for k, v in b.r:
                        if mx.get(k, 0) < v:
                            mx[k] = v
                    b.r = list(mx.items())
        for it in writes:
            if isinstance(it, T):
                for b in it.parts.values():
                    b.w = rec
                    b.r = []
            for b in self._bufs(it)[0]:
                b.w = rec
                b.r = []

    def _waits(self, eng, deps, skip_self=False):
        ws = []
        for k, v in deps.items():
            if skip_self and k == eng:
                continue
            if self.waited[eng].get(k, 0) < v:
                self.waited[eng][k] = v
                ws.append((k, v))
        return ws

    def op(self, eng, fn, reads=(), writes=(), mm=False):
        isps = lambda it: (it.psum if isinstance(it, T) else it[0].psum)
        writes = list(writes) + [it for it in reads if isps(it)]
        reads = [it for it in reads if not isps(it)]
        deps = self._deps(reads, writes)
        ws = self._waits(eng, deps, skip_self=(mm or eng in SKIP_SELF))
        self.cnt[eng] += 1
        self._commit(reads, writes, (eng, self.cnt[eng]))
        self.ops[eng].append((ws, fn, (eng, 1)))

    def dma(self, eng, out, in_, reads=(), writes=(), sem="dma", ring=1, **kw):
        if ring > 1:
            i = self.rings.get(sem, 0)
            self.rings[sem] = i + 1
            sem = "%s_%d" % (sem, i % ring)
        deps = self._deps(reads, writes)
        prev = self.dma_cnt.get(sem, 0)
        if prev:
            deps[sem] = max(deps.get(sem, 0), prev)
        ws = self._waits(eng, deps)
        self.semkeys.add(sem)
        self.dma_cnt[sem] = prev + 16
        self._commit(reads, writes, (sem, self.dma_cnt[sem]))
        if callable(out):
            self.ops[eng].append((ws, lambda e: e.dma_start(out=out(e), in_=in_, **kw), (sem, 16)))
        else:
            self.ops[eng].append((ws, lambda e: e.dma_start(out=out, in_=in_, **kw), (sem, 16)))

    def custom(self, eng, fn, reads=()):
        deps = self._deps(reads, ())
        ws = self._waits(eng, deps)
        self.ops[eng].append((ws, fn, None))

    def hhreg(self, e):
        if self._hh is None:
            self._hh = e.to_reg(e.partition_id() % 2)
        return self._hh

    def dma2s(self, eng, out, in0, in1, reads=(), writes=(), sem="dma2s"):
        deps = self._deps(reads, writes)
        prev = self.dma_cnt.get(sem, 0)
        if prev:
            deps[sem] = max(deps.get(sem, 0), prev)
        ws = self._waits(eng, deps)
        self.semkeys.add(sem)
        self.dma_cnt[sem] = prev + 16
        self._commit(reads, writes, (sem, self.dma_cnt[sem]))

        def fn(e):
            r = self.hhreg(e)
            with e.If_eq(r, 0):
                e.dma_start(out=out, in_=in0).then_inc(self.sem[sem], 16)
            with e.Else():
                e.dma_start(out=out, in_=in1).then_inc(self.sem[sem], 16)
        self.ops[eng].append((ws, fn, None))

    def dma2(self, eng, out0, out1, in_, reads=(), writes=(), sem="dma2"):
        deps = self._deps(reads, writes)
        prev = self.dma_cnt.get(sem, 0)
        if prev:
            deps[sem] = max(deps.get(sem, 0), prev)
        ws = self._waits(eng, deps)
        self.semkeys.add(sem)
        self.dma_cnt[sem] = prev + 16
        self._commit(reads, writes, (sem, self.dma_cnt[sem]))

        def fn(e):
            r = self.hhreg(e)
            with e.If_eq(r, 0):
                e.dma_start(out=out0, in_=in_).then_inc(self.sem[sem], 16)
            with e.Else():
                e.dma_start(out=out1, in_=in_).then_inc(self.sem[sem], 16)
        self.ops[eng].append((ws, fn, None))

    def raw(self, eng, fn, reads=(), writes=(), sem=None, inc=16):
        deps = self._deps(reads, writes)
        ws = self._waits(eng, deps)
        self.semkeys.add(sem)
        self.dma_cnt[sem] = self.dma_cnt.get(sem, 0) + inc
        self._commit(reads, writes, (sem, self.dma_cnt[sem]))
        self.ops[eng].append((ws, fn, (sem, inc)))

    def barrier(self):
        tot = dict((e, self.cnt[e]) for e in ENGS)
        tot.update(self.dma_cnt)
        for e in ENGS:
            ws = self._waits(e, tot)
            if ws:
                self.ops[e].append((ws, None, None))

    def finish(self):
        nc = self.nc
        self.barrier()
        with ExitStack() as st:
            for k in sorted(self.semkeys):
                self.sem[k] = st.enter_context(nc.semaphore("s_" + k))
            block = st.enter_context(nc.Block())

            def runner(ename):
                def run(e):
                    for ws, fn, inc in self.ops[ename]:
                        for k, v in ws:
                            e.wait_ge(self.sem[k], v)
                        if fn is not None and inc is None:
                            fn(e)
                        elif fn is not None:
                            fn(e).then_inc(self.sem[inc[0]], inc[1])
                return run
            block.tensor(runner("pe"))
            block.scalar(runner("act"))
            block.vector(runner("dve"))
            block.gpsimd(runner("pool"))
            block.sync(runner("sp"))
        for s in self.scopes:
            s.close()
        self.stack.close()


def mm(P, out, lhsT, rhs, start, stop, reads, writes):
    P.op("pe", lambda e: e.matmul(out, lhsT=lhsT, rhs=rhs, start=start, stop=stop), reads, writes, mm=True)


def tr(P, out, in_, ident, reads, writes):
    P.op("pe", lambda e: e.transpose(out=out, in_=in_, identity=ident), reads, writes, mm=True)


def actf(P, out, in_, func, reads, writes, bias=0.0, scale=1.0):
    P.op("act", lambda e: e.activation(out=out, in_=in_, func=func, bias=bias, scale=scale), reads, writes)


def tt(P, eng, out, in0, in1, op, reads, writes):
    P.op(eng, lambda e: e.tensor_tensor(out=out, in0=in0, in1=in1, op=op), reads, writes)


def ts(P, eng, out, in0, s1, s2, op0, op1, reads, writes):
    if s2 is None:
        P.op(eng, lambda e: e.tensor_scalar(out=out, in0=in0, scalar1=s1, scalar2=None, op0=op0), reads, writes)
    else:
        P.op(eng, lambda e: e.tensor_scalar(out=out, in0=in0, scalar1=s1, scalar2=s2, op0=op0, op1=op1), reads, writes)


def stt(P, out, in0, scalar, in1, op0, op1, reads, writes):
    P.op("dve", lambda e: e.scalar_tensor_tensor(out=out, in0=in0, scalar=scalar, in1=in1, op0=op0, op1=op1), reads, writes)


def cp(P, eng, out, in_, reads, writes):
    if eng == "act":
        P.op("act", lambda e: e.copy(out=out, in_=in_), reads, writes)
    else:
        P.op(eng, lambda e: e.tensor_copy(out=out, in_=in_), reads, writes)


def recip(P, out, in_, reads, writes):
    P.op("dve", lambda e: e.reciprocal(out=out, in_=in_), reads, writes)


def memset(P, eng, ap, val, writes):
    P.op(eng, lambda e: e.memset(ap, val), (), writes)


class WS:
    def __init__(self, P, name, tiles, nslot=4, kcmax=16, ncol=128):
        self.P = P
        self.name = name
        self.tiles = tiles
        self.nslot = nslot
        self.slots = [P.sb(name + "_s", [128, kcmax, ncol], BF16) for _ in range(nslot)]
        self.issued = 0
        self.used = 0

    def _issue(self):
        ap, kc, tag = self.tiles[self.issued]
        i = self.issued % self.nslot
        slot = self.slots[i]
        self.P.dma("pool", slot[:, 0:kc, :], ap, writes=[slot], sem="%s_%d" % (self.name, i))
        self.issued += 1

    def prefetch(self):
        while self.issued < min(self.used + self.nslot, len(self.tiles)):
            self._issue()

    def next(self, tag=None):
        while self.issued < min(self.used + self.nslot, len(self.tiles)):
            self._issue()
        ap, kc, tg = self.tiles[self.used]
        assert tag is None or tg == tag, (tg, tag)
        slot = self.slots[self.used % self.nslot]
        self.used += 1
        return slot


class Ctx:
    pass


def blocks_of(n):
    out = []
    t = 0
    while t < n:
        w = min(512, n - t)
        out.append((t, w))
        t += w
    return out


def dense(C, wt, kcn, xT, blks, banks, xdep=None):
    P = C.P
    for kc in range(kcn):
        for i, (t0, w) in enumerate(blks):
            mm(P, banks[i][:, 0:w], wt[:, kc, :], xT[:, kc, t0:t0 + w], kc == 0, kc == kcn - 1,
               [wt, xdep if xdep is not None else xT.p(kc)], [banks[i]])


def rmsnorm_fm(C, h, kcn, blks, gcols, gdep, out, n_feat, ocol=0):
    P = C.P
    tot = blks[-1][0] + blks[-1][1]
    ssb = [C.misc[i] for i in range(len(blks))]
    for kc in range(kcn):
        sq = C.sq[kc % len(C.sq)]
        actf(P, sq[:, 0:tot], h[:, kc, 0:tot], AF.Square, [h.p(kc)], [sq])
        for i, (t0, w) in enumerate(blks):
            mm(P, ssb[i][:, 0:w], C.ones_bf[:, :], sq[:, t0:t0 + w], kc == 0, kc == kcn - 1, [sq, C.ones_bf], [ssb[i]])
    rstd = C.rstd
    for i, (t0, w) in enumerate(blks):
        actf(P, rstd[:, t0:t0 + w], ssb[i][:, 0:w], AF.Ln, [ssb[i], C.epsc], [rstd.p(i)], bias=C.epsc[:, 0:1], scale=1.0 / n_feat)
        actf(P, rstd[:, t0:t0 + w], rstd[:, t0:t0 + w], AF.Exp, [rstd.p(i)], [rstd.p(i)], scale=-0.5)
    for kc in range(kcn):
        stt(P, out[:, kc, ocol:ocol + tot], h[:, kc, 0:tot], gcols[:, kc:kc + 1], rstd[:, 0:tot], ALU.mult, ALU.mult,
            [h.p(kc), rstd, gdep], [out.p(kc)])


def pnorm(C, src, n, gcol, gdep, out_ap, out_dep, extra_scale=1.0, div=128.0):
    P = C.P
    sq = C.sq[0]
    actf(P, sq[:, 0:n], src[:, 0:n], AF.Square, [src], [sq])
    nm = len(C.misc)
    for i, (t0, w) in enumerate(blocks_of(n)):
        bank = C.misc[i % nm]
        mm(P, bank[:, 0:w], C.ones_bf[:, :], sq[:, t0:t0 + w], True, True, [sq, C.ones_bf], [bank])
        actf(P, C.rstd[:, t0:t0 + w], bank[:, 0:w], AF.Ln, [bank, C.epsc], [C.rstd.p(i)], bias=C.epsc[:, 0:1], scale=1.0 / div)
        actf(P, C.rstd[:, t0:t0 + w], C.rstd[:, t0:t0 + w], AF.Exp, [C.rstd.p(i)], [C.rstd.p(i)], scale=-0.5)
    rd = [src, C.rstd] + ([gdep] if gdep is not None else [])
    if gcol is None:
        stt(P, out_ap, src[:, 0:n], extra_scale, C.rstd[:, 0:n], ALU.mult, ALU.mult, rd, [out_dep])
    elif extra_scale != 1.0:
        ts(P, "dve", src[:, 0:n], src[:, 0:n], gcol, extra_scale, ALU.mult, ALU.mult, rd, [src])
        tt(P, "dve", out_ap, src[:, 0:n], C.rstd[:, 0:n], ALU.mult, [src, C.rstd], [out_dep])
    else:
        stt(P, out_ap, src[:, 0:n], gcol, C.rstd[:, 0:n], ALU.mult, ALU.mult, rd, [out_dep])


def row_phase(C, L, io):
    P = C.P
    blks = blocks_of(TH)
    mblks = blocks_of(TM)
    P.push()
    h = P.sb("h", [128, KC, TH], F32)
    hn = P.sb("hn", [128, KC, TH], BF16)
    C.sq = [P.sb("sq", [128, TH], BF16) for _ in range(2)]
    C.rstd = P.sb("rstd", [128, TH], F32)
    vec = P.sb("vec", [128, 64], F32)
    hv = P.sb("hv", [128, 4], F32)
    P.dma("sp", vec[:, :], io["vec"][:, :], writes=[vec], sem="ld_vec")
    P.dma("sp", hv[:, :], io["hv"][:, :], writes=[hv], sem="ld_hv")
    acc = [P.ps("acc", [128, 512], F32) for _ in range(6)]
    C.misc = [P.ps("misc", [128, 512], F32) for _ in range(2)]
    C.misc.append(acc[5])
    accsets = [acc[0:3], acc[3:6]]
    C.accn = 0

    def next_acc():
        C.accn += 1
        return accsets[C.accn % 2]

    kT = P.sb("kT", [128, 4, 256], BF16)
    vtm = P.sb("vtm", [128, 2, 512], BF16)
    P.push()
    wsk = WS(P, "wkv", [(io["xa_wk"][cc], 16, ("wk", cc)) for cc in range(4)], nslot=2)
    memf = P.sb("memf", [128, KC, 256], F32)
    memn = P.sb("memn", [128, KC, 256], BF16)
    wv = P.sb("wv", [128, KC, 512], BF16)
    P.dma("sp", memf[:, :, :], io["memT"][:, :].rearrange("(k p) t -> p k t", p=128), writes=[memf], sem="ld_mem")
    P.dma("pool", wv[:, :, :], io["xa_wv"][:, :, :], writes=[wv], sem="ld_wv")
    rmsnorm_fm(C, memf, KC, [(0, 256)], vec[:, 48:64], vec, memn, D)
    kf = P.sb("kf", [128, 256], F32)
    for cc in range(4):
        wt = wsk.next(("wk", cc))
        bank = next_acc()
        dense(C, wt, 16, memn, [(0, 256)], bank)
        cp(P, "act", kf[:, :], bank[0][:, 0:256], [bank[0]], [kf])
        pnorm(C, kf, 256, hv[:, 1:2], hv, kT[:, cc, :], kT.p(cc))
    for mt in range(2):
        bank = next_acc()
        for kc in range(KC):
            mm(P, bank[0][:, :], memn[:, kc, mt * 128:(mt + 1) * 128], wv[:, kc, :], kc == 0, kc == KC - 1, [memn, wv], [bank[0]])
        cp(P, "act", vtm[:, mt, :], bank[0][:, :], [bank[0]], [vtm.p(mt)])
    P.pop()

    def add_res(cc, banks, bl):
        for i, (t0, w) in enumerate(bl):
            tt(P, "dve", h[:, cc, t0:t0 + w], h[:, cc, t0:t0 + w], banks[i][:, 0:w], ALU.add, [banks[i], h.p(cc)], [h.p(cc)])

    hsrc = io["h_in"]
    hout = io["h_out"]
    og = io["o_gath"]
    for tk in range(2):
        tok0 = tk * TM
        tiles = []
        for cc in range(16):
            tiles.append((io["w_mix_out"][cc], 16, ("mo", cc)))
        for cc in range(4):
            tiles.append((io["xa_wq"][cc], 16, ("wq", cc)))
        for cc in range(16):
            tiles.append((io["xa_wo"][cc], 4, ("wo", cc)))
        for qd in range(4):
            for j in range(11):
                tiles.append((io["ffn_w_in"][qd * 11 + j], 16, ("fg", qd, j)))
                tiles.append((io["ffn_w_in"][44 + qd * 11 + j], 16, ("fv", qd, j)))
            for cc in range(16):
                tiles.append((io["ffn_w_out"][qd, cc], 11, ("fo", qd, cc)))
        P.push()
        ws = WS(P, "wrow", tiles, nslot=4)
        P.dma("sp", h[:, :, 0:TM], hsrc[:, tok0:tok0 + TM].rearrange("(k p) t -> p k t", p=128), reads=[hsrc], writes=[h], sem="ld_h")
        P.dma("sp", h[:, :, TM:TH], hsrc[:, TM - 2:TM].rearrange("(k p) t -> p k t", p=128), reads=[hsrc], writes=[h], sem="ld_hh")
        for kc in range(KC):
            r, c0 = kc // 8, (kc % 8) * 128
            P.dma("sp", hn[:, kc, 0:TM], og[r, c0:c0 + 128, tok0:tok0 + TM], reads=[og], writes=[hn.p(kc)], sem="ld_oa", ring=4)
            P.dma("sp", hn[:, kc, TM:TH], og[r, c0:c0 + 128, TM - 2:TM], reads=[og], writes=[hn.p(kc)], sem="ld_oh", ring=4)

        for cc in range(16):
            wt = ws.next(("mo", cc))
            banks = next_acc()
            dense(C, wt, 16, hn, blks, banks)
            add_res(cc, banks, blks)

        rmsnorm_fm(C, h, KC, blks, vec[:, 0:16], vec, hn, D)
        P.push()
        qT = P.sb("qT", [128, 4, TH], BF16)
        qf = P.sb("qf", [128, TH], F32)
        oxa = P.sb("oxa", [128, 4, TH], BF16)
        pT = [P.sb("pT", [128, 512], BF16) for _ in range(2)]
        rden = P.sb("rden", [128, 512], F32)
        for cc in range(4):
            wt = ws.next(("wq", cc))
            banks = next_acc()
            dense(C, wt, 16, hn, blks, banks)
            for i, (t0, w) in enumerate(blks):
                cp(P, "act", qf[:, t0:t0 + w], banks[i][:, 0:w], [banks[i]], [qf])
            pnorm(C, qf, TH, hv[:, 0:1], hv, qT[:, cc, :], qT.p(cc))
        sc = 128.0 ** -0.5
        for hd in range(4):
            for (t0, w) in blks:
                banks = next_acc()
                for mt in range(2):
                    mm(P, C.misc[mt][:, 0:w], kT[:, hd, mt * 128:(mt + 1) * 128], qT[:, hd, t0:t0 + w], True, True, [kT.p(hd), qT.p(hd)], [C.misc[mt]])
                    actf(P, pT[mt][:, 0:w], C.misc[mt][:, 0:w], AF.Exp, [C.misc[mt]], [pT[mt]], scale=sc)
                for mt in range(2):
                    mm(P, banks[0][:, 0:w], vtm[:, mt, hd * 128:(hd + 1) * 128], pT[mt][:, 0:w], mt == 0, mt == 1, [vtm, pT[mt]], [banks[0]])
                for mt in range(2):
                    mm(P, banks[1][:, 0:w], C.ones_bf[:, :], pT[mt][:, 0:w], mt == 0, mt == 1, [C.ones_bf, pT[mt]], [banks[1]])
                actf(P, rden[:, 0:w], banks[1][:, 0:w], AF.Ln, [banks[1]], [rden])
                actf(P, rden[:, 0:w], rden[:, 0:w], AF.Exp, [rden], [rden], scale=-1.0)
                tt(P, "dve", oxa[:, hd, t0:t0 + w], banks[0][:, 0:w], rden[:, 0:w], ALU.mult, [banks[0], rden], [oxa.p(hd)])
        for cc in range(16):
            wt = ws.next(("wo", cc))
            banks = next_acc()
            dense(C, wt, 4, oxa, blks, banks)
            add_res(cc, banks, blks)
        P.pop()

        rmsnorm_fm(C, h, KC, blks, vec[:, 16:32], vec, hn, D)
        P.push()
        aT = P.sb("aT", [128, 11, TM], BF16)
        ub = [P.sb("ub", [128, TH], F32) for _ in range(4)]
        cg = P.sb("cg", [128, TM], F32)
        cv = P.sb("cv", [128, TM], F32)
        cw = P.sb("cw", [128, 88, 4], F32)
        P.dma("sp", cw[:, :, :], io["ffn_cw"][:, :, :], writes=[cw], sem="ld_cw")
        ubi = 0
        for qd in range(4):
            for j in range(11):
                for which, tagn in ((0, "fg"), (1, "fv")):
                    wt = ws.next((tagn, qd, j))
                    banks = next_acc()
                    dense(C, wt, 16, hn, blks, banks)
                    u = ub[ubi % 4]
                    ubi += 1
                    ch = which * 44 + qd * 11 + j
                    ts(P, "dve", u[:, 0:2], banks[2][:, 0:2], float(tk), None, ALU.mult, None, [banks[2]], [u])
                    cp(P, "act", u[:, 2:514], banks[0][:, 0:512], [banks[0]], [u])
                    cp(P, "act", u[:, 514:1026], banks[1][:, 0:512], [banks[1]], [u])
                    dst = cg if which == 0 else cv
                    ts(P, "dve", dst[:, :], u[:, 0:TM], cw[:, ch, 0:1], cw[:, ch, 3:4], ALU.mult, ALU.add, [u, cw], [dst])
                    stt(P, dst[:, :], u[:, 1:TM + 1], cw[:, ch, 1:2], dst[:, :], ALU.mult, ALU.add, [u, cw, dst], [dst])
                    stt(P, dst[:, :], u[:, 2:TM + 2], cw[:, ch, 2:3], dst[:, :], ALU.mult, ALU.add, [u, cw, dst], [dst])
                actf(P, cg[:, :], cg[:, :], AF.Silu, [cg], [cg])
                tt(P, "pool", aT[:, j, :], cg[:, :], cv[:, :], ALU.mult, [cg, cv], [aT.p(j)])
            for cc in range(16):
                wt = ws.next(("fo", qd, cc))
                banks = next_acc()
                dense(C, wt, 11, aT, mblks, banks)
                add_res(cc, banks, mblks)
        P.pop()

        for kc in range(KC):
            P.dma("sp", hout[kc * 128:(kc + 1) * 128, tok0:tok0 + TM], h[:, kc, 0:TM], reads=[h.p(kc)], writes=[hout], sem="st_h", ring=4)
        P.pop()
    P.pop()


def row_phase8(C, L, io, mid=None):
    P = C.P
    blks = blocks_of(TH)
    mblks = blocks_of(TM)
    P.push()
    h = P.sb("h", [128, KC, TH], F32)
    hn = P.sb("hn", [128, KC, TH], BF16)
    C.sq = [P.sb("sq", [128, TH], BF16) for _ in range(2)]
    C.rstd = P.sb("rstd", [128, TH], F32)
    vec = P.sb("vec", [128, 64], F32)
    hv = P.sb("hv", [128, 4], F32)
    P.dma("sp", vec[:, :], io["vec"][:, :], writes=[vec], sem="ld_vec")
    P.dma("sp", hv[:, :], io["hv"][:, :], writes=[hv], sem="ld_hv")
    acc = [P.ps("acc", [128, 512], F32) for _ in range(6)]
    C.misc = [P.ps("misc", [128, 512], F32) for _ in range(2)]
    C.misc.append(acc[5])
    accsets = [acc[0:3], acc[3:6]]
    C.accn = 0

    def next_acc():
        C.accn += 1
        return accsets[C.accn % 2]

    tiles = []
    for cc in range(4):
        tiles.append((io["xa_wk"][cc], 16, ("wk", cc)))
    for cc in range(16):
        tiles.append((io["w_mix_out"][cc], 16, ("mo", cc)))
    for cc in range(4):
        tiles.append((io["xa_wq"][cc], 16, ("wq", cc)))
    for cc in range(16):
        tiles.append((io["xa_wo"][cc], 4, ("wo", cc)))
    for qd in range(4):
        for j in range(11):
            tiles.append((io["ffn_w_in"][qd * 11 + j], 16, ("fg", qd, j)))
            tiles.append((io["ffn_w_in"][44 + qd * 11 + j], 16, ("fv", qd, j)))
        for cc in range(16):
            tiles.append((io["ffn_w_out"][qd, cc], 11, ("fo", qd, cc)))
    ws = WS(P, "wrow", tiles, nslot=4)

    hsrc = io["h_in"]
    hhalo = io["h_halo"]
    P.dma("sp", h[:, :, 0:TM], hsrc[:, :].rearrange("(k p) t -> p k t", p=128), reads=[hsrc], writes=[h], sem="ld_h")
    P.dma("sp", h[:, :, TM:TH], hhalo[:, :].rearrange("(k p) t -> p k t", p=128), reads=[hhalo], writes=[h], sem="ld_hh")
    kT = P.sb("kT", [128, 4, 256], BF16)
    vtm = P.sb("vtm", [128, 2, 512], BF16)
    P.push()
    memf = P.sb("memf", [128, KC, 256], F32)
    memn = P.sb("memn", [128, KC, 256], BF16)
    wv = P.sb("wv", [128, KC, 512], BF16)
    P.dma("sp", memf[:, :, :], io["memT"][:, :].rearrange("(k p) t -> p k t", p=128), writes=[memf], sem="ld_mem")
    P.dma("pool", wv[:, :, :], io["xa_wv"][:, :, :], writes=[wv], sem="ld_wv")
    rmsnorm_fm(C, memf, KC, [(0, 256)], vec[:, 48:64], vec, memn, D)
    kf = P.sb("kf", [128, 256], F32)
    for cc in range(4):
        wt = ws.next(("wk", cc))
        bank = next_acc()
        dense(C, wt, 16, memn, [(0, 256)], bank)
        cp(P, "act", kf[:, :], bank[0][:, 0:256], [bank[0]], [kf])
        pnorm(C, kf, 256, hv[:, 1:2], hv, kT[:, cc, :], kT.p(cc))
    for mt in range(2):
        bank = next_acc()
        for kc in range(KC):
            mm(P, bank[0][:, :], memn[:, kc, mt * 128:(mt + 1) * 128], wv[:, kc, :], kc == 0, kc == KC - 1, [memn, wv], [bank[0]])
        cp(P, "act", vtm[:, mt, :], bank[0][:, :], [bank[0]], [vtm.p(mt)])
    P.pop()

    ws.prefetch()
    if mid is not None:
        mid()
    og = io["o_gath"]
    for r in range(2):
        P.dma2s("sp", hn[:, r * 8:(r + 1) * 8, 0:TM], og[r, :, 0:TM].rearrange("(k p) t -> p k t", p=128),
                og[r, :, TM:S].rearrange("(k p) t -> p k t", p=128), reads=[og], writes=[hn], sem="ld_oa%d" % r)
        P.dma("sp", hn[:, r * 8:(r + 1) * 8, TM:TH], og[r, :, TM - 2:TM].rearrange("(k p) t -> p k t", p=128), reads=[og], writes=[hn], sem="ld_oh%d" % r)

    def add_res(cc, banks, bl):
        for i, (t0, w) in enumerate(bl):
            tt(P, "dve", h[:, cc, t0:t0 + w], h[:, cc, t0:t0 + w], banks[i][:, 0:w], ALU.add, [banks[i], h.p(cc)], [h.p(cc)])

    for cc in range(16):
        wt = ws.next(("mo", cc))
        banks = next_acc()
        dense(C, wt, 16, hn, blks, banks)
        add_res(cc, banks, blks)

    rmsnorm_fm(C, h, KC, blks, vec[:, 0:16], vec, hn, D)
    P.push()
    qT = P.sb("qT", [128, 4, TH], BF16)
    qf = P.sb("qf", [128, TH], F32)
    oxa = P.sb("oxa", [128, 4, TH], BF16)
    pT4 = [P.sb("pT", [128, 512], BF16) for _ in range(4)]
    rden2 = [P.sb("rden", [128, 512], F32) for _ in range(2)]
    for cc in range(4):
        wt = ws.next(("wq", cc))
        banks = next_acc()
        dense(C, wt, 16, hn, blks, banks)
        for i, (t0, w) in enumerate(blks):
            cp(P, "act", qf[:, t0:t0 + w], banks[i][:, 0:w], [banks[i]], [qf])
        pnorm(C, qf, TH, hv[:, 0:1], hv, qT[:, cc, :], qT.p(cc))
    sc = 128.0 ** -0.5
    xsteps = [(hd, t0, w) for hd in range(4) for (t0, w) in blks]

    def xscore(i):
        hd, t0, w = xsteps[i]
        for mt in range(2):
            p_ = pT4[2 * (i % 2) + mt]
            mm(P, C.misc[mt][:, 0:w], kT[:, hd, mt * 128:(mt + 1) * 128], qT[:, hd, t0:t0 + w], True, True, [kT.p(hd), qT.p(hd)], [C.misc[mt]])
            actf(P, p_[:, 0:w], C.misc[mt][:, 0:w], AF.Exp, [C.misc[mt]], [p_], scale=sc)

    def xpv(i):
        hd, t0, w = xsteps[i]
        banks = next_acc()
        ps_ = [pT4[2 * (i % 2)], pT4[2 * (i % 2) + 1]]
        for mt in range(2):
            mm(P, banks[0][:, 0:w], vtm[:, mt, hd * 128:(hd + 1) * 128], ps_[mt][:, 0:w], mt == 0, mt == 1, [vtm, ps_[mt]], [banks[0]])
        for mt in range(2):
            mm(P, banks[1][:, 0:w], C.ones_bf[:, :], ps_[mt][:, 0:w], mt == 0, mt == 1, [C.ones_bf, ps_[mt]], [banks[1]])
        rd_ = rden2[i % 2]
        actf(P, rd_[:, 0:w], banks[1][:, 0:w], AF.Ln, [banks[1]], [rd_])
        actf(P, rd_[:, 0:w], rd_[:, 0:w], AF.Exp, [rd_], [rd_], scale=-1.0)
        tt(P, "dve", oxa[:, hd, t0:t0 + w], banks[0][:, 0:w], rd_[:, 0:w], ALU.mult, [banks[0], rd_], [oxa.p(hd)])

    xscore(0)
    for i in range(len(xsteps)):
        if i + 1 < len(xsteps):
            xscore(i + 1)
        xpv(i)
    for cc in range(16):
        wt = ws.next(("wo", cc))
        banks = next_acc()
        dense(C, wt, 4, oxa, blks, banks)
        add_res(cc, banks, blks)
    P.pop()

    rmsnorm_fm(C, h, KC, blks, vec[:, 16:32], vec, hn, D)
    P.push()
    aT = P.sb("aT", [128, 11, TM], BF16)
    ub = [P.sb("ub", [128, TH], F32) for _ in range(4)]
    cg = P.sb("cg", [128, TM], F32)
    cv = P.sb("cv", [128, TM], F32)
    cw = P.sb("cw", [128, 88, 4], F32)
    P.dma("sp", cw[:, :, :], io["ffn_cw"][:, :, :], writes=[cw], sem="ld_cw")
    ubi = 0
    for qd in range(4):
        for j in range(11):
            res = []
            for which, tagn in ((0, "fg"), (1, "fv")):
                wt = ws.next((tagn, qd, j))
                banks = next_acc()
                dense(C, wt, 16, hn, blks, banks)
                u = ub[ubi % 4]
                ubi += 1
                ch = which * 44 + qd * 11 + j
                ts(P, "dve", u[:, 0:2], banks[2][:, 0:2], C.flags[:, 1:2], None, ALU.mult, None, [banks[2], C.flags], [u])
                cp(P, "act", u[:, 2:514], banks[0][:, 0:512], [banks[0]], [u])
                cp(P, "act", u[:, 514:1026], banks[1][:, 0:512], [banks[1]], [u])
                dst = cg if which == 0 else cv
                ts(P, "dve", dst[:, :], u[:, 0:TM], cw[:, ch, 0:1], cw[:, ch, 3:4], ALU.mult, ALU.add, [u, cw], [dst])
                stt(P, dst[:, :], u[:, 1:TM + 1], cw[:, ch, 1:2], dst[:, :], ALU.mult, ALU.add, [u, cw, dst], [dst])
                stt(P, dst[:, :], u[:, 2:TM + 2], cw[:, ch, 2:3], dst[:, :], ALU.mult, ALU.add, [u, cw, dst], [dst])
            actf(P, cg[:, :], cg[:, :], AF.Silu, [cg], [cg])
            tt(P, "pool", aT[:, j, :], cg[:, :], cv[:, :], ALU.mult, [cg, cv], [aT.p(j)])
        for cc in range(16):
            wt = ws.next(("fo", qd, cc))
            banks = next_acc()
            dense(C, wt, 11, aT, mblks, banks)
            add_res(cc, banks, mblks)
    P.pop()

    hout = io["h_out"]
    for kc in range(KC):
        P.dma("sp", hout[kc * 128:(kc + 1) * 128, :], h[:, kc, 0:TM], reads=[h.p(kc)], writes=[hout], sem="st_h", ring=4)
    if L == 0:
        tsh = io["h_tail"]
        P.dma2("sp", tsh.h[0].rearrange("(k p) t -> p k t", p=128), tsh.h[1].rearrange("(k p) t -> p k t", p=128), h[:, :, TM - 2:TM], reads=[h], writes=[tsh], sem="st_ht")
        rmsnorm_fm(C, h, KC, mblks, vec[:, 32:48], vec, hn, D)
        hn1 = io["hn1_out"]
        P.dma2("sp", hn1.h[0].rearrange("(k p) t -> p k t", p=128), hn1.h[1].rearrange("(k p) t -> p k t", p=128), hn[:, :, 0:TM], reads=[hn], writes=[hn1], sem="st_hn")
    P.pop()


def make_ctx(P, io):
    C = Ctx()
    C.P = P
    C.ones_bf = P.sb("ones_bf", [128, 128], BF16)
    memset(P, "pool", C.ones_bf[:, :], 1.0, [C.ones_bf])
    C.cf = P.sb("cf", [128, io["consts"].h.shape[1]], F32)
    P.dma("sp", C.cf[:, :], io["consts"][:, :], writes=[C.cf], sem="ld_c")
    C.flags = P.sb("flags", [128, 2], F32)
    P.dma("sp", C.flags[:, :], io["flags"][:, :], writes=[C.flags], sem="ld_f")
    C.epsc = P.sb("epsc", [128, 1], F32)
    memset(P, "pool", C.epsc[:, :], EPS, [C.epsc])
    return C


def tile_w(W, kcn=None):
    K, N = W.shape
    return np.ascontiguousarray(W.reshape(K // 128, 128, N // 128, 128).transpose(2, 1, 0, 3))


def cols128(v):
    return np.ascontiguousarray(v.reshape(-1, 128).T)


def gath_rows(hh):
    return list(range(512 * hh, 512 * hh + 512)) + list(range(1024 + 512 * hh, 1024 + 512 * hh + 512))


def prep_row(inp, L, b, hh):
    d = {}
    s = "_r%d" % L
    d["vec" + s] = np.concatenate([cols128(inp["norm_mem"][L]), cols128(inp["norm_ffn"][L]),
                                   cols128(inp["norm_mix"][1]), cols128(inp["mem_norm"])], axis=1).astype(np.float32)
    hv = np.zeros((128, 4), np.float32)
    hv[:, 0] = inp["xa_qnorm"][L]
    hv[:, 1] = inp["xa_knorm"][L]
    d["hv" + s] = hv
    d["xa_wk" + s] = tile_w(inp["xa_wk"][L])
    d["xa_wq" + s] = tile_w(inp["xa_wq"][L])
    d["xa_wv" + s] = np.ascontiguousarray(inp["xa_wv"][L].reshape(16, 128, 512).transpose(1, 0, 2))
    d["xa_wo" + s] = tile_w(inp["xa_wo"][L])
    wmo = inp["ab_w_out"][0] if L == 0 else inp["cd_w_out"][0]
    d["w_mix_out" + s] = tile_w(wmo[gath_rows(0) + gath_rows(1)])
    d["ffn_w_in" + s] = tile_w(inp["ffn_w_in"][L])
    d["ffn_w_out" + s] = np.ascontiguousarray(inp["ffn_w_out"][L].reshape(4, 11, 128, 16, 128).transpose(0, 3, 2, 1, 4))
    cw = np.zeros((128, 88, 4), np.float32)
    cw[:, :, 0:3] = inp["ffn_conv_w"][L].reshape(3, 88, 128).transpose(2, 1, 0)
    cw[:, :, 3] = inp["ffn_conv_b"][L].reshape(88, 128).T
    d["ffn_cw" + s] = cw
    return d


def decl_row(P, L, ext):
    s = "_r%d" % L
    io = {}

    def din(key, shape, dt=F32):
        io[key] = P.dram(key + s, shape, dt, kind="ExternalInput")
    din("vec", [128, 64])
    din("hv", [128, 4])
    din("xa_wk", [4, 128, 16, 128])
    din("xa_wq", [4, 128, 16, 128])
    din("xa_wv", [128, 16, 512])
    din("xa_wo", [16, 128, 4, 128])
    din("w_mix_out", [16, 128, 16, 128])
    din("ffn_w_in", [88, 128, 16, 128])
    din("ffn_w_out", [4, 16, 128, 11, 128])
    din("ffn_cw", [128, 88, 4])
    return io


CF_TRI, CF_NEGL, CF_NEGU, CF_ID, CF_MU, CF_ROT, CF_SEL, CF_INV, CF_W = 0, 128, 256, 384, 512, 640, 768, 1792, 1793


def make_consts():
    cf = np.zeros((128, CF_W), np.float32)
    p = np.arange(128)[:, None]
    j = np.arange(128)[None, :]
    cf[:, CF_TRI:CF_TRI + 128] = (p <= j)
    cf[:, CF_NEGL:CF_NEGL + 128] = np.where(p > j, 0.0, NEG)
    cf[:, CF_NEGU:CF_NEGU + 128] = np.where(j >= p, 0.0, NEG)
    cf[:, CF_ID:CF_ID + 128] = (p == j)
    cf[:, CF_MU:CF_MU + 128] = (j >= p)
    rot = np.zeros((128, 128), np.float32)
    for q in range(16):
        rot[q + 16, q] = -1.0
        rot[q, q + 16] = 1.0
    cf[:, CF_ROT:CF_ROT + 128] = rot
    sel = np.zeros((128, 8, 128), np.float32)
    for n in range(8):
        sel[n, n, :] = -NEG
    cf[:, CF_SEL:CF_SEL + 1024] = sel.reshape(128, 1024)
    import math
    inv = np.exp(-math.log(500000.0) * np.arange(0, 32, 2, dtype=np.float32) / 32).astype(np.float32)
    cf[0:16, CF_INV] = inv
    cf[16:32, CF_INV] = inv
    return cf


def mix_common(C):
    P = C.P
    C.hnT = P.sb("hnT", [128, KC, S], BF16)
    C.sq = [P.sb("sq", [128, S], BF16)]
    C.rstd = P.sb("rstd", [128, S], F32)
    C.acc = [P.ps("acc", [128, 512], F32) for _ in range(4)]
    C.pf = [P.ps("pf", [128, 512], F32) for _ in range(2)]
    C.pb = [P.ps("pb", [128, 1024], BF16) for _ in range(2)]
    C.misc = C.pf
    C.fbanks = [C.pf[0], C.pf[1], C.acc[2], C.acc[3]]
    C.fsn = 0
    C.bsn = 0
    C.accn = 0
    C.ident_bf = P.sb("identb", [128, 128], BF16)
    cp(P, "dve", C.ident_bf[:, :], C.cf[:, CF_ID:CF_ID + 128], [C.cf], [C.ident_bf])
    C.mu_bf = P.sb("mub", [128, 128], BF16)
    cp(P, "dve", C.mu_bf[:, :], C.cf[:, CF_MU:CF_MU + 128], [C.cf], [C.mu_bf])


def fs(C):
    C.fsn += 1
    banks = C.fbanks
    i = C.fsn % (4 * len(banks))
    t = banks[i % len(banks)]
    s = i // len(banks)
    return t[:, s * 128:(s + 1) * 128], t


def bs(C):
    C.bsn += 1
    i = C.bsn % 8
    t = C.pb[i % 2]
    s = i // 2
    return t[:, s * 128:(s + 1) * 128], t


def acc2(C):
    C.accn += 1
    return C.acc[0:2] if C.accn % 2 else C.acc[2:4]


def conv_fm_g(C, wt, cwap, cwdep, ntap, bias_ap, y):
    P = C.P
    xp = C.xpad
    for half in range(2):
        banks = acc2(C)
        dense(C, wt, 16, C.hnT, [(half * 1024, 512), (half * 1024 + 512, 512)], banks)
        for i in range(2):
            c0 = 3 + half * 1024 + i * 512
            cp(P, "act", xp[:, c0:c0 + 512], banks[i][:, 0:512], [banks[i]], [xp])
        yield
    if bias_ap is None:
        ts(P, "dve", y[:, :], xp[:, 0:S], cwap[:, 0:1], None, ALU.mult, None, [xp, cwdep], [y])
    else:
        ts(P, "dve", y[:, :], xp[:, 0:S], cwap[:, 0:1], bias_ap, ALU.mult, ALU.add, [xp, cwdep], [y])
    for j in range(1, ntap):
        stt(P, y[:, :], xp[:, j:j + S], cwap[:, j:j + 1], y[:, :], ALU.mult, ALU.add, [xp, cwdep, y], [y])
    actf(P, y[:, :], y[:, :], AF.Silu, [y], [y])
    yield


def conv_fm(C, wt, cwap, cwdep, ntap, bias_ap, y):
    for _ in conv_fm_g(C, wt, cwap, cwdep, ntap, bias_ap, y):
        pass


def proj_plain_g(C, wt, y, fixed_banks=None):
    P = C.P
    for half in range(2):
        banks = fixed_banks if fixed_banks is not None else acc2(C)
        dense(C, wt, 16, C.hnT, [(half * 1024, 512), (half * 1024 + 512, 512)], banks)
        for i in range(2):
            c0 = half * 1024 + i * 512
            cp(P, "act", y[:, c0:c0 + 512], banks[i][:, 0:512], [banks[i]], [y])
        yield


def proj_plain(C, wt, y):
    for _ in proj_plain_g(C, wt, y):
        pass


def interleave(gens, weights=None):
    gens = list(gens)
    wts = dict((id(g_), (weights[i] if weights else 1)) for i, g_ in enumerate(gens))
    while gens:
        for g_ in list(gens):
            for _ in range(wts[id(g_)]):
                try:
                    next(g_)
                except StopIteration:
                    gens.remove(g_)
                    break


def proj_tm(C, w, ncol, tile_i, out_ap, out_dep, eng="act", bank=None, func=None, wdep=None):
    P = C.P
    if bank is None:
        bank = C.acc[tile_i % 4]
    for kc in range(KC):
        mm(P, bank[:, 0:ncol], C.hnT[:, kc, tile_i * 128:(tile_i + 1) * 128], w[:, kc, :], kc == 0, kc == KC - 1, [C.hnT, wdep if wdep is not None else w], [bank])
    if func is not None:
        actf(P, out_ap, bank[:, 0:ncol], func, [bank], [out_dep])
    else:
        cp(P, eng, out_ap, bank[:, 0:ncol], [bank], [out_dep])


def proj_tm4(C, w, g4, out, out_deps, wdep, bank):
    P = C.P
    for j in range(4):
        t = 4 * g4 + j
        for kc in range(KC):
            mm(P, bank[:, j * 128:(j + 1) * 128], C.hnT[:, kc, t * 128:(t + 1) * 128], w[:, kc, :], kc == 0, kc == KC - 1, [C.hnT, wdep], [bank])
    cp(P, "act", out[:, 4 * g4:4 * g4 + 4, :], bank[:, :].rearrange("p (a b) -> p a b", a=4), [bank], out_deps)


SM0 = dict(gcw=0, gnorm=48, scw=49, scb=73, gdtb=79, galog=143, sdtb=207, salog=335, sD=463, snorm=471, nmix=983, W=999)


def load_hn(C, xs, gcols, gdep):
    P = C.P
    P.push()
    hfs = [P.sb("hf", [128, KC, 512], F32) for _ in range(2)]

    def load(tb):
        hf = hfs[tb % 2]
        for q in range(2):
            P.dma("sp", hf[:, q * 8:(q + 1) * 8, :], xs[q * 1024:(q + 1) * 1024, tb * 512:(tb + 1) * 512].rearrange("(k p) t -> p k t", p=128),
                  reads=[xs], writes=[hf], sem="ld_x%d" % (tb % 2), ring=2)
    load(0)
    load(1)
    for tb in range(4):
        rmsnorm_fm(C, hfs[tb % 2], KC, [(0, 512)], gcols, gdep, C.hnT, D, ocol=tb * 512)
        if tb + 2 < 4:
            load(tb + 2)
    P.pop()


def mixer0(C, ios, xs):
    P = C.P
    P.push()
    mix_common(C)
    first = True
    for io in ios:
        P.push()
        sm = P.sb("sm", [128, SM0["W"]], F32)
        P.dma("sp", sm[:, :], io["sm"][:, :], writes=[sm], sem="ld_sm")
        wg = P.sb("wg", [128, KC, 8], BF16)
        wdt = P.sb("wdt", [128, KC, 8], BF16)
        P.dma("pool", wg[:, :, :], io["wg"][:, :, :], writes=[wg], sem="ld_wg")
        P.dma("pool", wdt[:, :, :], io["wdt"][:, :, :], writes=[wdt], sem="ld_wdt")
        ia = [hl * 4 + j for hl in range(4) for j in range(3)] + list(range(16, 22))
        ws = WS(P, "wmix", [(io["wfm"][i], 16, i) for i in ia], nslot=3)
        wsz = WS(P, "wmixz", [(io["wfm"][hl * 4 + 3], 16, hl * 4 + 3) for hl in range(4)], nslot=2)
        if first:
            load_hn(C, xs, sm[:, SM0["nmix"]:SM0["nmix"] + 16], sm)
            first = False
        gdn(C, io, sm, ws, wsz, wg)
        ssd(C, io, sm, ws, wdt)
        P.pop()
    P.pop()


def gdn(C, io, sm, ws, wsz, wg):
    P = C.P
    hnT = C.hnT
    cf = C.cf
    tri = cf[:, CF_TRI:CF_TRI + 128]
    P.push()
    C.xpad = P.sb("xpad", [128, 3 + S], F32)
    memset(P, "pool", C.xpad[:, 0:3], 0.0, [C.xpad])
    y = P.sb("y", [128, S], F32)
    graw = P.sb("graw", [128, 16, 8], F32)
    for t in range(16):
        proj_tm(C, wg, 8, t, graw[:, t, :], graw.p(t))
    beta = P.sb("beta", [128, 16, 4], F32)
    gcol = P.sb("gcol", [128, 16, 4], F32)
    gc = P.sb("gc", [128, 16, 4], F32)
    ngc = P.sb("ngc", [128, 16, 4], F32)
    bexp = P.sb("bexp", [128, 16, 4], F32)
    nea = P.sb("nea", [128, 64], F32)
    actf(P, beta[:, :, :], graw[:, :, 0:4], AF.Sigmoid, [graw], [beta])
    g3 = lambda t_, o: t_[:, o:o + 64].rearrange("p (t h) -> p t h", h=4)
    tt(P, "dve", gcol[:, :, :], graw[:, :, 4:8], g3(sm, SM0["gdtb"]), ALU.add, [graw, sm], [gcol])
    actf(P, gcol[:, :, :], gcol[:, :, :], AF.Exp, [gcol], [gcol])
    actf(P, gcol[:, :, :], gcol[:, :, :], AF.Ln, [gcol], [gcol], bias=1.0)
    actf(P, nea[:, :], sm[:, SM0["galog"]:SM0["galog"] + 64], AF.Exp, [sm], [nea])
    stt(P, gcol[:, :, :], gcol[:, :, :], -1.0, g3(nea, 0), ALU.mult, ALU.mult, [gcol, nea], [gcol])
    for t in range(16):
        ps, pd = fs(C)
        mm(P, ps[:, 0:4], tri, gcol[:, t, :], True, True, [cf, gcol], [pd])
        cp(P, "act", gc[:, t, :], ps[:, 0:4], [pd], [gc.p(t)])
    ts(P, "dve", ngc[:, :, :], gc[:, :, :], -1.0, None, ALU.mult, None, [gc], [ngc])
    actf(P, bexp[:, :, :], gc[:, :, :], AF.Exp, [gc], [bexp])
    tt(P, "dve", bexp[:, :, :], bexp[:, :, :], beta[:, :, :], ALU.mult, [bexp, beta], [bexp])
    QT = P.sb("QT", [128, S], BF16)
    KT = P.sb("KT", [128, S], BF16)
    VT = P.sb("VT", [128, S], BF16)
    qgT = P.sb("qgT", [128, 16, 128], BF16)
    AT = P.sb("AT", [128, 16, 128], BF16)
    kd = P.sb("kd", [128, 16, 128], BF16)
    wT = P.sb("wT", [128, 16, 128], BF16)
    uu = P.sb("uu", [128, 16, 128], BF16)
    egl = P.sb("egl", [128, 16], F32)
    oraw = P.sb("oraw", [128, S], F32)
    Sst = P.sb("Sst", [128, 128], F32)
    Sbf = P.sb("Sbf", [128, 128], BF16)
    GI = 4
    bsets = []
    for _ in range(GI):
        bsets.append(dict(sm=P.sb("sml", [128, 8], F32), EL=P.sb("EL", [128, 128], F32), EU=P.sb("EU", [128, 128], F32),
                          erb=P.sb("erb", [128, 128], F32), t0=P.sb("tmpf", [128, 128], F32), t1=P.sb("tmpf", [128, 128], F32),
                          LU=[P.sb("LU", [128, 2, 128], BF16) for _ in range(3)],
                          Rbf=P.sb("Rbf", [128, 128], BF16),
                          kbg=P.sb("kbg", [128, 128], BF16), vb=P.sb("vb", [128, 128], BF16)))
    vnew = P.sb("vnew", [128, 128], BF16)
    ob = P.sb("ob", [128, S], BF16)
    og = io["o_half"].mine if isinstance(io["o_half"], DynSlot) else io["o_half"]
    ogd = io["o_half"]
    yz = P.sb("yz", [128, S], F32)
    NH = 4

    def stageA(hl):
        cwq = sm[:, SM0["gcw"] + (hl * 3 + 0) * 4:SM0["gcw"] + (hl * 3 + 0) * 4 + 4]
        cwk = sm[:, SM0["gcw"] + (hl * 3 + 1) * 4:SM0["gcw"] + (hl * 3 + 1) * 4 + 4]
        cwv = sm[:, SM0["gcw"] + (hl * 3 + 2) * 4:SM0["gcw"] + (hl * 3 + 2) * 4 + 4]
        yield from conv_fm_g(C, ws.next(hl * 4 + 0), cwq, sm, 4, None, y)
        pnorm(C, y, S, None, None, QT[:, :], QT, extra_scale=128.0 ** -0.5, div=1.0)
        yield
        yield from conv_fm_g(C, ws.next(hl * 4 + 1), cwk, sm, 4, None, y)
        pnorm(C, y, S, None, None, KT[:, :], KT, div=1.0)
        yield
        yield from conv_fm_g(C, ws.next(hl * 4 + 2), cwv, sm, 4, None, y)
        cp(P, "dve", VT[:, :], y[:, :], [y], [VT])

    def stageB(hl):
            def pre(c, B):
                cs = slice(c * 128, (c + 1) * 128)
                sm_c, EL, EU, erb, tmp0, tmp1 = B["sm"], B["EL"], B["EU"], B["erb"], B["t0"], B["t1"]
                LU, Rbf, kbg, vb = B["LU"], B["Rbf"], B["kbg"], B["vb"]
                rb, rbd = fs(C)
                mm(P, rb, gcol[:, c, hl:hl + 1].to_broadcast([128, 128]), tri, True, True, [gcol, cf], [rbd])
                cp(P, "act", sm_c[:, 0:1], rb[:, 127:128], [rbd], [sm_c])
                stt(P, tmp0[:, :], rb, -1.0, cf[:, CF_NEGL:CF_NEGL + 128], ALU.mult, ALU.add, [rbd, cf], [tmp0])
                tt(P, "dve", tmp1[:, :], rb, cf[:, CF_NEGU:CF_NEGU + 128], ALU.add, [rbd, cf], [tmp1])
                actf(P, erb[:, :], rb, AF.Exp, [rbd], [erb])
                yield
                actf(P, egl[:, c:c + 1], sm_c[:, 0:1], AF.Exp, [sm_c], [egl.p(c)])
                actf(P, sm_c[:, 1:2], gc[:, c, hl:hl + 1], AF.Exp, [gc, sm_c], [sm_c], bias=sm_c[:, 0:1], scale=-1.0)
                actf(P, EL[:, :], tmp0[:, :], AF.Exp, [tmp0, gc], [EL], bias=gc[:, c, hl:hl + 1])
                actf(P, EU[:, :], tmp1[:, :], AF.Exp, [tmp1, ngc], [EU], bias=ngc[:, c, hl:hl + 1])
                tt(P, "pool", qgT[:, c, :], QT[:, cs], erb[:, :], ALU.mult, [QT, erb], [qgT.p(c)])
                yield
                kk, kkd = fs(C)
                mm(P, kk, KT[:, cs], KT[:, cs], True, True, [KT], [kkd])
                stt(P, LU[0][:, 0, :], kk, beta[:, c, hl:hl + 1], EL[:, :], ALU.mult, ALU.mult, [kkd, beta, EL], [LU[0].p(0)])
                yield
                ub_, ubd = bs(C)
                tr(P, ub_, LU[0][:, 0, :], C.ident_bf[:, :], [LU[0].p(0), C.ident_bf], [ubd])
                cp(P, "act", LU[0][:, 1, :], ub_, [ubd], [LU[0].p(1)])
                yield
                aa, aad = fs(C)
                mm(P, aa, KT[:, cs], QT[:, cs], True, True, [KT, QT], [aad])
                tt(P, "dve", AT[:, c, :], aa, EU[:, :], ALU.mult, [aad, EU], [AT.p(c)])
                yield
                kt_, ktd = bs(C)
                tr(P, kt_, KT[:, cs], C.ident_bf[:, :], [KT, C.ident_bf], [ktd])
                ts(P, "dve", kbg[:, :], kt_, bexp[:, c, hl:hl + 1], None, ALU.mult, None, [ktd, bexp], [kbg])
                ts(P, "dve", kd[:, c, :], kt_, sm_c[:, 1:2], None, ALU.mult, None, [ktd, sm_c], [kd.p(c)])
                yield
                vt_, vtd = bs(C)
                tr(P, vt_, VT[:, cs], C.ident_bf[:, :], [VT, C.ident_bf], [vtd])
                ts(P, "dve", vb[:, :], vt_, beta[:, c, hl:hl + 1], None, ALU.mult, None, [vtd, beta], [vb])
                tt(P, "dve", Rbf[:, :], cf[:, CF_ID:CF_ID + 128], LU[0][:, 1, :], ALU.subtract, [cf, LU[0].p(1)], [Rbf])
                yield
                li = 0
                for k in range(1, 7):
                    ln = (li + 1) % 3
                    C.fsn += 1
                    bank = C.fbanks[C.fsn % len(C.fbanks)]
                    half = (C.fsn // len(C.fbanks)) % 2
                    sl = bank[:, half * 256:(half + 1) * 256]
                    mm(P, sl[:, 0:128], LU[li][:, 1, :], LU[li][:, 0, :], True, True, [LU[li].p(1), LU[li].p(0)], [bank])
                    if k < 6:
                        mm(P, sl[:, 128:256], LU[li][:, 0, :], LU[li][:, 1, :], True, True, [LU[li].p(1), LU[li].p(0)], [bank])
                        cp(P, "act" if k % 2 else "dve", LU[ln][:, :, :], sl.rearrange("p (a b) -> p a b", a=2), [bank], [LU[ln]])
                    else:
                        cp(P, "act", LU[ln][:, 0, :], sl[:, 0:128], [bank], [LU[ln].p(0)])
                    yield
                    rr, rrd = fs(C)
                    mm(P, rr, LU[ln][:, 0, :], Rbf[:, :], True, True, [LU[ln].p(0), Rbf], [rrd])
                    tt(P, "dve", Rbf[:, :], Rbf[:, :], rr, ALU.add, [Rbf, rrd], [Rbf])
                    yield
                    li = ln
                up, upd = fs(C)
                mm(P, up, Rbf[:, :], vb[:, :], True, True, [Rbf, vb], [upd])
                cp(P, "act", uu[:, c, :], up, [upd], [uu.p(c)])
                yield
                wp, wpd = fs(C)
                mm(P, wp, kbg[:, :], Rbf[:, :], True, True, [kbg, Rbf], [wpd])
                cp(P, "act", wT[:, c, :], wp, [wpd], [wT.p(c)])

            for c0 in range(0, 16, GI):
                gens = [pre(c0 + i, bsets[i]) for i in range(GI)]
                while gens:
                    for g_ in list(gens):
                        try:
                            next(g_)
                        except StopIteration:
                            gens.remove(g_)

    def stageC(hl):
        memset(P, "pool", Sst[:, :], 0.0, [Sst])
        memset(P, "pool", Sbf[:, :], 0.0, [Sbf])
        for c in range(16):
            cs = slice(c * 128, (c + 1) * 128)
            w1, w1d = fs(C)
            mm(P, w1, wT[:, c, :], Sbf[:, :], True, True, [wT.p(c), Sbf], [w1d])
            tt(P, "dve", vnew[:, :], uu[:, c, :], w1, ALU.subtract, [uu.p(c), w1d], [vnew])
            ob_, obd = fs(C)
            mm(P, ob_, Sbf[:, :], qgT[:, c, :], True, False, [Sbf, qgT.p(c)], [obd])
            mm(P, ob_, vnew[:, :], AT[:, c, :], False, True, [vnew, AT.p(c)], [obd])
            cp(P, "act", oraw[:, cs], ob_, [obd], [oraw.p(c)])
            s1, s1d = fs(C)
            mm(P, s1, kd[:, c, :], vnew[:, :], True, True, [kd.p(c), vnew], [s1d])
            stt(P, Sst[:, :], Sst[:, :], egl[:, c:c + 1], s1, ALU.mult, ALU.add, [Sst, egl.p(c), s1d], [Sst])
            cp(P, "act", Sbf[:, :], Sst[:, :], [Sst], [Sbf])
            yield
        if "dbg_oraw" in io:
            P.dma("sp", io["dbg_oraw"][hl * 128:(hl + 1) * 128, :], oraw[:, :], reads=[oraw], writes=[io["dbg_oraw"]], sem="st_dbg", ring=2)
        yield from proj_plain_g(C, wsz.next(hl * 4 + 3), yz)
        actf(P, yz[:, :], yz[:, :], AF.Silu, [yz], [yz])
        pnorm(C, oraw, S, sm[:, SM0["gnorm"]:SM0["gnorm"] + 1], sm, oraw[:, :], oraw)
        yield
        tt(P, "dve", ob[:, :], oraw[:, :], yz[:, :], ALU.mult, [oraw, yz], [ob])
        P.dma("sp", og[hl * 128:(hl + 1) * 128, :], ob[:, :], reads=[ob], writes=[ogd], sem="st_o", ring=2)

    interleave([stageA(0)])
    stageB(0)
    for hl in range(NH):
        C.fbanks = [C.pf[0], C.pf[1]]
        interleave([stageC(hl)] + ([stageA(hl + 1)] if hl + 1 < NH else []))
        C.fbanks = [C.pf[0], C.pf[1], C.acc[2], C.acc[3]]
        if hl + 1 < NH:
            stageB(hl + 1)
    P.pop()


def bc(ap, n):
    return ap.unsqueeze(2).to_broadcast([ap.shape[0], ap.shape[1], n])


def ssd(C, io, sm, ws, wdt):
    P = C.P
    cf = C.cf
    tri = cf[:, CF_TRI:CF_TRI + 128]
    P.push()
    wz = P.sb("wz", [128, KC, 512], BF16)
    P.dma("pool", wz[:, :, :], io["wz"][:, :, :], writes=[wz], sem="ld_wz")
    xT = P.sb("xT", [128, 4, S], BF16)
    BT = P.sb("BT", [128, S], BF16)
    CT = P.sb("CT", [128, S], BF16)
    P.push()
    C.xpad = P.sb("xpad", [128, 3 + S], F32)
    memset(P, "pool", C.xpad[:, 0:3], 0.0, [C.xpad])
    y = P.sb("y", [128, S], F32)
    for j in range(6):
        cw = sm[:, SM0["scw"] + j * 4:SM0["scw"] + j * 4 + 4]
        cb = sm[:, SM0["scb"] + j:SM0["scb"] + j + 1]
        conv_fm(C, ws.next(16 + j), cw, sm, 4, cb, y)
        dst = xT[:, j, :] if j < 4 else (BT[:, :] if j == 4 else CT[:, :])
        dd = xT.p(j) if j < 4 else (BT if j == 4 else CT)
        cp(P, "dve", dst, y[:, :], [y], [dd])
    P.pop()
    draw = P.sb("draw", [128, 16, 8], F32)
    for t in range(16):
        proj_tm(C, wdt, 8, t, draw[:, t, :], draw.p(t))
    dt = P.sb("dt", [128, 16, 8], F32)
    acol = P.sb("acol", [128, 16, 8], F32)
    acs = P.sb("acs", [128, 16, 8], F32)
    nacs = P.sb("nacs", [128, 16, 8], F32)
    eacs = P.sb("eacs", [128, 16, 8], F32)
    nea = P.sb("nea", [128, 128], F32)
    g3 = lambda t_, o: t_[:, o:o + 128].rearrange("p (t h) -> p t h", h=8)
    tt(P, "dve", dt[:, :, :], draw[:, :, :], g3(sm, SM0["sdtb"]), ALU.add, [draw, sm], [dt])
    actf(P, dt[:, :, :], dt[:, :, :], AF.Exp, [dt], [dt])
    actf(P, dt[:, :, :], dt[:, :, :], AF.Ln, [dt], [dt], bias=1.0)
    actf(P, nea[:, :], sm[:, SM0["salog"]:SM0["salog"] + 128], AF.Exp, [sm], [nea])
    stt(P, acol[:, :, :], dt[:, :, :], -1.0, g3(nea, 0), ALU.mult, ALU.mult, [dt, nea], [acol])
    for t in range(16):
        ps, pd = fs(C)
        mm(P, ps[:, 0:8], tri, acol[:, t, :], True, True, [cf, acol], [pd])
        cp(P, "act", acs[:, t, :], ps[:, 0:8], [pd], [acs.p(t)])
    ts(P, "dve", nacs[:, :, :], acs[:, :, :], -1.0, None, ALU.mult, None, [acs], [nacs])
    actf(P, eacs[:, :, :], acs[:, :, :], AF.Exp, [acs], [eacs])
    Sst = P.sb("Sst", [128, 512], F32)
    Sbf = P.sb("Sbf", [128, 512], BF16)
    memset(P, "pool", Sst[:, :], 0.0, [Sst])
    memset(P, "pool", Sbf[:, :], 0.0, [Sbf])
    AO = dict(CBT=P.sb("CBT", [128, 128], F32), rhsB=P.sb("rhsB", [128, 1024], F32), Bm=P.sb("Bm", [128, 1024], F32),
              tuA=P.sb("tuA", [128, 1024], F32), MA=P.sb("MA", [128, 1024], BF16))
    onesf = P.sb("onesf", [128, 128], F32)
    memset(P, "pool", onesf[:, :], 1.0, [onesf])
    BO = dict(ytm=P.sb("ytm", [128, 512], F32), t2=P.sb("t2", [128, 512], F32), ssq=P.sb("ssq", [128, 2], F32),
              ybf=P.sb("ybf", [128, 512], BF16), wcol=P.sb("wcol", [128, 8], F32), elast=P.sb("elast", [128, 8], F32),
              xdtw=P.sb("xdtw", [128, 512], BF16))
    HS = [dict(Btm=P.sb("Btm", [128, 128], BF16), xtm=P.sb("xtm", [128, 512], F32), xdt=P.sb("xdt", [128, 512], BF16),
               rbl=P.sb("rbl", [128, 8], F32), zs=P.sb("zs", [128, 512], F32), yps=C.acc[i]) for i in range(2)]
    oT = P.sb("oT", [128, 4, S], BF16)
    Drow = sm[:, SM0["sD"]:SM0["sD"] + 8]
    grow = sm[:, SM0["snorm"]:SM0["snorm"] + 512]
    def partA(c, H):
        cs = slice(c * 128, (c + 1) * 128)
        CBT = AO["CBT"]
        Btm, xtm, xdt, rbl, zs, yps = H["Btm"], H["xtm"], H["xdt"], H["rbl"], H["zs"], H["yps"]
        cb_, cbd = fs(C)
        mm(P, cb_, BT[:, cs], CT[:, cs], True, True, [BT, CT], [cbd])
        cp(P, "act", CBT[:, :], cb_, [cbd], [CBT])
        bt_, btd = bs(C)
        tr(P, bt_, BT[:, cs], C.ident_bf[:, :], [BT, C.ident_bf], [btd])
        cp(P, "act", Btm[:, :], bt_, [btd], [Btm])
        yield
        for j in range(4):
            xt_, xtd = bs(C)
            tr(P, xt_, xT[:, j, cs], C.ident_bf[:, :], [xT.p(j), C.ident_bf], [xtd])
            cp(P, "act", xtm[:, j * 128:(j + 1) * 128], xt_, [xtd], [xtm])
        tt(P, "dve", xdt[:, :].rearrange("p (h d) -> p h d", d=64), xtm[:, :].rearrange("p (h d) -> p h d", d=64),
           bc(dt[:, c, :], 64), ALU.mult, [xtm, dt], [xdt])
        yield
        rhsB, Bm, tuA, MA = AO["rhsB"], AO["Bm"], AO["tuA"], AO["MA"]
        v3 = lambda t_: t_[:, :].rearrange("p (h j) -> p h j", j=128)
        tri3 = tri.unsqueeze(1).to_broadcast([128, 8, 128])
        negu3 = cf[:, CF_NEGU:CF_NEGU + 128].unsqueeze(1).to_broadcast([128, 8, 128])
        tt(P, "dve", v3(rhsB), tri3, bc(acol[:, c, :], 128), ALU.mult, [cf, acol], [rhsB])
        tt(P, "pool", v3(Bm), negu3, bc(acs[:, c, :], 128), ALU.subtract, [cf, acs], [Bm])
        yield
        for hf_ in range(2):
            bank = C.pf[hf_]
            mm(P, bank[:, :], onesf[:, :], rhsB[:, hf_ * 512:(hf_ + 1) * 512], True, True, [onesf, rhsB], [bank])
            cp(P, "act", rbl[:, hf_ * 4:(hf_ + 1) * 4], bank[:, :].rearrange("p (h j) -> p h j", j=128)[:, :, 127], [bank], [rbl])
            tt(P, "dve", tuA[:, hf_ * 512:(hf_ + 1) * 512], bank[:, :], Bm[:, hf_ * 512:(hf_ + 1) * 512], ALU.add, [bank, Bm], [tuA])
            yield
        actf(P, tuA[:, :], tuA[:, :], AF.Exp, [tuA], [tuA])
        cbt3 = CBT[:, :].unsqueeze(1).to_broadcast([128, 8, 128])
        tt(P, "dve", v3(MA), v3(tuA), cbt3, ALU.mult, [tuA, CBT], [MA])
        yield
        for hd in range(8):
            mm(P, yps[:, hd * 64:(hd + 1) * 64], MA[:, hd * 128:(hd + 1) * 128], xdt[:, hd * 64:(hd + 1) * 64], True, True, [MA, xdt], [yps])
        yield
        proj_tm(C, wz, 512, c, zs[:, :], zs, bank=C.acc[3], func=AF.Silu)

    def partB(c, H):
        cs = slice(c * 128, (c + 1) * 128)
        Btm, xtm, xdt, rbl, zs, yps = H["Btm"], H["xtm"], H["xdt"], H["rbl"], H["zs"], H["yps"]
        ytm, t2, ssq, ybf, wcol, elast, xdtw = BO["ytm"], BO["t2"], BO["ssq"], BO["ybf"], BO["wcol"], BO["elast"], BO["xdtw"]
        yoff = upd = C.acc[2]
        mm(P, yoff[:, :], CT[:, cs], Sbf[:, :], True, True, [CT, Sbf], [yoff])
        tt(P, "dve", ytm[:, :].rearrange("p (h d) -> p h d", d=64), yoff[:, :].rearrange("p (h d) -> p h d", d=64),
           bc(eacs[:, c, :], 64), ALU.mult, [yoff, eacs], [ytm])
        tt(P, "dve", ytm[:, :], ytm[:, :], yps[:, :], ALU.add, [ytm, yps], [ytm])
        yield
        tt(P, "dve", wcol[:, :], rbl[:, :], acs[:, c, :], ALU.subtract, [rbl, acs], [wcol])
        actf(P, wcol[:, :], wcol[:, :], AF.Exp, [wcol], [wcol])
        actf(P, elast[:, :], rbl[:, :], AF.Exp, [rbl], [elast])
        tt(P, "dve", xdtw[:, :].rearrange("p (h d) -> p h d", d=64), xdt[:, :].rearrange("p (h d) -> p h d", d=64),
           bc(wcol[:, :], 64), ALU.mult, [xdt, wcol], [xdtw])
        mm(P, upd[:, :], Btm[:, :], xdtw[:, :], True, True, [Btm, xdtw], [upd])
        tt(P, "dve", Sst[:, :].rearrange("p (h d) -> p h d", d=64), Sst[:, :].rearrange("p (h d) -> p h d", d=64),
           bc(elast[:, :], 64), ALU.mult, [Sst, elast], [Sst])
        tt(P, "dve", Sst[:, :], Sst[:, :], upd[:, :], ALU.add, [Sst, upd], [Sst])
        cp(P, "act", Sbf[:, :], Sst[:, :], [Sst], [Sbf])
        yield
        tt(P, "pool", t2[:, :].rearrange("p (h d) -> p h d", d=64), xtm[:, :].rearrange("p (h d) -> p h d", d=64),
           bc(Drow, 64), ALU.mult, [xtm, sm], [t2])
        tt(P, "dve", ytm[:, :], ytm[:, :], t2[:, :], ALU.add, [ytm, t2], [ytm])
        if "dbg_y0" in io:
            P.dma("sp", io["dbg_y0"][cs, :], ytm[:, :], reads=[ytm], writes=[io["dbg_y0"]], sem="st_dbg", ring=2)
        tt(P, "dve", ytm[:, :], ytm[:, :], zs[:, :], ALU.mult, [ytm, zs], [ytm])
        tt(P, "pool", t2[:, :], ytm[:, :], ytm[:, :], ALU.mult, [ytm], [t2])
        yield
        P.op("dve", lambda e, o=ssq[:, 0:1], i=t2[:, :]: e.tensor_reduce(out=o, in_=i, axis=AX.X, op=ALU.add), [t2], [ssq])
        actf(P, ssq[:, 1:2], ssq[:, 0:1], AF.Sqrt, [ssq, C.epsc], [ssq], bias=C.epsc[:, 0:1], scale=1.0 / 512.0)
        recip(P, ssq[:, 1:2], ssq[:, 1:2], [ssq], [ssq])
        stt(P, ybf[:, :], ytm[:, :], ssq[:, 1:2], grow, ALU.mult, ALU.mult, [ytm, ssq, sm], [ybf])
        yield
        for j in range(4):
            yt_, ytd = bs(C)
            tr(P, yt_, ybf[:, j * 128:(j + 1) * 128], C.ident_bf[:, :], [ybf, C.ident_bf], [ytd])
            cp(P, "act", oT[:, j, cs], yt_, [ytd], [oT.p(j)])
            yield

    C.fbanks = [C.pf[0], C.pf[1]]
    interleave([partA(0, HS[0])])
    for c in range(16):
        interleave([partB(c, HS[c % 2])] + ([partA(c + 1, HS[(c + 1) % 2])] if c + 1 < 16 else []))
    C.fbanks = [C.pf[0], C.pf[1], C.acc[2], C.acc[3]]
    og = io["o_half"].mine if isinstance(io["o_half"], DynSlot) else io["o_half"]
    for j in range(4):
        P.dma("sp", og[512 + j * 128:512 + (j + 1) * 128, :], oT[:, j, :], reads=[oT.p(j)], writes=[io["o_half"]], sem="st_o", ring=2)
    P.pop()


def prep_mix0(inp, b, hh):
    d = {}
    W = inp["ab_w_in"][0]

    def fm(c0):
        return W[:, c0:c0 + 128].reshape(16, 128, 128).transpose(1, 0, 2)

    def tmw(cols):
        return np.ascontiguousarray(W[:, cols].reshape(16, 128, len(cols)).transpose(1, 0, 2))
    tl = []
    for hl in range(4):
        h = 4 * hh + hl
        tl += [fm(h * 128), fm(1024 + h * 128), fm(2048 + h * 128), fm(3072 + h * 128)]
    for j in range(4):
        tl.append(fm(5136 + hh * 512 + j * 128))
    tl.append(fm(6160 + hh * 128))
    tl.append(fm(6416 + hh * 128))
    d["wfm_m0"] = np.ascontiguousarray(np.stack(tl))
    d["wg_m0"] = tmw([4096 + 4 * hh + i for i in range(4)] + [4104 + 4 * hh + i for i in range(4)])
    d["wz_m0"] = tmw(list(range(4112 + hh * 512, 4112 + hh * 512 + 512)))
    d["wdt_m0"] = tmw(list(range(6672 + 8 * hh, 6672 + 8 * hh + 8)))
    sm = np.zeros((128, SM0["W"]), np.float32)
    gcw = inp["gdn_conv_w"][0]
    for hl in range(4):
        h = 4 * hh + hl
        for wi in range(3):
            c0 = wi * 1024 + h * 128
            sm[:, SM0["gcw"] + (hl * 3 + wi) * 4:SM0["gcw"] + (hl * 3 + wi) * 4 + 4] = gcw[:, c0:c0 + 128].T
    sm[:, SM0["gnorm"]] = inp["gdn_norm"][0]
    mcw = inp["m2_conv_w"][0]
    mcb = inp["m2_conv_b"][0]
    offs = [hh * 512 + j * 128 for j in range(4)] + [1024 + hh * 128, 1280 + hh * 128]
    for j, c0 in enumerate(offs):
        sm[:, SM0["scw"] + j * 4:SM0["scw"] + j * 4 + 4] = mcw[:, c0:c0 + 128].T
        sm[:, SM0["scb"] + j] = mcb[c0:c0 + 128]
    sm[:, SM0["gdtb"]:SM0["gdtb"] + 64] = np.tile(inp["gdn_dt_bias"][0][4 * hh:4 * hh + 4], 16)[None, :]
    sm[:, SM0["galog"]:SM0["galog"] + 64] = np.tile(inp["gdn_a_log"][0][4 * hh:4 * hh + 4], 16)[None, :]
    sm[:, SM0["sdtb"]:SM0["sdtb"] + 128] = np.tile(inp["m2_dt_bias"][0][8 * hh:8 * hh + 8], 16)[None, :]
    sm[:, SM0["salog"]:SM0["salog"] + 128] = np.tile(inp["m2_a_log"][0][8 * hh:8 * hh + 8], 16)[None, :]
    sm[:, SM0["sD"]:SM0["sD"] + 8] = inp["m2_d"][0][8 * hh:8 * hh + 8][None, :]
    sm[:, SM0["snorm"]:SM0["snorm"] + 512] = inp["m2_norm"][0][512 * hh:512 * hh + 512][None, :]
    sm[:, SM0["nmix"]:SM0["nmix"] + 16] = cols128(inp["norm_mix"][0])
    d["sm_m0"] = sm
    return d


def decl_mix0(P):
    io = {}
    io["wfm"] = P.dram("wfm_m0", [22, 128, 16, 128], F32, kind="ExternalInput")
    io["wg"] = P.dram("wg_m0", [128, 16, 8], F32, kind="ExternalInput")
    io["wz"] = P.dram("wz_m0", [128, 16, 512], F32, kind="ExternalInput")
    io["wdt"] = P.dram("wdt_m0", [128, 16, 8], F32, kind="ExternalInput")
    io["sm"] = P.dram("sm_m0", [128, SM0["W"]], F32, kind="ExternalInput")
    return io


SM1 = dict(lb0=0, lb1=4, hnorm=8, qn=9, kn=10, nmix=11, W=27)


def mixer1(C, ios, xs, hn_gath=None):
    P = C.P
    P.push()
    mix_common(C)
    first = True
    if hn_gath is not None:
        for r in range(2):
            P.dma("sp", C.hnT[:, :, r * TM:(r + 1) * TM], hn_gath[r].rearrange("(k p) t -> p k t", p=128), reads=[hn_gath], writes=[C.hnT], sem="ld_hn", ring=2)
        first = False
    for io in ios:
        P.push()
        sm = P.sb("sm1", [128, SM1["W"]], F32)
        P.dma("sp", sm[:, :], io["sm"][:, :], writes=[sm], sem="ld_sm")
        tiles = [(io["wfm"][i], 16, i) for i in range(20)]
        ws = WS(P, "wmix", tiles, nslot=3)
        if first:
            load_hn(C, xs, sm[:, SM1["nmix"]:SM1["nmix"] + 16], sm)
            first = False
        hgrn(C, io, sm, ws)
        moba(C, io, sm, ws)
        P.pop()
    P.pop()


def hgrn(C, io, sm, ws):
    P = C.P
    cf = C.cf
    P.push()
    whi = P.sb("whi", [128, KC, 512], BF16)
    P.dma("pool", whi[:, :, :], io["whi"][:, :, :], writes=[whi], sem="ld_wz")
    lb = P.sb("lb", [128, 8], F32)
    tt(P, "dve", lb[:, 0:4], sm[:, SM1["lb1"]:SM1["lb1"] + 4], sm[:, SM1["lb0"]:SM1["lb0"] + 4], ALU.subtract, [sm], [lb])
    actf(P, lb[:, 0:4], lb[:, 0:4], AF.Sigmoid, [lb], [lb])
    ts(P, "dve", lb[:, 4:8], lb[:, 0:4], -1.0, 1.0, ALU.mult, ALU.add, [lb], [lb])
    ones = P.sb("onesf", [128, S], F32)
    memset(P, "pool", ones[:, :], 1.0, [ones])
    qT = P.sb("qT", [128, S], F32)
    fT = P.sb("fT", [128, S], F32)
    kT = P.sb("kT", [128, S], F32)
    bG = P.sb("bG", [128, S], F32)
    y = P.sb("y", [128, S], F32)
    vi = P.sb("vi", [128, 16, 128], BF16)
    oraw = P.sb("oraw", [128, S], F32)
    ob = P.sb("ob", [128, S], BF16)
    Sst = P.sb("Sst", [128, 128], F32)
    Sbf = P.sb("Sbf", [128, 128], BF16)
    psets = [dict(n3=P.sb("nb", [128, 4], F32), ex=[P.sb("ex", [128, 128], F32) for _ in range(4)], qe=P.sb("qe", [128, 128], BF16),
                  ke=P.sb("ke", [128, 128], BF16), kdT=P.sb("kdT", [128, 128], BF16)) for _ in range(4)]
    qsA = P.sb("qsA", [128, 16, 128], BF16)
    ATA = P.sb("ATA", [128, 16, 128], BF16)
    kdA = P.sb("kdA", [128, 16, 128], BF16)
    edec = P.sb("edec", [128, 16], F32)
    og = io["o_half"].mine if isinstance(io["o_half"], DynSlot) else io["o_half"]
    ogd = io["o_half"]
    import os
    for hl in range(4):
        proj_plain(C, ws.next(hl * 3 + 0), qT)
        actf(P, qT[:, :], qT[:, :], AF.Silu, [qT], [qT])
        proj_plain(C, ws.next(hl * 3 + 1), fT)
        actf(P, fT[:, :], fT[:, :], AF.Sigmoid, [fT], [fT])
        ts(P, "dve", fT[:, :], fT[:, :], lb[:, 4 + hl:5 + hl], lb[:, hl:hl + 1], ALU.mult, ALU.add, [fT, lb], [fT])
        ts(P, "dve", kT[:, :], fT[:, :], -1.0, 1.0, ALU.mult, ALU.add, [fT], [kT])
        actf(P, fT[:, :], fT[:, :], AF.Ln, [fT], [fT])
        P.op("dve", lambda e: e.tensor_tensor_scan(out=bG[:, :], data0=ones[:, :], data1=fT[:, :], initial=0.0, op0=ALU.mult, op1=ALU.add), [ones, fT], [bG])
        for g4 in range(4):
            proj_tm4(C, whi[:, :, hl * 128:(hl + 1) * 128], g4, vi, [vi.p(4 * g4 + j) for j in range(4)], whi, C.acc[g4 % 4])
        def pre(c, B):
            cs = slice(c * 128, (c + 1) * 128)
            mid, end, prev = c * 128 + 63, c * 128 + 127, c * 128 - 1
            n3, ex, qe, ke, kdT = B["n3"], B["ex"], B["qe"], B["ke"], B["kdT"]
            ts(P, "dve", n3[:, 0:1], bG[:, mid:mid + 1], -1.0, None, ALU.mult, None, [bG], [n3])
            if c > 0:
                ts(P, "dve", n3[:, 1:2], bG[:, prev:prev + 1], -1.0, None, ALU.mult, None, [bG], [n3])
            else:
                memset(P, "dve", n3[:, 1:2], 0.0, [n3])
            yield
            actf(P, ex[0][:, :], bG[:, cs], AF.Exp, [bG, n3], [ex[0]], bias=n3[:, 0:1])
            actf(P, ex[1][:, :], bG[:, cs], AF.Exp, [bG], [ex[1]], bias=bG[:, mid:mid + 1], scale=-1.0)
            actf(P, ex[2][:, :], bG[:, cs], AF.Exp, [bG, n3], [ex[2]], bias=n3[:, 1:2])
            actf(P, ex[3][:, :], bG[:, cs], AF.Exp, [bG], [ex[3]], bias=bG[:, end:end + 1], scale=-1.0)
            actf(P, edec[:, c:c + 1], bG[:, end:end + 1], AF.Exp, [bG, n3], [edec.p(c)], bias=n3[:, 1:2])
            yield
            tt(P, "dve", qe[:, :], qT[:, cs], ex[0][:, :], ALU.mult, [qT, ex[0]], [qe])
            tt(P, "dve", ke[:, :], kT[:, cs], ex[1][:, :], ALU.mult, [kT, ex[1]], [ke])
            tt(P, "dve", qsA[:, c, :], qT[:, cs], ex[2][:, :], ALU.mult, [qT, ex[2]], [qsA.p(c)])
            tt(P, "pool", kdT[:, :], kT[:, cs], ex[3][:, :], ALU.mult, [kT, ex[3]], [kdT])
            yield
            a_, ad = fs(C)
            mm(P, a_, ke[:, :], qe[:, :], True, True, [ke, qe], [ad])
            tt(P, "dve", ATA[:, c, :], a_, cf[:, CF_MU:CF_MU + 128], ALU.mult, [ad, cf], [ATA.p(c)])
            yield
            k_, kdd = bs(C)
            tr(P, k_, kdT[:, :], C.ident_bf[:, :], [kdT, C.ident_bf], [kdd])
            cp(P, "act", kdA[:, c, :], k_, [kdd], [kdA.p(c)])

        for c0 in range(0, 16, 4):
            interleave([pre(c0 + i, psets[i]) for i in range(4)])
        memset(P, "pool", Sst[:, :], 0.0, [Sst])
        memset(P, "pool", Sbf[:, :], 0.0, [Sbf])
        for c in range(16):
            cs = slice(c * 128, (c + 1) * 128)
            ob_, obd = fs(C)
            mm(P, ob_, vi[:, c, :], ATA[:, c, :], True, False, [vi.p(c), ATA.p(c)], [obd])
            mm(P, ob_, Sbf[:, :], qsA[:, c, :], False, True, [Sbf, qsA.p(c)], [obd])
            cp(P, "act", oraw[:, cs], ob_, [obd], [oraw.p(c)])
            s1, s1d = fs(C)
            mm(P, s1, kdA[:, c, :], vi[:, c, :], True, True, [kdA.p(c), vi.p(c)], [s1d])
            stt(P, Sst[:, :], Sst[:, :], edec[:, c:c + 1], s1, ALU.mult, ALU.add, [Sst, edec.p(c), s1d], [Sst])
            cp(P, "act", Sbf[:, :], Sst[:, :], [Sst], [Sbf])
        if "dbg_hraw" in io:
            P.dma("sp", io["dbg_hraw"][hl * 128:(hl + 1) * 128, :], oraw[:, :], reads=[oraw], writes=[io["dbg_hraw"]], sem="st_dbg", ring=2)
        proj_plain(C, ws.next(hl * 3 + 2), y)
        actf(P, y[:, :], y[:, :], AF.Silu, [y], [y])
        pnorm(C, oraw, S, sm[:, SM1["hnorm"]:SM1["hnorm"] + 1], sm, oraw[:, :], oraw)
        tt(P, "dve", ob[:, :], oraw[:, :], y[:, :], ALU.mult, [oraw, y], [ob])
        P.dma("sp", og[hl * 128:(hl + 1) * 128, :], ob[:, :], reads=[ob], writes=[ogd], sem="st_o", ring=2)
    P.pop()


def moba(C, io, sm, ws):
    P = C.P
    cf = C.cf
    P.push()
    wmv = P.sb("wmv", [128, KC, 512], BF16)
    cosF = P.sb("cosF", [128, S], F32)
    sinF = P.sb("sinF", [128, S], F32)
    tf = P.sb("tf", [128, S], F32)
    npi = P.sb("npi", [128, 1], F32)
    memset(P, "pool", npi[:, :], -float(np.pi), [npi])
    selb = P.sb("selb", [128, 8, 128], BF16)
    cp(P, "dve", selb[:, :, :], cf[:, CF_SEL:CF_SEL + 1024].rearrange("p (n k) -> p n k", k=128), [cf], [selb])
    maskA = P.sb("maskA", [128, 256], BF16)
    cp(P, "dve", maskA[:, 0:128], cf[:, CF_MU:CF_MU + 128], [cf], [maskA])
    memset(P, "pool", maskA[:, 128:256], 1.0, [maskA])
    y = P.sb("y", [128, S], F32)
    qf = P.sb("qf", [128, S], F32)
    qbs = [P.sb("qb", [128, S], BF16) for _ in range(2)]
    kbs = [P.sb("kb", [128, S], BF16) for _ in range(2)]
    vtms = [P.sb("vtm", [128, 16, 128], BF16) for _ in range(2)]
    nselTs = [P.sb("nselT", [128, S], BF16) for _ in range(2)]
    kmean = P.sb("kmean", [128, 8], F32)
    gate = P.sb("gate", [128, 16, 8], F32)
    mx8 = P.sb("mx8", [128, 8], F32)
    nsel = P.sb("nsel", [128, 128], F32)
    memset(P, "pool", nsel[:, :], 0.0, [nsel])
    pT = [P.sb("pT", [128, 256], BF16) for _ in range(3)]
    rden = P.sb("rden", [128, 256], F32)
    ob = P.sb("ob", [128, S], BF16)
    posi = qf[:, :].bitcast(I32)
    ti = wmv[:, 0:8, :].bitcast(I32).rearrange("p a b -> p (a b)")
    P.dma("sp", posi, io["pos"][0:1, :].partition_broadcast(128), writes=[qf], sem="ld_pos")
    cp(P, "dve", y[:, :], posi, [qf], [y])
    ts(P, "dve", y[:, :], y[:, :], cf[:, CF_INV:CF_INV + 1], 1.0 / (2.0 * np.pi), ALU.mult, ALU.mult, [y, cf], [y])
    for dst, off in ((sinF, 0.0), (cosF, 0.25)):
        ts(P, "dve", tf[:, :], y[:, :], off, None, ALU.add, None, [y], [tf])
        cp(P, "dve", ti, tf[:, :], [tf], [wmv])
        cp(P, "dve", dst[:, :], ti, [wmv], [dst])
        tt(P, "dve", tf[:, :], tf[:, :], dst[:, :], ALU.subtract, [tf, dst], [tf])
        ts(P, "dve", dst[:, :], tf[:, :], 0.0, None, ALU.is_lt, None, [tf], [dst])
        tt(P, "dve", tf[:, :], tf[:, :], dst[:, :], ALU.add, [tf, dst], [tf])
        actf(P, dst[:, :], tf[:, :], AF.Sin, [tf, npi], [dst], bias=npi[:, 0:1], scale=2.0 * float(np.pi))
        ts(P, "dve", dst[:, :], dst[:, :], -1.0, None, ALU.mult, None, [dst], [dst])
    P.dma("pool", wmv[:, :, :], io["wmv"][:, :, :], writes=[wmv], sem="ld_wz")
    og = io["o_half"].mine if isinstance(io["o_half"], DynSlot) else io["o_half"]
    ogd = io["o_half"]
    sc = 128.0 ** -0.5
    pfb = [C.pf[0], C.pf[1]]
    C.fbanks = pfb
    NH = 4
    MBW = 3

    def prep(hl, D):
        qb, kb, vtm, nselT = qbs[D], kbs[D], vtms[D], nselTs[D]
        for which in range(2):
            yield from proj_plain_g(C, ws.next(12 + hl * 2 + which), y, fixed_banks=pfb)
            gcol = sm[:, SM1["qn"] + which:SM1["qn"] + which + 1]
            pnorm(C, y, S, gcol, sm, y[:, :], y)
            yield
            dstf = qf if which == 0 else y
            for (t0, w) in blocks_of(S):
                r_, bank = fs(C)
                mm(P, bank[:, 0:w], cf[:, CF_ROT:CF_ROT + 128], y[:, t0:t0 + w], True, True, [cf, y], [bank])
                tt(P, "dve", tf[:, t0:t0 + w], bank[:, 0:w], sinF[:, t0:t0 + w], ALU.mult, [bank, sinF], [tf.p(t0)])
                yield
            tt(P, "dve", dstf[:, :], y[:, :], cosF[:, :], ALU.mult, [y, cosF], [dstf])
            tt(P, "dve", dstf[:, :], dstf[:, :], tf[:, :], ALU.add, [dstf, tf], [dstf])
            cp(P, "act", (qb if which == 0 else kb)[:, :], dstf[:, :], [dstf], [qb if which == 0 else kb])
            yield
        P.op("dve", lambda e: e.tensor_reduce(out=kmean[:, :], in_=y[:, :].rearrange("p (n k) -> p n k", k=256), axis=AX.X, op=ALU.add), [y], [kmean])
        ts(P, "dve", kmean[:, :], kmean[:, :], 1.0 / 256.0, None, ALU.mult, None, [kmean], [kmean])
        for g4 in range(4):
            proj_tm4(C, wmv[:, :, hl * 128:(hl + 1) * 128], g4, vtm, [vtm.p(4 * g4 + j) for j in range(4)], wmv, pfb[g4 % 2])
            yield
        memset(P, "pool", gate[:, :, :], -1.0e30, [gate])
        for t in range(2, 16):
            jq = t // 2
            g_, gd = fs(C)
            mm(P, g_[:, 0:8], qf[:, t * 128:(t + 1) * 128], kmean[:, :], True, True, [qf, kmean], [gd])
            cp(P, "act", gate[:, t, 0:jq], g_[:, 0:jq], [gd], [gate.p(t)])
            P.op("dve", lambda e, o=mx8[:, :], i=gate[:, t, :]: e.max(out=o, in_=i), [gate.p(t)], [mx8])
            ts(P, "dve", nsel[:, 0:8], gate[:, t, :], mx8[:, 2:3], 1.0, ALU.is_ge, ALU.subtract, [gate.p(t), mx8], [nsel])
            n_, nd = fs(C)
            mm(P, n_[:, :], nsel[:, :], cf[:, CF_ID:CF_ID + 128], True, True, [nsel, cf], [nd])
            cp(P, "act", nselT[:, t * 128:(t + 1) * 128], n_[:, :], [nd], [nselT.p(t)])
            yield

    def attend(hl, D):
        qb, kb, vtm, nselT = qbs[D], kbs[D], vtms[D], nselTs[D]
        steps = []
        for jq in range(8):
            kts = [(kt, "past", kt // 2) for kt in range(2 * jq)] + [(2 * jq, "own0", jq), (2 * jq + 1, "own1", jq)]
            for idx, (kt, kind, n) in enumerate(kts):
                steps.append((jq, kt, kind, n, idx == 0, idx == len(kts) - 1))

        def score(i):
            jq, kt, kind, n, first, last = steps[i]
            q0 = jq * 256
            sb_ = C.acc[i % 2]
            p_ = pT[i % 3]
            c0, w = (128, 128) if kind == "own1" else (0, 256)
            if kind == "past":
                mm(P, sb_[:, 0:256], kb[:, kt * 128:(kt + 1) * 128], qb[:, q0:q0 + 256], True, False, [kb, qb], [sb_])
                mm(P, sb_[:, 0:256], selb[:, n, :], nselT[:, q0:q0 + 256], False, True, [selb, nselT.p(2 * jq), nselT.p(2 * jq + 1)], [sb_])
                actf(P, p_[:, 0:256], sb_[:, 0:256], AF.Exp, [sb_], [p_], scale=sc)
            else:
                mm(P, sb_[:, 0:w], kb[:, kt * 128:(kt + 1) * 128], qb[:, q0 + c0:q0 + c0 + w], True, True, [kb, qb], [sb_])
                actf(P, p_[:, 0:w], sb_[:, 0:w], AF.Exp, [sb_], [p_], scale=sc)
                mk = maskA[:, 0:256] if kind == "own0" else maskA[:, 0:128]
                tt(P, "pool", p_[:, 0:w], p_[:, 0:w], mk, ALU.mult, [p_, maskA], [p_])

        def pv(i):
            jq, kt, kind, n, first, last = steps[i]
            q0 = jq * 256
            oT, den = C.acc[2], C.acc[3]
            p_ = pT[i % 3]
            c0, w = (128, 128) if kind == "own1" else (0, 256)
            mm(P, oT[:, c0:c0 + w], vtm[:, kt, :], p_[:, 0:w], first, last, [vtm.p(kt), p_], [oT])
            mm(P, den[:, c0:c0 + w], C.ones_bf[:, :], p_[:, 0:w], first, last, [C.ones_bf, p_], [den])
            if last:
                actf(P, rden[:, :], den[:, 0:256], AF.Ln, [den], [rden])
                actf(P, rden[:, :], rden[:, :], AF.Exp, [rden], [rden], scale=-1.0)
                tt(P, "dve", ob[:, q0:q0 + 256], oT[:, 0:256], rden[:, :], ALU.mult, [oT, rden], [ob.p(jq)])

        score(0)
        for i in range(len(steps)):
            if i + 1 < len(steps):
                score(i + 1)
            pv(i)
            yield
        P.dma("sp", og[512 + hl * 128:512 + (hl + 1) * 128, :], ob[:, :], reads=[ob], writes=[ogd], sem="st_o", ring=2)

    interleave([prep(0, 0)])
    for hl in range(NH):
        interleave([attend(hl, hl % 2)] + ([prep(hl + 1, (hl + 1) % 2)] if hl + 1 < NH else []), weights=[MBW, 1])
    C.fbanks = [C.pf[0], C.pf[1], C.acc[2], C.acc[3]]
    P.pop()


def prep_mix1(inp, b, hh):
    d = {}
    W = inp["cd_w_in"][0]

    def fm(c0):
        return W[:, c0:c0 + 128].reshape(16, 128, 128).transpose(1, 0, 2)

    def tmw(cols):
        return np.ascontiguousarray(W[:, cols].reshape(16, 128, len(cols)).transpose(1, 0, 2))
    tl = []
    for hl in range(4):
        h = 4 * hh + hl
        tl += [fm(h * 128), fm(1024 + h * 128), fm(3072 + h * 128)]
    for hl in range(4):
        h = 4 * hh + hl
        tl += [fm(4096 + h * 128), fm(5120 + h * 128)]
    d["wfm_m1"] = np.ascontiguousarray(np.stack(tl))
    d["whi_m1"] = tmw(list(range(2048 + hh * 512, 2048 + hh * 512 + 512)))
    d["wmv_m1"] = tmw(list(range(6144 + hh * 512, 6144 + hh * 512 + 512)))
    sm = np.zeros((128, SM1["W"]), np.float32)
    lb = inp["hgrn_lb"]
    sm[:, SM1["lb0"]:SM1["lb0"] + 4] = lb[0][hh * 512:hh * 512 + 512].reshape(4, 128).T
    sm[:, SM1["lb1"]:SM1["lb1"] + 4] = lb[1][hh * 512:hh * 512 + 512].reshape(4, 128).T
    sm[:, SM1["hnorm"]] = inp["hgrn_norm"][0]
    sm[:, SM1["qn"]] = inp["moba_qnorm"][0]
    sm[:, SM1["kn"]] = inp["moba_knorm"][0]
    sm[:, SM1["nmix"]:SM1["nmix"] + 16] = cols128(inp["norm_mix"][1])
    d["sm_m1"] = sm
    d["pos_m1"] = np.ascontiguousarray(inp["positions"][b:b + 1]).astype(np.int32)
    return d


def decl_mix1(P):
    io = {}
    io["wfm"] = P.dram("wfm_m1", [20, 128, 16, 128], F32, kind="ExternalInput")
    io["whi"] = P.dram("whi_m1", [128, 16, 512], F32, kind="ExternalInput")
    io["wmv"] = P.dram("wmv_m1", [128, 16, 512], F32, kind="ExternalInput")
    io["sm"] = P.dram("sm_m1", [128, SM1["W"]], F32, kind="ExternalInput")
    io["pos"] = P.dram("pos_m1", [1, S], I32, kind="ExternalInput")
    return io


NCORES = 8


def pair_sync(C, k, data):
    P = C.P
    fl, nonce = C.fl, C.nonce
    P.dma2("sp", fl.h[0, k:k + 1, :], fl.h[1, k:k + 1, :], nonce[0:1, k:k + 1], reads=list(data), writes=[fl], sem="flag")

    def spin(e):
        hr = P.hhreg(e)
        with e.register("spin_r%d" % k) as r, e.register("spin_x%d" % k) as r2:
            e.load(r2, nonce[0:1, k:k + 1])
            e.reg_mov(r, 1)
            with e.While(r):
                with e.If_eq(hr, 0):
                    e.load(r, fl.h[1, k:k + 1, :])
                with e.Else():
                    e.load(r, fl.h[0, k:k + 1, :])
                e.reg_sub(r, r, r2)
    P.custom("sp", spin, reads=[fl])


def build_all():
    nc = bass.Bass("TRN2", target_bir_lowering=False)
    P = Prog(nc)
    io = {}
    io["consts"] = P.dram("consts", [128, CF_W], F32, kind="ExternalInput")
    io["flags"] = P.dram("flags", [128, 2], F32, kind="ExternalInput")
    nonce = P.dram("nonce", [1, 8], I32, kind="ExternalInput")
    xT = P.dram("xT", [D, S], F32, kind="ExternalInput")
    memT = P.dram("memT", [D, 256], F32, kind="ExternalInput")
    outT = P.dram("outT", [D, TM], F32, kind="ExternalOutput")
    h1 = P.dram("h1", [D, TM], F32)

    def shared(name, shape, dt):
        return DynSlot(P, name, nc.dram_tensor(name, list(shape), dt, kind="Internal", addr_space="Shared").ap())
    o0 = shared("o0_sh", [2, 1024, S], BF16)
    o1 = shared("o1_sh", [2, 1024, S], BF16)
    hn1 = shared("hn1_sh", [2, D, TM], BF16)
    tail = shared("tail_sh", [2, D, 2], F32)
    fl = shared("flag_sh", [2, 8, 1], I32)
    m0 = decl_mix0(P)
    m1 = decl_mix1(P)
    rows = [decl_row(P, L, None) for L in range(2)]
    hin0 = P.dram("h_in_r0", [D, TM], F32, kind="ExternalInput")
    hhalo0 = P.dram("h_halo_r0", [D, 2], F32, kind="ExternalInput")
    C = make_ctx(P, io)
    C.fl, C.nonce = fl, nonce
    oloc = P.dram("o_loc", [1024, S], BF16)
    m0["o_half"] = oloc
    mixer0(C, [m0], xT)
    def xchg0():
        P.dma2("sp", o0.h[0], o0.h[1], oloc[:, :], reads=[oloc], writes=[o0], sem="cp_o")
        pair_sync(C, 0, [o0])
    rows[0].update(memT=memT, h_in=hin0, h_halo=hhalo0, o_gath=o0, h_out=h1, h_tail=tail, hn1_out=hn1)
    row_phase8(C, 0, rows[0], mid=xchg0)
    pair_sync(C, 1, [hn1, tail])
    m1["o_half"] = oloc
    mixer1(C, [m1], None, hn_gath=hn1)
    def xchg1():
        P.dma2("sp", o1.h[0], o1.h[1], oloc[:, :], reads=[oloc], writes=[o1], sem="cp_o")
        pair_sync(C, 2, [o1])
    rows[1].update(memT=memT, h_in=h1, h_halo=T("tail0", tail.h[0]), o_gath=o1, h_out=outT)
    row_phase8(C, 1, rows[1], mid=xchg1)
    P.finish()
    return nc


def kernel(**inp):
    inp = {k: np.asarray(v) for k, v in inp.items()}
    cores = list(range(NCORES))
    consts = make_consts()
    base = (int.from_bytes(os.urandom(4), "little") & 0x3FFFFFF0) | 0x10
    nonce = (base + np.arange(8)).astype(np.int32).reshape(1, 8)
    shared = {"consts": consts, "nonce": nonce}
    for L in range(2):
        shared.update(prep_row(inp, L, 0, 0))
    xT = [np.ascontiguousarray(inp["x"][b].T) for b in range(4)]
    memT = [np.ascontiguousarray(inp["mem"][b].T) for b in range(4)]
    maps = []
    for c in cores:
        b, hh = c // 2, c % 2
        d = dict(shared)
        fl = np.zeros((128, 2), np.float32)
        fl[:, 0] = 1 - hh
        fl[:, 1] = hh
        d["flags"] = fl
        d["xT"] = xT[b]
        d["memT"] = memT[b]
        d["h_in_r0"] = np.ascontiguousarray(xT[b][:, hh * TM:(hh + 1) * TM])
        d["h_halo_r0"] = np.ascontiguousarray(xT[b][:, TM - 2:TM])
        d.update(prep_mix0(inp, b, hh))
        d.update(prep_mix1(inp, b, hh))
        maps.append(d)
    r = run_bass_kernel_spmd(build_all(), maps, core_ids=cores).results
    out = np.empty((4, S, D), np.float32)
    for c in cores:
        b, hh = c // 2, c % 2
        out[b, hh * TM:(hh + 1) * TM, :] = np.asarray(r[c]["outT"]).T
    return out
```

```python
import os
import numpy as np
from contextlib import ExitStack
import ml_dtypes
import concourse.bass as bass
import concourse.mybir as mybir
from concourse.bass_utils import run_bass_kernel_spmd

F32 = mybir.dt.float32
BF16 = mybir.dt.bfloat16
I32 = mybir.dt.int32
AF = mybir.ActivationFunctionType
ALU = mybir.AluOpType
AX = mybir.AxisListType

D = 2048
S = 2048
TM = 1024
TH = TM + 2
KC = 16
EPS = 1e-6
NEG = -30000.0


class Buf:
    __slots__ = ("name", "w", "r")

    def __init__(self, name):
        self.name = name
        self.w = None
        self.r = []


class T:
    def __init__(self, name, h):
        self.name = name
        self.h = h
        self.whole = Buf(name)
        self.parts = {}
        self.psum = False

    def __getitem__(self, key):
        return self.h[key]

    def p(self, key):
        if key not in self.parts:
            self.parts[key] = Buf("%s/%s" % (self.name, key))
        return (self, key)


class Mine:
    def __init__(self, slot):
        self.slot = slot

    def __getitem__(self, key):
        slot = self.slot
        return lambda e: slot.h[slot.P.hh(e)][key]


class DynSlot(T):
    def __init__(self, P, name, ap):
        T.__init__(self, name, ap)
        self.P = P
        self.mine = Mine(self)


ENGS = ("pe", "act", "dve", "pool", "sp")
SKIP_SELF = set()


class Prog:
    def __init__(self, nc):
        self.nc = nc
        self.ops = {e: [] for e in ENGS}
        self.cnt = {e: 0 for e in ENGS}
        self.waited = {e: {} for e in ENGS}
        self.dma_cnt = {}
        self.semkeys = set(ENGS)
        self.sem = {}
        self.stack = ExitStack()
        self.scopes = []
        self.n_alloc = 0
        self.rings = {}
        self._hh = None

    def sb(self, name, shape, dtype):
        self.n_alloc += 1
        nm = "%s_%d" % (name, self.n_alloc)
        st = self.scopes[-1] if self.scopes else self.stack
        return T(nm, st.enter_context(self.nc.sbuf_tensor(nm, list(shape), dtype)))

    def ps(self, name, shape, dtype):
        self.n_alloc += 1
        nm = "%s_%d" % (name, self.n_alloc)
        st = self.scopes[-1] if self.scopes else self.stack
        t = T(nm, st.enter_context(self.nc.psum_tensor(nm, list(shape), dtype)))
        t.psum = True
        return t

    def dram(self, name, shape, dtype, kind="Internal"):
        return T(name, self.nc.dram_tensor(name, list(shape), dtype, kind=kind).ap())

    def push(self):
        self.scopes.append(ExitStack())

    def pop(self):
        self.barrier()
        self.scopes.pop().close()

    def _bufs(self, item):
        if isinstance(item, T):
            return [item.whole], [item.whole] + list(item.parts.values())
        t, key = item
        t.p(key)
        return [t.parts[key]], [t.whole, t.parts[key]]

    def _deps(self, reads, writes):
        deps = {}

        def add(rec):
            if rec is not None and deps.get(rec[0], 0) < rec[1]:
                deps[rec[0]] = rec[1]
        for it in reads:
            for b in self._bufs(it)[1]:
                add(b.w)
        for it in writes:
            for b in self._bufs(it)[1]:
                add(b.w)
                for rr in b.r:
                    add(rr)
        return deps

    def _commit(self, reads, writes, rec):
        for it in reads:
            for b in self._bufs(it)[0]:
                b.r.append(rec)
                if len(b.r) > 48:
                    mx = {}
                    for k, v in b.r:
                        if mx.get(k, 0) < v:
                            mx[k] = v
                    b.r = list(mx.items())
        for it in writes:
            if isinstance(it, T):
                for b in it.parts.values():
                    b.w = rec
                    b.r = []
            for b in self._bufs(it)[0]:
                b.w = rec
                b.r = []

    def _waits(self, eng, deps, skip_self=False):
        ws = []
        for k, v in deps.items():
            if skip_self and k == eng:
                continue
            if self.waited[eng].get(k, 0) < v:
                self.waited[eng][k] = v
                ws.append((k, v))
        return ws

    def op(self, eng, fn, reads=(), writes=(), mm=False):
        isps = lambda it: (it.psum if isinstance(it, T) else it[0].psum)
        writes = list(writes) + [it for it in reads if isps(it)]
        reads = [it for it in reads if not isps(it)]
        deps = self._deps(reads, writes)
        ws = self._waits(eng, deps, skip_self=(mm or eng in SKIP_SELF))
        self.cnt[eng] += 1
        self._commit(reads, writes, (eng, self.cnt[eng]))
        self.ops[eng].append((ws, fn, (eng, 1)))

    def dma(self, eng, out, in_, reads=(), writes=(), sem="dma", ring=1, **kw):
        if ring > 1:
            i = self.rings.get(sem, 0)
            self.rings[sem] = i + 1
            sem = "%s_%d" % (sem, i % ring)
        deps = self._deps(reads, writes)
        prev = self.dma_cnt.get(sem, 0)
        if prev:
            deps[sem] = max(deps.get(sem, 0), prev)
        ws = self._waits(eng, deps)
        self.semkeys.add(sem)
        self.dma_cnt[sem] = prev + 16
        self._commit(reads, writes, (sem, self.dma_cnt[sem]))
        if callable(out):
            self.ops[eng].append((ws, lambda e: e.dma_start(out=out(e), in_=in_, **kw), (sem, 16)))
        else:
            self.ops[eng].append((ws, lambda e: e.dma_start(out=out, in_=in_, **kw), (sem, 16)))

    def custom(self, eng, fn, reads=()):
        deps = self._deps(reads, ())
        ws = self._waits(eng, deps)
        self.ops[eng].append((ws, fn, None))

    def hhreg(self, e):
        if self._hh is None:
            self._hh = e.to_reg(e.partition_id() % 2)
        return self._hh

    def dma2s(self, eng, out, in0, in1, reads=(), writes=(), sem="dma2s"):
        deps = self._deps(reads, writes)
        prev = self.dma_cnt.get(sem, 0)
        if prev:
            deps[sem] = max(deps.get(sem, 0), prev)
        ws = self._waits(eng, deps)
        self.semkeys.add(sem)
        self.dma_cnt[sem] = prev + 16
        self._commit(reads, writes, (sem, self.dma_cnt[sem]))

        def fn(e):
            r = self.hhreg(e)
            with e.If_eq(r, 0):
                e.dma_start(out=out, in_=in0).then_inc(self.sem[sem], 16)
            with e.Else():
                e.dma_start(out=out, in_=in1).then_inc(self.sem[sem], 16)
        self.ops[eng].append((ws, fn, None))

    def dma2(self, eng, out0, out1, in_, reads=(), writes=(), sem="dma2"):
        deps = self._deps(reads, writes)
        prev = self.dma_cnt.get(sem, 0)
        if prev:
            deps[sem] = max(deps.get(sem, 0), prev)
        ws = self._waits(eng, deps)
        self.semkeys.add(sem)
        self.dma_cnt[sem] = prev + 16
        self._commit(reads, writes, (sem, self.dma_cnt[sem]))

        def fn(e):
            r = self.hhreg(e)
            with e.If_eq(r, 0):
                e.dma_start(out=out0, in_=in_).then_inc(self.sem[sem], 16)
            with e.Else():
                e.dma_start(out=out1, in_=in_).then_inc(self.sem[sem], 16)
        self.ops[eng].append((ws, fn, None))

    def raw(self, eng, fn, reads=(), writes=(), sem=None, inc=16):
        deps = self._deps(reads, writes)
        ws = self._waits(eng, deps)
        self.semkeys.add(sem)
        self.dma_cnt[sem] = self.dma_cnt.get(sem, 0) + inc
        self._commit(reads, writes, (sem, self.dma_cnt[sem]))
        self.ops[eng].append((ws, fn, (sem, inc)))

    def barrier(self):
        tot = dict((e, self.cnt[e]) for e in ENGS)
        tot.update(self.dma_cnt)
        for e in ENGS:
            ws = self._waits(e, tot)
            if ws:
                self.ops[e].append((ws, None, None))

    def finish(self):
        nc = self.nc
        self.barrier()
        with ExitStack() as st:
            for k in sorted(self.semkeys):
                self.sem[k] = st.enter_context(nc.semaphore("s_" + k))
            block = st.enter_context(nc.Block())

            def runner(ename):
                def run(e):
                    for ws, fn, inc in self.ops[ename]:
                        for k, v in ws:
                            e.wait_ge(self.sem[k], v)
                        if fn is not None and inc is None:
                            fn(e)
                        elif fn is not None:
                            fn(e).then_inc(self.sem[inc[0]], inc[1])
                return run
            block.tensor(runner("pe"))
            block.scalar(runner("act"))
            block.vector(runner("dve"))
            block.gpsimd(runner("pool"))
            block.sync(runner("sp"))
        for s in self.scopes:
            s.close()
        self.stack.close()


def mm(P, out, lhsT, rhs, start, stop, reads, writes):
    P.op("pe", lambda e: e.matmul(out, lhsT=lhsT, rhs=rhs, start=start, stop=stop), reads, writes, mm=True)


def tr(P, out, in_, ident, reads, writes):
    P.op("pe", lambda e: e.transpose(out=out, in_=in_, identity=ident), reads, writes, mm=True)


def actf(P, out, in_, func, reads, writes, bias=0.0, scale=1.0):
    P.op("act", lambda e: e.activation(out=out, in_=in_, func=func, bias=bias, scale=scale), reads, writes)


def tt(P, eng, out, in0, in1, op, reads, writes):
    P.op(eng, lambda e: e.tensor_tensor(out=out, in0=in0, in1=in1, op=op), reads, writes)


def ts(P, eng, out, in0, s1, s2, op0, op1, reads, writes):
    if s2 is None:
        P.op(eng, lambda e: e.tensor_scalar(out=out, in0=in0, scalar1=s1, scalar2=None, op0=op0), reads, writes)
    else:
        P.op(eng, lambda e: e.tensor_scalar(out=out, in0=in0, scalar1=s1, scalar2=s2, op0=op0, op1=op1), reads, writes)


def stt(P, out, in0, scalar, in1, op0, op1, reads, writes):
    P.op("dve", lambda e: e.scalar_tensor_tensor(out=out, in0=in0, scalar=scalar, in1=in1, op0=op0, op1=op1), reads, writes)


def cp(P, eng, out, in_, reads, writes):
    if eng == "act":
        P.op("act", lambda e: e.copy(out=out, in_=in_), reads, writes)
    else:
        P.op(eng, lambda e: e.tensor_copy(out=out, in_=in_), reads, writes)


def recip(P, out, in_, reads, writes):
    P.op("dve", lambda e: e.reciprocal(out=out, in_=in_), reads, writes)


def memset(P, eng, ap, val, writes):
    P.op(eng, lambda e: e.memset(ap, val), (), writes)


class WS:
    def __init__(self, P, name, tiles, nslot=4, kcmax=16, ncol=128):
        self.P = P
        self.name = name
        self.tiles = tiles
        self.nslot = nslot
        self.slots = [P.sb(name + "_s", [128, kcmax, ncol], BF16) for _ in range(nslot)]
        self.issued = 0
        self.used = 0

    def _issue(self):
        ap, kc, tag = self.tiles[self.issued]
        i = self.issued % self.nslot
        slot = self.slots[i]
        self.P.dma("pool", slot[:, 0:kc, :], ap, writes=[slot], sem="%s_%d" % (self.name, i))
        self.issued += 1

    def prefetch(self):
        while self.issued < min(self.used + self.nslot, len(self.tiles)):
            self._issue()

    def next(self, tag=None):
        while self.issued < min(self.used + self.nslot, len(self.tiles)):
            self._issue()
        ap, kc, tg = self.tiles[self.used]
        assert tag is None or tg == tag, (tg, tag)
        slot = self.slots[self.used % self.nslot]
        self.used += 1
        return slot


class Ctx:
    pass


def blocks_of(n):
    out = []
    t = 0
    while t < n:
        w = min(512, n - t)
        out.append((t, w))
        t += w
    return out


def dense(C, wt, kcn, xT, blks, banks, xdep=None):
    P = C.P
    for kc in range(kcn):
        for i, (t0, w) in enumerate(blks):
            mm(P, banks[i][:, 0:w], wt[:, kc, :], xT[:, kc, t0:t0 + w], kc == 0, kc == kcn - 1,
               [wt, xdep if xdep is not None else xT.p(kc)], [banks[i]])


def rmsnorm_fm(C, h, kcn, blks, gcols, gdep, out, n_feat, ocol=0):
    P = C.P
    tot = blks[-1][0] + blks[-1][1]
    ssb = [C.misc[i] for i in range(len(blks))]
    for kc in range(kcn):
        sq = C.sq[kc % len(C.sq)]
        actf(P, sq[:, 0:tot], h[:, kc, 0:tot], AF.Square, [h.p(kc)], [sq])
        for i, (t0, w) in enumerate(blks):
            mm(P, ssb[i][:, 0:w], C.ones_bf[:, :], sq[:, t0:t0 + w], kc == 0, kc == kcn - 1, [sq, C.ones_bf], [ssb[i]])
    rstd = C.rstd
    for i, (t0, w) in enumerate(blks):
        actf(P, rstd[:, t0:t0 + w], ssb[i][:, 0:w], AF.Ln, [ssb[i], C.epsc], [rstd.p(i)], bias=C.epsc[:, 0:1], scale=1.0 / n_feat)
        actf(P, rstd[:, t0:t0 + w], rstd[:, t0:t0 + w], AF.Exp, [rstd.p(i)], [rstd.p(i)], scale=-0.5)
    for kc in range(kcn):
        stt(P, out[:, kc, ocol:ocol + tot], h[:, kc, 0:tot], gcols[:, kc:kc + 1], rstd[:, 0:tot], ALU.mult, ALU.mult,
            [h.p(kc), rstd, gdep], [out.p(kc)])


def pnorm(C, src, n, gcol, gdep, out_ap, out_dep, extra_scale=1.0, div=128.0):
    P = C.P
    sq = C.sq[0]
    actf(P, sq[:, 0:n], src[:, 0:n], AF.Square, [src], [sq])
    nm = len(C.misc)
    for i, (t0, w) in enumerate(blocks_of(n)):
        bank = C.misc[i % nm]
        mm(P, bank[:, 0:w], C.ones_bf[:, :], sq[:, t0:t0 + w], True, True, [sq, C.ones_bf], [bank])
        actf(P, C.rstd[:, t0:t0 + w], bank[:, 0:w], AF.Ln, [bank, C.epsc], [C.rstd.p(i)], bias=C.epsc[:, 0:1], scale=1.0 / div)
        actf(P, C.rstd[:, t0:t0 + w], C.rstd[:, t0:t0 + w], AF.Exp, [C.rstd.p(i)], [C.rstd.p(i)], scale=-0.5)
    rd = [src, C.rstd] + ([gdep] if gdep is not None else [])
    if gcol is None:
        stt(P, out_ap, src[:, 0:n], extra_scale, C.rstd[:, 0:n], ALU.mult, ALU.mult, rd, [out_dep])
    elif extra_scale != 1.0:
        ts(P, "dve", src[:, 0:n], src[:, 0:n], gcol, extra_scale, ALU.mult, ALU.mult, rd, [src])
        tt(P, "dve", out_ap, src[:, 0:n], C.rstd[:, 0:n], ALU.mult, [src, C.rstd], [out_dep])
    else:
        stt(P, out_ap, src[:, 0:n], gcol, C.rstd[:, 0:n], ALU.mult, ALU.mult, rd, [out_dep])


def row_phase(C, L, io):
    P = C.P
    blks = blocks_of(TH)
    mblks = blocks_of(TM)
    P.push()
    h = P.sb("h", [128, KC, TH], F32)
    hn = P.sb("hn", [128, KC, TH], BF16)
    C.sq = [P.sb("sq", [128, TH], BF16) for _ in range(2)]
    C.rstd = P.sb("rstd", [128, TH], F32)
    vec = P.sb("vec", [128, 64], F32)
    hv = P.sb("hv", [128, 4], F32)
    P.dma("sp", vec[:, :], io["vec"][:, :], writes=[vec], sem="ld_vec")
    P.dma("sp", hv[:, :], io["hv"][:, :], writes=[hv], sem="ld_hv")
    acc = [P.ps("acc", [128, 512], F32) for _ in range(6)]
    C.misc = [P.ps("misc", [128, 512], F32) for _ in range(2)]
    C.misc.append(acc[5])
    accsets = [acc[0:3], acc[3:6]]
    C.accn = 0

    def next_acc():
        C.accn += 1
        return accsets[C.accn % 2]

    kT = P.sb("kT", [128, 4, 256], BF16)
    vtm = P.sb("vtm", [128, 2, 512], BF16)
    P.push()
    wsk = WS(P, "wkv", [(io["xa_wk"][cc], 16, ("wk", cc)) for cc in range(4)], nslot=2)
    memf = P.sb("memf", [128, KC, 256], F32)
    memn = P.sb("memn", [128, KC, 256], BF16)
    wv = P.sb("wv", [128, KC, 512], BF16)
    P.dma("sp", memf[:, :, :], io["memT"][:, :].rearrange("(k p) t -> p k t", p=128), writes=[memf], sem="ld_mem")
    P.dma("pool", wv[:, :, :], io["xa_wv"][:, :, :], writes=[wv], sem="ld_wv")
    rmsnorm_fm(C, memf, KC, [(0, 256)], vec[:, 48:64], vec, memn, D)
    kf = P.sb("kf", [128, 256], F32)
    for cc in range(4):
        wt = wsk.next(("wk", cc))
        bank = next_acc()
        dense(C, wt, 16, memn, [(0, 256)], bank)
        cp(P, "act", kf[:, :], bank[0][:, 0:256], [bank[0]], [kf])
        pnorm(C, kf, 256, hv[:, 1:2], hv, kT[:, cc, :], kT.p(cc))
    for mt in range(2):
        bank = next_acc()
        for kc in range(KC):
            mm(P, bank[0][:, :], memn[:, kc, mt * 128:(mt + 1) * 128], wv[:, kc, :], kc == 0, kc == KC - 1, [memn, wv], [bank[0]])
        cp(P, "act", vtm[:, mt, :], bank[0][:, :], [bank[0]], [vtm.p(mt)])
    P.pop()

    def add_res(cc, banks, bl):
        for i, (t0, w) in enumerate(bl):
            tt(P, "dve", h[:, cc, t0:t0 + w], h[:, cc, t0:t0 + w], banks[i][:, 0:w], ALU.add, [banks[i], h.p(cc)], [h.p(cc)])

    hsrc = io["h_in"]
    hout = io["h_out"]
    og = io["o_gath"]
    for tk in range(2):
        tok0 = tk * TM
        tiles = []
        for cc in range(16):
            tiles.append((io["w_mix_out"][cc], 16, ("mo", cc)))
        for cc in range(4):
            tiles.append((io["xa_wq"][cc], 16, ("wq", cc)))
        for cc in range(16):
            tiles.append((io["xa_wo"][cc], 4, ("wo", cc)))
        for qd in range(4):
            for j in range(11):
                tiles.append((io["ffn_w_in"][qd * 11 + j], 16, ("fg", qd, j)))
                tiles.append((io["ffn_w_in"][44 + qd * 11 + j], 16, ("fv", qd, j)))
            for cc in range(16):
                tiles.append((io["ffn_w_out"][qd, cc], 11, ("fo", qd, cc)))
        P.push()
        ws = WS(P, "wrow", tiles, nslot=4)
        P.dma("sp", h[:, :, 0:TM], hsrc[:, tok0:tok0 + TM].rearrange("(k p) t -> p k t", p=128), reads=[hsrc], writes=[h], sem="ld_h")
        P.dma("sp", h[:, :, TM:TH], hsrc[:, TM - 2:TM].rearrange("(k p) t -> p k t", p=128), reads=[hsrc], writes=[h], sem="ld_hh")
        for kc in range(KC):
            r, c0 = kc // 8, (kc % 8) * 128
            P.dma("sp", hn[:, kc, 0:TM], og[r, c0:c0 + 128, tok0:tok0 + TM], reads=[og], writes=[hn.p(kc)], sem="ld_oa", ring=4)
            P.dma("sp", hn[:, kc, TM:TH], og[r, c0:c0 + 128, TM - 2:TM], reads=[og], writes=[hn.p(kc)], sem="ld_oh", ring=4)

        for cc in range(16):
            wt = ws.next(("mo", cc))
            banks = next_acc()
            dense(C, wt, 16, hn, blks, banks)
            add_res(cc, banks, blks)

        rmsnorm_fm(C, h, KC, blks, vec[:, 0:16], vec, hn, D)
        P.push()
        qT = P.sb("qT", [128, 4, TH], BF16)
        qf = P.sb("qf", [128, TH], F32)
        oxa = P.sb("oxa", [128, 4, TH], BF16)
        pT = [P.sb("pT", [128, 512], BF16) for _ in range(2)]
        rden = P.sb("rden", [128, 512], F32)
        for cc in range(4):
            wt = ws.next(("wq", cc))
            banks = next_acc()
            dense(C, wt, 16, hn, blks, banks)
            for i, (t0, w) in enumerate(blks):
                cp(P, "act", qf[:, t0:t0 + w], banks[i][:, 0:w], [banks[i]], [qf])
            pnorm(C, qf, TH, hv[:, 0:1], hv, qT[:, cc, :], qT.p(cc))
        sc = 128.0 ** -0.5
        for hd in range(4):
            for (t0, w) in blks:
                banks = next_acc()
                for mt in range(2):
                    mm(P, C.misc[mt][:, 0:w], kT[:, hd, mt * 128:(mt + 1) * 128], qT[:, hd, t0:t0 + w], True, True, [kT.p(hd), qT.p(hd)], [C.misc[mt]])
                    actf(P, pT[mt][:, 0:w], C.misc[mt][:, 0:w], AF.Exp, [C.misc[mt]], [pT[mt]], scale=sc)
                for mt in range(2):
                    mm(P, banks[0][:, 0:w], vtm[:, mt, hd * 128:(hd + 1) * 128], pT[mt][:, 0:w], mt == 0, mt == 1, [vtm, pT[mt]], [banks[0]])
                for mt in range(2):
                    mm(P, banks[1][:, 0:w], C.ones_bf[:, :], pT[mt][:, 0:w], mt == 0, mt == 1, [C.ones_bf, pT[mt]], [banks[1]])
                actf(P, rden[:, 0:w], banks[1][:, 0:w], AF.Ln, [banks[1]], [rden])
                actf(P, rden[:, 0:w], rden[:, 0:w], AF.Exp, [rden], [rden], scale=-1.0)
                tt(P, "dve", oxa[:, hd, t0:t0 + w], banks[0][:, 0:w], rden[:, 0:w], ALU.mult, [banks[0], rden], [oxa.p(hd)])
        for cc in range(16):
            wt = ws.next(("wo", cc))
            banks = next_acc()
            dense(C, wt, 4, oxa, blks, banks)
            add_res(cc, banks, blks)
        P.pop()

        rmsnorm_fm(C, h, KC, blks, vec[:, 16:32], vec, hn, D)
        P.push()
        aT = P.sb("aT", [128, 11, TM], BF16)
        ub = [P.sb("ub", [128, TH], F32) for _ in range(4)]
        cg = P.sb("cg", [128, TM], F32)
        cv = P.sb("cv", [128, TM], F32)
        cw = P.sb("cw", [128, 88, 4], F32)
        P.dma("sp", cw[:, :, :], io["ffn_cw"][:, :, :], writes=[cw], sem="ld_cw")
        ubi = 0
        for qd in range(4):
            for j in range(11):
                for which, tagn in ((0, "fg"), (1, "fv")):
                    wt = ws.next((tagn, qd, j))
                    banks = next_acc()
                    dense(C, wt, 16, hn, blks, banks)
                    u = ub[ubi % 4]
                    ubi += 1
                    ch = which * 44 + qd * 11 + j
                    ts(P, "dve", u[:, 0:2], banks[2][:, 0:2], float(tk), None, ALU.mult, None, [banks[2]], [u])
                    cp(P, "act", u[:, 2:514], banks[0][:, 0:512], [banks[0]], [u])
                    cp(P, "act", u[:, 514:1026], banks[1][:, 0:512], [banks[1]], [u])
                    dst = cg if which == 0 else cv
                    ts(P, "dve", dst[:, :], u[:, 0:TM], cw[:, ch, 0:1], cw[:, ch, 3:4], ALU.mult, ALU.add, [u, cw], [dst])
                    stt(P, dst[:, :], u[:, 1:TM + 1], cw[:, ch, 1:2], dst[:, :], ALU.mult, ALU.add, [u, cw, dst], [dst])
                    stt(P, dst[:, :], u[:, 2:TM + 2], cw[:, ch, 2:3], dst[:, :], ALU.mult, ALU.add, [u, cw, dst], [dst])
                actf(P, cg[:, :], cg[:, :], AF.Silu, [cg], [cg])
                tt(P, "pool", aT[:, j, :], cg[:, :], cv[:, :], ALU.mult, [cg, cv], [aT.p(j)])
            for cc in range(16):
                wt = ws.next(("fo", qd, cc))
                banks = next_acc()
                dense(C, wt, 11, aT, mblks, banks)
                add_res(cc, banks, mblks)
        P.pop()

        for kc in range(KC):
            P.dma("sp", hout[kc * 128:(kc + 1) * 128, tok0:tok0 + TM], h[:, kc, 0:TM], reads=[h.p(kc)], writes=[hout], sem="st_h", ring=4)
        P.pop()
    P.pop()


def row_phase8(C, L, io, mid=None):
    P = C.P
    blks = blocks_of(TH)
    mblks = blocks_of(TM)
    P.push()
    h = P.sb("h", [128, KC, TH], F32)
    hn = P.sb("hn", [128, KC, TH], BF16)
    C.sq = [P.sb("sq", [128, TH], BF16) for _ in range(2)]
    C.rstd = P.sb("rstd", [128, TH], F32)
    vec = P.sb("vec", [128, 64], F32)
    hv = P.sb("hv", [128, 4], F32)
    P.dma("sp", vec[:, :], io["vec"][:, :], writes=[vec], sem="ld_vec")
    P.dma("sp", hv[:, :], io["hv"][:, :], writes=[hv], sem="ld_hv")
    acc = [P.ps("acc", [128, 512], F32) for _ in range(6)]
    C.misc = [P.ps("misc", [128, 512], F32) for _ in range(2)]
    C.misc.append(acc[5])
    accsets = [acc[0:3], acc[3:6]]
    C.accn = 0

    def next_acc():
        C.accn += 1
        return accsets[C.accn % 2]

    tiles = []
    for cc in range(4):
        tiles.append((io["xa_wk"][cc], 16, ("wk", cc)))
    for cc in range(16):
        tiles.append((io["w_mix_out"][cc], 16, ("mo", cc)))
    for cc in range(4):
        tiles.append((io["xa_wq"][cc], 16, ("wq", cc)))
    for cc in range(16):
        tiles.append((io["xa_wo"][cc], 4, ("wo", cc)))
    for qd in range(4):
        for j in range(11):
            tiles.append((io["ffn_w_in"][qd * 11 + j], 16, ("fg", qd, j)))
            tiles.append((io["ffn_w_in"][44 + qd * 11 + j], 16, ("fv", qd, j)))
        for cc in range(16):
            tiles.append((io["ffn_w_out"][qd, cc], 11, ("fo", qd, cc)))
    ws = WS(P, "wrow", tiles, nslot=4)

    hsrc = io["h_in"]
    hhalo = io["h_halo"]
    P.dma("sp", h[:, :, 0:TM], hsrc[:, :].rearrange("(k p) t -> p k t", p=128), reads=[hsrc], writes=[h], sem="ld_h")
    P.dma("sp", h[:, :, TM:TH], hhalo[:, :].rearrange("(k p) t -> p k t", p=128), reads=[hhalo], writes=[h], sem="ld_hh")
    kT = P.sb("kT", [128, 4, 256], BF16)
    vtm = P.sb("vtm", [128, 2, 512], BF16)
    P.push()
    memf = P.sb("memf", [128, KC, 256], F32)
    memn = P.sb("memn", [128, KC, 256], BF16)
    wv = P.sb("wv", [128, KC, 512], BF16)
    P.dma("sp", memf[:, :, :], io["memT"][:, :].rearrange("(k p) t -> p k t", p=128), writes=[memf], sem="ld_mem")
    P.dma("pool", wv[:, :, :], io["xa_wv"][:, :, :], writes=[wv], sem="ld_wv")
    rmsnorm_fm(C, memf, KC, [(0, 256)], vec[:, 48:64], vec, memn, D)
    kf = P.sb("kf", [128, 256], F32)
    for cc in range(4):
        wt = ws.next(("wk", cc))
        bank = next_acc()
        dense(C, wt, 16, memn, [(0, 256)], bank)
        cp(P, "act", kf[:, :], bank[0][:, 0:256], [bank[0]], [kf])
        pnorm(C, kf, 256, hv[:, 1:2], hv, kT[:, cc, :], kT.p(cc))
    for mt in range(2):
        bank = next_acc()
        for kc in range(KC):
            mm(P, bank[0][:, :], memn[:, kc, mt * 128:(mt + 1) * 128], wv[:, kc, :], kc == 0, kc == KC - 1, [memn, wv], [bank[0]])
        cp(P, "act", vtm[:, mt, :], bank[0][:, :], [bank[0]], [vtm.p(mt)])
    P.pop()

    ws.prefetch()
    if mid is not None:
        mid()
    og = io["o_gath"]
    for r in range(2):
        P.dma2s("sp", hn[:, r * 8:(r + 1) * 8, 0:TM], og[r, :, 0:TM].rearrange("(k p) t -> p k t", p=128),
                og[r, :, TM:S].rearrange("(k p) t -> p k t", p=128), reads=[og], writes=[hn], sem="ld_oa%d" % r)
        P.dma("sp", hn[:, r * 8:(r + 1) * 8, TM:TH], og[r, :, TM - 2:TM].rearrange("(k p) t -> p k t", p=128), reads=[og], writes=[hn], sem="ld_oh%d" % r)

    def add_res(cc, banks, bl):
        for i, (t0, w) in enumerate(bl):
            tt(P, "dve", h[:, cc, t0:t0 + w], h[:, cc, t0:t0 + w], banks[i][:, 0:w], ALU.add, [banks[i], h.p(cc)], [h.p(cc)])

    for cc in range(16):
        wt = ws.next(("mo", cc))
        banks = next_acc()
        dense(C, wt, 16, hn, blks, banks)
        add_res(cc, banks, blks)

    rmsnorm_fm(C, h, KC, blks, vec[:, 0:16], vec, hn, D)
    P.push()
    qT = P.sb("qT", [128, 4, TH], BF16)
    qf = P.sb("qf", [128, TH], F32)
    oxa = P.sb("oxa", [128, 4, TH], BF16)
    pT4 = [P.sb("pT", [128, 512], BF16) for _ in range(4)]
    rden2 = [P.sb("rden", [128, 512], F32) for _ in range(2)]
    for cc in range(4):
        wt = ws.next(("wq", cc))
        banks = next_acc()
        dense(C, wt, 16, hn, blks, banks)
        for i, (t0, w) in enumerate(blks):
            cp(P, "act", qf[:, t0:t0 + w], banks[i][:, 0:w], [banks[i]], [qf])
        pnorm(C, qf, TH, hv[:, 0:1], hv, qT[:, cc, :], qT.p(cc))
    sc = 128.0 ** -0.5
    xsteps = [(hd, t0, w) for hd in range(4) for (t0, w) in blks]

    def xscore(i):
        hd, t0, w = xsteps[i]
        for mt in range(2):
            p_ = pT4[2 * (i % 2) + mt]
            mm(P, C.misc[mt][:, 0:w], kT[:, hd, mt * 128:(mt + 1) * 128], qT[:, hd, t0:t0 + w], True, True, [kT.p(hd), qT.p(hd)], [C.misc[mt]])
            actf(P, p_[:, 0:w], C.misc[mt][:, 0:w], AF.Exp, [C.misc[mt]], [p_], scale=sc)

    def xpv(i):
        hd, t0, w = xsteps[i]
        banks = next_acc()
        ps_ = [pT4[2 * (i % 2)], pT4[2 * (i % 2) + 1]]
        for mt in range(2):
            mm(P, banks[0][:, 0:w], vtm[:, mt, hd * 128:(hd + 1) * 128], ps_[mt][:, 0:w], mt == 0, mt == 1, [vtm, ps_[mt]], [banks[0]])
        for mt in range(2):
            mm(P, banks[1][:, 0:w], C.ones_bf[:, :], ps_[mt][:, 0:w], mt == 0, mt == 1, [C.ones_bf, ps_[mt]], [banks[1]])
        rd_ = rden2[i % 2]
        actf(P, rd_[:, 0:w], banks[1][:, 0:w], AF.Ln, [banks[1]], [rd_])
        actf(P, rd_[:, 0:w], rd_[:, 0:w], AF.Exp, [rd_], [rd_], scale=-1.0)
        tt(P, "dve", oxa[:, hd, t0:t0 + w], banks[0][:, 0:w], rd_[:, 0:w], ALU.mult, [banks[0], rd_], [oxa.p(hd)])

    xscore(0)
    for i in range(len(xsteps)):
        if i + 1 < len(xsteps):
            xscore(i + 1)
        xpv(i)
    for cc in range(16):
        wt = ws.next(("wo", cc))
        banks = next_acc()
        dense(C, wt, 4, oxa, blks, banks)
        add_res(cc, banks, blks)
    P.pop()

    rmsnorm_fm(C, h, KC, blks, vec[:, 16:32], vec, hn, D)
    P.push()
    aT = P.sb("aT", [128, 11, TM], BF16)
    ub = [P.sb("ub", [128, TH], F32) for _ in range(4)]
    cg = P.sb("cg", [128, TM], F32)
    cv = P.sb("cv", [128, TM], F32)
    cw = P.sb("cw", [128, 88, 4], F32)
    P.dma("sp", cw[:, :, :], io["ffn_cw"][:, :, :], writes=[cw], sem="ld_cw")
    ubi = 0
    for qd in range(4):
        for j in range(11):
            res = []
            for which, tagn in ((0, "fg"), (1, "fv")):
                wt = ws.next((tagn, qd, j))
                banks = next_acc()
                dense(C, wt, 16, hn, blks, banks)
                u = ub[ubi % 4]
                ubi += 1
                ch = which * 44 + qd * 11 + j
                ts(P, "dve", u[:, 0:2], banks[2][:, 0:2], C.flags[:, 1:2], None, ALU.mult, None, [banks[2], C.flags], [u])
                cp(P, "act", u[:, 2:514], banks[0][:, 0:512], [banks[0]], [u])
                cp(P, "act", u[:, 514:1026], banks[1][:, 0:512], [banks[1]], [u])
                dst = cg if which == 0 else cv
                ts(P, "dve", dst[:, :], u[:, 0:TM], cw[:, ch, 0:1], cw[:, ch, 3:4], ALU.mult, ALU.add, [u, cw], [dst])
                stt(P, dst[:, :], u[:, 1:TM + 1], cw[:, ch, 1:2], dst[:, :], ALU.mult, ALU.add, [u, cw, dst], [dst])
                stt(P, dst[:, :], u[:, 2:TM + 2], cw[:, ch, 2:3], dst[:, :], ALU.mult, ALU.add, [u, cw, dst], [dst])
            actf(P, cg[:, :], cg[:, :], AF.Silu, [cg], [cg])
            tt(P, "pool", aT[:, j, :], cg[:, :], cv[:, :], ALU.mult, [cg, cv], [aT.p(j)])
        for cc in range(16):
            wt = ws.next(("fo", qd, cc))
            banks = next_acc()
            dense(C, wt, 11, aT, mblks, banks)
            add_res(cc, banks, mblks)
    P.pop()

    hout = io["h_out"]
    for kc in range(KC):
        P.dma("sp", hout[kc * 128:(kc + 1) * 128, :], h[:, kc, 0:TM], reads=[h.p(kc)], writes=[hout], sem="st_h", ring=4)
    if L == 0:
        tsh = io["h_tail"]
        P.dma2("sp", tsh.h[0].rearrange("(k p) t -> p k t", p=128), tsh.h[1].rearrange("(k p) t -> p k t", p=128), h[:, :, TM - 2:TM], reads=[h], writes=[tsh], sem="st_ht")
        rmsnorm_fm(C, h, KC, mblks, vec[:, 32:48], vec, hn, D)
        hn1 = io["hn1_out"]
        P.dma2("sp", hn1.h[0].rearrange("(k p) t -> p k t", p=128), hn1.h[1].rearrange("(k p) t -> p k t", p=128), hn[:, :, 0:TM], reads=[hn], writes=[hn1], sem="st_hn")
    P.pop()


def make_ctx(P, io):
    C = Ctx()
    C.P = P
    C.ones_bf = P.sb("ones_bf", [128, 128], BF16)
    memset(P, "pool", C.ones_bf[:, :], 1.0, [C.ones_bf])
    C.cf = P.sb("cf", [128, io["consts"].h.shape[1]], F32)
    P.dma("sp", C.cf[:, :], io["consts"][:, :], writes=[C.cf], sem="ld_c")
    C.flags = P.sb("flags", [128, 2], F32)
    P.dma("sp", C.flags[:, :], io["flags"][:, :], writes=[C.flags], sem="ld_f")
    C.epsc = P.sb("epsc", [128, 1], F32)
    memset(P, "pool", C.epsc[:, :], EPS, [C.epsc])
    return C


def tile_w(W, kcn=None):
    K, N = W.shape
    return np.ascontiguousarray(W.reshape(K // 128, 128, N // 128, 128).transpose(2, 1, 0, 3))


def cols128(v):
    return np.ascontiguousarray(v.reshape(-1, 128).T)


def gath_rows(hh):
    return list(range(512 * hh, 512 * hh + 512)) + list(range(1024 + 512 * hh, 1024 + 512 * hh + 512))


def prep_row(inp, L, b, hh):
    d = {}
    s = "_r%d" % L
    d["vec" + s] = np.concatenate([cols128(inp["norm_mem"][L]), cols128(inp["norm_ffn"][L]),
                                   cols128(inp["norm_mix"][1]), cols128(inp["mem_norm"])], axis=1).astype(np.float32)
    hv = np.zeros((128, 4), np.float32)
    hv[:, 0] = inp["xa_qnorm"][L]
    hv[:, 1] = inp["xa_knorm"][L]
    d["hv" + s] = hv
    d["xa_wk" + s] = tile_w(inp["xa_wk"][L])
    d["xa_wq" + s] = tile_w(inp["xa_wq"][L])
    d["xa_wv" + s] = np.ascontiguousarray(inp["xa_wv"][L].reshape(16, 128, 512).transpose(1, 0, 2))
    d["xa_wo" + s] = tile_w(inp["xa_wo"][L])
    wmo = inp["ab_w_out"][0] if L == 0 else inp["cd_w_out"][0]
    d["w_mix_out" + s] = tile_w(wmo[gath_rows(0) + gath_rows(1)])
    d["ffn_w_in" + s] = tile_w(inp["ffn_w_in"][L])
    d["ffn_w_out" + s] = np.ascontiguousarray(inp["ffn_w_out"][L].reshape(4, 11, 128, 16, 128).transpose(0, 3, 2, 1, 4))
    cw = np.zeros((128, 88, 4), np.float32)
    cw[:, :, 0:3] = inp["ffn_conv_w"][L].reshape(3, 88, 128).transpose(2, 1, 0)
    cw[:, :, 3] = inp["ffn_conv_b"][L].reshape(88, 128).T
    d["ffn_cw" + s] = cw
    return d


def decl_row(P, L, ext):
    s = "_r%d" % L
    io = {}

    def din(key, shape, dt=F32):
        io[key] = P.dram(key + s, shape, dt, kind="ExternalInput")
    din("vec", [128, 64])
    din("hv", [128, 4])
    din("xa_wk", [4, 128, 16, 128])
    din("xa_wq", [4, 128, 16, 128])
    din("xa_wv", [128, 16, 512])
    din("xa_wo", [16, 128, 4, 128])
    din("w_mix_out", [16, 128, 16, 128])
    din("ffn_w_in", [88, 128, 16, 128])
    din("ffn_w_out", [4, 16, 128, 11, 128])
    din("ffn_cw", [128, 88, 4])
    return io


CF_TRI, CF_NEGL, CF_NEGU, CF_ID, CF_MU, CF_ROT, CF_SEL, CF_INV, CF_W = 0, 128, 256, 384, 512, 640, 768, 1792, 1793


def make_consts():
    cf = np.zeros((128, CF_W), np.float32)
    p = np.arange(128)[:, None]
    j = np.arange(128)[None, :]
    cf[:, CF_TRI:CF_TRI + 128] = (p <= j)
    cf[:, CF_NEGL:CF_NEGL + 128] = np.where(p > j, 0.0, NEG)
    cf[:, CF_NEGU:CF_NEGU + 128] = np.where(j >= p, 0.0, NEG)
    cf[:, CF_ID:CF_ID + 128] = (p == j)
    cf[:, CF_MU:CF_MU + 128] = (j >= p)
    rot = np.zeros((128, 128), np.float32)
    for q in range(16):
        rot[q + 16, q] = -1.0
        rot[q, q + 16] = 1.0
    cf[:, CF_ROT:CF_ROT + 128] = rot
    sel = np.zeros((128, 8, 128), np.float32)
    for n in range(8):
        sel[n, n, :] = -NEG
    cf[:, CF_SEL:CF_SEL + 1024] = sel.reshape(128, 1024)
    import math
    inv = np.exp(-math.log(500000.0) * np.arange(0, 32, 2, dtype=np.float32) / 32).astype(np.float32)
    cf[0:16, CF_INV] = inv
    cf[16:32, CF_INV] = inv
    return cf


def mix_common(C):
    P = C.P
    C.hnT = P.sb("hnT", [128, KC, S], BF16)
    C.sq = [P.sb("sq", [128, S], BF16)]
    C.rstd = P.sb("rstd", [128, S], F32)
    C.acc = [P.ps("acc", [128, 512], F32) for _ in range(4)]
    C.pf = [P.ps("pf", [128, 512], F32) for _ in range(2)]
    C.pb = [P.ps("pb", [128, 1024], BF16) for _ in range(2)]
    C.misc = C.pf
    C.fbanks = [C.pf[0], C.pf[1], C.acc[2], C.acc[3]]
    C.fsn = 0
    C.bsn = 0
    C.accn = 0
    C.ident_bf = P.sb("identb", [128, 128], BF16)
    cp(P, "dve", C.ident_bf[:, :], C.cf[:, CF_ID:CF_ID + 128], [C.cf], [C.ident_bf])
    C.mu_bf = P.sb("mub", [128, 128], BF16)
    cp(P, "dve", C.mu_bf[:, :], C.cf[:, CF_MU:CF_MU + 128], [C.cf], [C.mu_bf])


def fs(C):
    C.fsn += 1
    banks = C.fbanks
    i = C.fsn % (4 * len(banks))
    t = banks[i % len(banks)]
    s = i // len(banks)
    return t[:, s * 128:(s + 1) * 128], t


def bs(C):
    C.bsn += 1
    i = C.bsn % 8
    t = C.pb[i % 2]
    s = i // 2
    return t[:, s * 128:(s + 1) * 128], t


def acc2(C):
    C.accn += 1
    return C.acc[0:2] if C.accn % 2 else C.acc[2:4]


def conv_fm_g(C, wt, cwap, cwdep, ntap, bias_ap, y):
    P = C.P
    xp = C.xpad
    for half in range(2):
        banks = acc2(C)
        dense(C, wt, 16, C.hnT, [(half * 1024, 512), (half * 1024 + 512, 512)], banks)
        for i in range(2):
            c0 = 3 + half * 1024 + i * 512
            cp(P, "act", xp[:, c0:c0 + 512], banks[i][:, 0:512], [banks[i]], [xp])
        yield
    if bias_ap is None:
        ts(P, "dve", y[:, :], xp[:, 0:S], cwap[:, 0:1], None, ALU.mult, None, [xp, cwdep], [y])
    else:
        ts(P, "dve", y[:, :], xp[:, 0:S], cwap[:, 0:1], bias_ap, ALU.mult, ALU.add, [xp, cwdep], [y])
    for j in range(1, ntap):
        stt(P, y[:, :], xp[:, j:j + S], cwap[:, j:j + 1], y[:, :], ALU.mult, ALU.add, [xp, cwdep, y], [y])
    actf(P, y[:, :], y[:, :], AF.Silu, [y], [y])
    yield


def conv_fm(C, wt, cwap, cwdep, ntap, bias_ap, y):
    for _ in conv_fm_g(C, wt, cwap, cwdep, ntap, bias_ap, y):
        pass


def proj_plain_g(C, wt, y, fixed_banks=None):
    P = C.P
    for half in range(2):
        banks = fixed_banks if fixed_banks is not None else acc2(C)
        dense(C, wt, 16, C.hnT, [(half * 1024, 512), (half * 1024 + 512, 512)], banks)
        for i in range(2):
            c0 = half * 1024 + i * 512
            cp(P, "act", y[:, c0:c0 + 512], banks[i][:, 0:512], [banks[i]], [y])
        yield


def proj_plain(C, wt, y):
    for _ in proj_plain_g(C, wt, y):
        pass


def interleave(gens, weights=None):
    gens = list(gens)
    wts = dict((id(g_), (weights[i] if weights else 1)) for i, g_ in enumerate(gens))
    while gens:
        for g_ in list(gens):
            for _ in range(wts[id(g_)]):
                try:
                    next(g_)
                except StopIteration:
                    gens.remove(g_)
                    break


def proj_tm(C, w, ncol, tile_i, out_ap, out_dep, eng="act", bank=None, func=None, wdep=None):
    P = C.P
    if bank is None:
        bank = C.acc[tile_i % 4]
    for kc in range(KC):
        mm(P, bank[:, 0:ncol], C.hnT[:, kc, tile_i * 128:(tile_i + 1) * 128], w[:, kc, :], kc == 0, kc == KC - 1, [C.hnT, wdep if wdep is not None else w], [bank])
    if func is not None:
        actf(P, out_ap, bank[:, 0:ncol], func, [bank], [out_dep])
    else:
        cp(P, eng, out_ap, bank[:, 0:ncol], [bank], [out_dep])


def proj_tm4(C, w, g4, out, out_deps, wdep, bank):
    P = C.P
    for j in range(4):
        t = 4 * g4 + j
        for kc in range(KC):
            mm(P, bank[:, j * 128:(j + 1) * 128], C.hnT[:, kc, t * 128:(t + 1) * 128], w[:, kc, :], kc == 0, kc == KC - 1, [C.hnT, wdep], [bank])
    cp(P, "act", out[:, 4 * g4:4 * g4 + 4, :], bank[:, :].rearrange("p (a b) -> p a b", a=4), [bank], out_deps)


SM0 = dict(gcw=0, gnorm=48, scw=49, scb=73, gdtb=79, galog=143, sdtb=207, salog=335, sD=463, snorm=471, nmix=983, W=999)


def load_hn(C, xs, gcols, gdep):
    P = C.P
    P.push()
    hfs = [P.sb("hf", [128, KC, 512], F32) for _ in range(2)]

    def load(tb):
        hf = hfs[tb % 2]
        for q in range(2):
            P.dma("sp", hf[:, q * 8:(q + 1) * 8, :], xs[q * 1024:(q + 1) * 1024, tb * 512:(tb + 1) * 512].rearrange("(k p) t -> p k t", p=128),
                  reads=[xs], writes=[hf], sem="ld_x%d" % (tb % 2), ring=2)
    load(0)
    load(1)
    for tb in range(4):
        rmsnorm_fm(C, hfs[tb % 2], KC, [(0, 512)], gcols, gdep, C.hnT, D, ocol=tb * 512)
        if tb + 2 < 4:
            load(tb + 2)
    P.pop()


def mixer0(C, ios, xs):
    P = C.P
    P.push()
    mix_common(C)
    first = True
    for io in ios:
        P.push()
        sm = P.sb("sm", [128, SM0["W"]], F32)
        P.dma("sp", sm[:, :], io["sm"][:, :], writes=[sm], sem="ld_sm")
        wg = P.sb("wg", [128, KC, 8], BF16)
        wdt = P.sb("wdt", [128, KC, 8], BF16)
        P.dma("pool", wg[:, :, :], io["wg"][:, :, :], writes=[wg], sem="ld_wg")
        P.dma("pool", wdt[:, :, :], io["wdt"][:, :, :], writes=[wdt], sem="ld_wdt")
        ia = [hl * 4 + j for hl in range(4) for j in range(3)] + list(range(16, 22))
        ws = WS(P, "wmix", [(io["wfm"][i], 16, i) for i in ia], nslot=3)
        wsz = WS(P, "wmixz", [(io["wfm"][hl * 4 + 3], 16, hl * 4 + 3) for hl in range(4)], nslot=2)
        if first:
            load_hn(C, xs, sm[:, SM0["nmix"]:SM0["nmix"] + 16], sm)
            first = False
        gdn(C, io, sm, ws, wsz, wg)
        ssd(C, io, sm, ws, wdt)
        P.pop()
    P.pop()


def gdn(C, io, sm, ws, wsz, wg):
    P = C.P
    hnT = C.hnT
    cf = C.cf
    tri = cf[:, CF_TRI:CF_TRI + 128]
    P.push()
    C.xpad = P.sb("xpad", [128, 3 + S], F32)
    memset(P, "pool", C.xpad[:, 0:3], 0.0, [C.xpad])
    y = P.sb("y", [128, S], F32)
    graw = P.sb("graw", [128, 16, 8], F32)
    for t in range(16):
        proj_tm(C, wg, 8, t, graw[:, t, :], graw.p(t))
    beta = P.sb("beta", [128, 16, 4], F32)
    gcol = P.sb("gcol", [128, 16, 4], F32)
    gc = P.sb("gc", [128, 16, 4], F32)
    ngc = P.sb("ngc", [128, 16, 4], F32)
    bexp = P.sb("bexp", [128, 16, 4], F32)
    nea = P.sb("nea", [128, 64], F32)
    actf(P, beta[:, :, :], graw[:, :, 0:4], AF.Sigmoid, [graw], [beta])
    g3 = lambda t_, o: t_[:, o:o + 64].rearrange("p (t h) -> p t h", h=4)
    tt(P, "dve", gcol[:, :, :], graw[:, :, 4:8], g3(sm, SM0["gdtb"]), ALU.add, [graw, sm], [gcol])
    actf(P, gcol[:, :, :], gcol[:, :, :], AF.Exp, [gcol], [gcol])
    actf(P, gcol[:, :, :], gcol[:, :, :], AF.Ln, [gcol], [gcol], bias=1.0)
    actf(P, nea[:, :], sm[:, SM0["galog"]:SM0["galog"] + 64], AF.Exp, [sm], [nea])
    stt(P, gcol[:, :, :], gcol[:, :, :], -1.0, g3(nea, 0), ALU.mult, ALU.mult, [gcol, nea], [gcol])
    for t in range(16):
        ps, pd = fs(C)
        mm(P, ps[:, 0:4], tri, gcol[:, t, :], True, True, [cf, gcol], [pd])
        cp(P, "act", gc[:, t, :], ps[:, 0:4], [pd], [gc.p(t)])
    ts(P, "dve", ngc[:, :, :], gc[:, :, :], -1.0, None, ALU.mult, None, [gc], [ngc])
    actf(P, bexp[:, :, :], gc[:, :, :], AF.Exp, [gc], [bexp])
    tt(P, "dve", bexp[:, :, :], bexp[:, :, :], beta[:, :, :], ALU.mult, [bexp, beta], [bexp])
    QT = P.sb("QT", [128, S], BF16)
    KT = P.sb("KT", [128, S], BF16)
    VT = P.sb("VT", [128, S], BF16)
    qgT = P.sb("qgT", [128, 16, 128], BF16)
    AT = P.sb("AT", [128, 16, 128], BF16)
    kd = P.sb("kd", [128, 16, 128], BF16)
    wT = P.sb("wT", [128, 16, 128], BF16)
    uu = P.sb("uu", [128, 16, 128], BF16)
    egl = P.sb("egl", [128, 16], F32)
    oraw = P.sb("oraw", [128, S], F32)
    Sst = P.sb("Sst", [128, 128], F32)
    Sbf = P.sb("Sbf", [128, 128], BF16)
    GI = 4
    bsets = []
    for _ in range(GI):
        bsets.append(dict(sm=P.sb("sml", [128, 8], F32), EL=P.sb("EL", [128, 128], F32), EU=P.sb("EU", [128, 128], F32),
                          erb=P.sb("erb", [128, 128], F32), t0=P.sb("tmpf", [128, 128], F32), t1=P.sb("tmpf", [128, 128], F32),
                          LU=[P.sb("LU", [128, 2, 128], BF16) for _ in range(3)],
                          Rbf=P.sb("Rbf", [128, 128], BF16),
                          kbg=P.sb("kbg", [128, 128], BF16), vb=P.sb("vb", [128, 128], BF16)))
    vnew = P.sb("vnew", [128, 128], BF16)
    ob = P.sb("ob", [128, S], BF16)
    og = io["o_half"].mine if isinstance(io["o_half"], DynSlot) else io["o_half"]
    ogd = io["o_half"]
    yz = P.sb("yz", [128, S], F32)
    NH = 4

    def stageA(hl):
        cwq = sm[:, SM0["gcw"] + (hl * 3 + 0) * 4:SM0["gcw"] + (hl * 3 + 0) * 4 + 4]
        cwk = sm[:, SM0["gcw"] + (hl * 3 + 1) * 4:SM0["gcw"] + (hl * 3 + 1) * 4 + 4]
        cwv = sm[:, SM0["gcw"] + (hl * 3 + 2) * 4:SM0["gcw"] + (hl * 3 + 2) * 4 + 4]
        yield from conv_fm_g(C, ws.next(hl * 4 + 0), cwq, sm, 4, None, y)
        pnorm(C, y, S, None, None, QT[:, :], QT, extra_scale=128.0 ** -0.5, div=1.0)
        yield
        yield from conv_fm_g(C, ws.next(hl * 4 + 1), cwk, sm, 4, None, y)
        pnorm(C, y, S, None, None, KT[:, :], KT, div=1.0)
        yield
        yield from conv_fm_g(C, ws.next(hl * 4 + 2), cwv, sm, 4, None, y)
        cp(P, "dve", VT[:, :], y[:, :], [y], [VT])

    def stageB(hl):
            def pre(c, B):
                cs = slice(c * 128, (c + 1) * 128)
                sm_c, EL, EU, erb, tmp0, tmp1 = B["sm"], B["EL"], B["EU"], B["erb"], B["t0"], B["t1"]
                LU, Rbf, kbg, vb = B["LU"], B["Rbf"], B["kbg"], B["vb"]
                rb, rbd = fs(C)
                mm(P, rb, gcol[:, c, hl:hl + 1].to_broadcast([128, 128]), tri, True, True, [gcol, cf], [rbd])
                cp(P, "act", sm_c[:, 0:1], rb[:, 127:128], [rbd], [sm_c])
                stt(P, tmp0[:, :], rb, -1.0, cf[:, CF_NEGL:CF_NEGL + 128], ALU.mult, ALU.add, [rbd, cf], [tmp0])
                tt(P, "dve", tmp1[:, :], rb, cf[:, CF_NEGU:CF_NEGU + 128], ALU.add, [rbd, cf], [tmp1])
                actf(P, erb[:, :], rb, AF.Exp, [rbd], [erb])
                yield
                actf(P, egl[:, c:c + 1], sm_c[:, 0:1], AF.Exp, [sm_c], [egl.p(c)])
                actf(P, sm_c[:, 1:2], gc[:, c, hl:hl + 1], AF.Exp, [gc, sm_c], [sm_c], bias=sm_c[:, 0:1], scale=-1.0)
                actf(P, EL[:, :], tmp0[:, :], AF.Exp, [tmp0, gc], [EL], bias=gc[:, c, hl:hl + 1])
                actf(P, EU[:, :], tmp1[:, :], AF.Exp, [tmp1, ngc], [EU], bias=ngc[:, c, hl:hl + 1])
                tt(P, "pool", qgT[:, c, :], QT[:, cs], erb[:, :], ALU.mult, [QT, erb], [qgT.p(c)])
                yield
                kk, kkd = fs(C)
                mm(P, kk, KT[:, cs], KT[:, cs], True, True, [KT], [kkd])
                stt(P, LU[0][:, 0, :], kk, beta[:, c, hl:hl + 1], EL[:, :], ALU.mult, ALU.mult, [kkd, beta, EL], [LU[0].p(0)])
                yield
                ub_, ubd = bs(C)
                tr(P, ub_, LU[0][:, 0, :], C.ident_bf[:, :], [LU[0].p(0), C.ident_bf], [ubd])
                cp(P, "act", LU[0][:, 1, :], ub_, [ubd], [LU[0].p(1)])
                yield
                aa, aad = fs(C)
                mm(P, aa, KT[:, cs], QT[:, cs], True, True, [KT, QT], [aad])
                tt(P, "dve", AT[:, c, :], aa, EU[:, :], ALU.mult, [aad, EU], [AT.p(c)])
                yield
                kt_, ktd = bs(C)
                tr(P, kt_, KT[:, cs], C.ident_bf[:, :], [KT, C.ident_bf], [ktd])
                ts(P, "dve", kbg[:, :], kt_, bexp[:, c, hl:hl + 1], None, ALU.mult, None, [ktd, bexp], [kbg])
                ts(P, "dve", kd[:, c, :], kt_, sm_c[:, 1:2], None, ALU.mult, None, [ktd, sm_c], [kd.p(c)])
                yield
                vt_, vtd = bs(C)
                tr(P, vt_, VT[:, cs], C.ident_bf[:, :], [VT, C.ident_bf], [vtd])
                ts(P, "dve", vb[:, :], vt_, beta[:, c, hl:hl + 1], None, ALU.mult, None, [vtd, beta], [vb])
                tt(P, "dve", Rbf[:, :], cf[:, CF_ID:CF_ID + 128], LU[0][:, 1, :], ALU.subtract, [cf, LU[0].p(1)], [Rbf])
                yield
                li = 0
                for k in range(1, 7):
                    ln = (li + 1) % 3
                    C.fsn += 1
                    bank = C.fbanks[C.fsn % len(C.fbanks)]
                    half = (C.fsn // len(C.fbanks)) % 2
                    sl = bank[:, half * 256:(half + 1) * 256]
                    mm(P, sl[:, 0:128], LU[li][:, 1, :], LU[li][:, 0, :], True, True, [LU[li].p(1), LU[li].p(0)], [bank])
                    if k < 6:
                        mm(P, sl[:, 128:256], LU[li][:, 0, :], LU[li][:, 1, :], True, True, [LU[li].p(1), LU[li].p(0)], [bank])
                        cp(P, "act" if k % 2 else "dve", LU[ln][:, :, :], sl.rearrange("p (a b) -> p a b", a=2), [bank], [LU[ln]])
                    else:
                        cp(P, "act", LU[ln][:, 0, :], sl[:, 0:128], [bank], [LU[ln].p(0)])
                    yield
                    rr, rrd = fs(C)
                    mm(P, rr, LU[ln][:, 0, :], Rbf[:, :], True, True, [LU[ln].p(0), Rbf], [rrd])
                    tt(P, "dve", Rbf[:, :], Rbf[:, :], rr, ALU.add, [Rbf, rrd], [Rbf])
                    yield
                    li = ln
                up, upd = fs(C)
                mm(P, up, Rbf[:, :], vb[:, :], True, True, [Rbf, vb], [upd])
                cp(P, "act", uu[:, c, :], up, [upd], [uu.p(c)])
                yield
                wp, wpd = fs(C)
                mm(P, wp, kbg[:, :], Rbf[:, :], True, True, [kbg, Rbf], [wpd])
                cp(P, "act", wT[:, c, :], wp, [wpd], [wT.p(c)])

            for c0 in range(0, 16, GI):
                gens = [pre(c0 + i, bsets[i]) for i in range(GI)]
                while gens:
                    for g_ in list(gens):
                        try:
                            next(g_)
                        except StopIteration:
                            gens.remove(g_)

    def stageC(hl):
        memset(P, "pool", Sst[:, :], 0.0, [Sst])
        memset(P, "pool", Sbf[:, :], 0.0, [Sbf])
        for c in range(16):
            cs = slice(c * 128, (c + 1) * 128)
            w1, w1d = fs(C)
            mm(P, w1, wT[:, c, :], Sbf[:, :], True, True, [wT.p(c), Sbf], [w1d])
            tt(P, "dve", vnew[:, :], uu[:, c, :], w1, ALU.subtract, [uu.p(c), w1d], [vnew])
            ob_, obd = fs(C)
            mm(P, ob_, Sbf[:, :], qgT[:, c, :], True, False, [Sbf, qgT.p(c)], [obd])
            mm(P, ob_, vnew[:, :], AT[:, c, :], False, True, [vnew, AT.p(c)], [obd])
            cp(P, "act", oraw[:, cs], ob_, [obd], [oraw.p(c)])
            s1, s1d = fs(C)
            mm(P, s1, kd[:, c, :], vnew[:, :], True, True, [kd.p(c), vnew], [s1d])
            stt(P, Sst[:, :], Sst[:, :], egl[:, c:c + 1], s1, ALU.mult, ALU.add, [Sst, egl.p(c), s1d], [Sst])
            cp(P, "act", Sbf[:, :], Sst[:, :], [Sst], [Sbf])
            yield
        if "dbg_oraw" in io:
            P.dma("sp", io["dbg_oraw"][hl * 128:(hl + 1) * 128, :], oraw[:, :], reads=[oraw], writes=[io["dbg_oraw"]], sem="st_dbg", ring=2)
        yield from proj_plain_g(C, wsz.next(hl * 4 + 3), yz)
        actf(P, yz[:, :], yz[:, :], AF.Silu, [yz], [yz])
        pnorm(C, oraw, S, sm[:, SM0["gnorm"]:SM0["gnorm"] + 1], sm, oraw[:, :], oraw)
        yield
        tt(P, "dve", ob[:, :], oraw[:, :], yz[:, :], ALU.mult, [oraw, yz], [ob])
        P.dma("sp", og[hl * 128:(hl + 1) * 128, :], ob[:, :], reads=[ob], writes=[ogd], sem="st_o", ring=2)

    interleave([stageA(0)])
    stageB(0)
    for hl in range(NH):
        C.fbanks = [C.pf[0], C.pf[1]]
        interleave([stageC(hl)] + ([stageA(hl + 1)] if hl + 1 < NH else []))
        C.fbanks = [C.pf[0], C.pf[1], C.acc[2], C.acc[3]]
        if hl + 1 < NH:
            stageB(hl + 1)
    P.pop()


def bc(ap, n):
    return ap.unsqueeze(2).to_broadcast([ap.shape[0], ap.shape[1], n])


def ssd(C, io, sm, ws, wdt):
    P = C.P
    cf = C.cf
    tri = cf[:, CF_TRI:CF_TRI + 128]
    P.push()
    wz = P.sb("wz", [128, KC, 512], BF16)
    P.dma("pool", wz[:, :, :], io["wz"][:, :, :], writes=[wz], sem="ld_wz")
    xT = P.sb("xT", [128, 4, S], BF16)
    BT = P.sb("BT", [128, S], BF16)
    CT = P.sb("CT", [128, S], BF16)
    P.push()
    C.xpad = P.sb("xpad", [128, 3 + S], F32)
    memset(P, "pool", C.xpad[:, 0:3], 0.0, [C.xpad])
    y = P.sb("y", [128, S], F32)
    for j in range(6):
        cw = sm[:, SM0["scw"] + j * 4:SM0["scw"] + j * 4 + 4]
        cb = sm[:, SM0["scb"] + j:SM0["scb"] + j + 1]
        conv_fm(C, ws.next(16 + j), cw, sm, 4, cb, y)
        dst = xT[:, j, :] if j < 4 else (BT[:, :] if j == 4 else CT[:, :])
        dd = xT.p(j) if j < 4 else (BT if j == 4 else CT)
        cp(P, "dve", dst, y[:, :], [y], [dd])
    P.pop()
    draw = P.sb("draw", [128, 16, 8], F32)
    for t in range(16):
        proj_tm(C, wdt, 8, t, draw[:, t, :], draw.p(t))
    dt = P.sb("dt", [128, 16, 8], F32)
    acol = P.sb("acol", [128, 16, 8], F32)
    acs = P.sb("acs", [128, 16, 8], F32)
    nacs = P.sb("nacs", [128, 16, 8], F32)
    eacs = P.sb("eacs", [128, 16, 8], F32)
    nea = P.sb("nea", [128, 128], F32)
    g3 = lambda t_, o: t_[:, o:o + 128].rearrange("p (t h) -> p t h", h=8)
    tt(P, "dve", dt[:, :, :], draw[:, :, :], g3(sm, SM0["sdtb"]), ALU.add, [draw, sm], [dt])
    actf(P, dt[:, :, :], dt[:, :, :], AF.Exp, [dt], [dt])
    actf(P, dt[:, :, :], dt[:, :, :], AF.Ln, [dt], [dt], bias=1.0)
    actf(P, nea[:, :], sm[:, SM0["salog"]:SM0["salog"] + 128], AF.Exp, [sm], [nea])
    stt(P, acol[:, :, :], dt[:, :, :], -1.0, g3(nea, 0), ALU.mult, ALU.mult, [dt, nea], [acol])
    for t in range(16):
        ps, pd = fs(C)
        mm(P, ps[:, 0:8], tri, acol[:, t, :], True, True, [cf, acol], [pd])
        cp(P, "act", acs[:, t, :], ps[:, 0:8], [pd], [acs.p(t)])
    ts(P, "dve", nacs[:, :, :], acs[:, :, :], -1.0, None, ALU.mult, None, [acs], [nacs])
    actf(P, eacs[:, :, :], acs[:, :, :], AF.Exp, [acs], [eacs])
    Sst = P.sb("Sst", [128, 512], F32)
    Sbf = P.sb("Sbf", [128, 512], BF16)
    memset(P, "pool", Sst[:, :], 0.0, [Sst])
    memset(P, "pool", Sbf[:, :], 0.0, [Sbf])
    AO = dict(CBT=P.sb("CBT", [128, 128], F32), rhsB=P.sb("rhsB", [128, 1024], F32), Bm=P.sb("Bm", [128, 1024], F32),
              tuA=P.sb("tuA", [128, 1024], F32), MA=P.sb("MA", [128, 1024], BF16))
    onesf = P.sb("onesf", [128, 128], F32)
    memset(P, "pool", onesf[:, :], 1.0, [onesf])
    BO = dict(ytm=P.sb("ytm", [128, 512], F32), t2=P.sb("t2", [128, 512], F32), ssq=P.sb("ssq", [128, 2], F32),
              ybf=P.sb("ybf", [128, 512], BF16), wcol=P.sb("wcol", [128, 8], F32), elast=P.sb("elast", [128, 8], F32),
              xdtw=P.sb("xdtw", [128, 512], BF16))
    HS = [dict(Btm=P.sb("Btm", [128, 128], BF16), xtm=P.sb("xtm", [128, 512], F32), xdt=P.sb("xdt", [128, 512], BF16),
               rbl=P.sb("rbl", [128, 8], F32), zs=P.sb("zs", [128, 512], F32), yps=C.acc[i]) for i in range(2)]
    oT = P.sb("oT", [128, 4, S], BF16)
    Drow = sm[:, SM0["sD"]:SM0["sD"] + 8]
    grow = sm[:, SM0["snorm"]:SM0["snorm"] + 512]
    def partA(c, H):
        cs = slice(c * 128, (c + 1) * 128)
        CBT = AO["CBT"]
        Btm, xtm, xdt, rbl, zs, yps = H["Btm"], H["xtm"], H["xdt"], H["rbl"], H["zs"], H["yps"]
        cb_, cbd = fs(C)
        mm(P, cb_, BT[:, cs], CT[:, cs], True, True, [BT, CT], [cbd])
        cp(P, "act", CBT[:, :], cb_, [cbd], [CBT])
        bt_, btd = bs(C)
        tr(P, bt_, BT[:, cs], C.ident_bf[:, :], [BT, C.ident_bf], [btd])
        cp(P, "act", Btm[:, :], bt_, [btd], [Btm])
        yield
        xbk = C.pb[c % 2]
        for j in range(4):
            tr(P, xbk[:, 512 + j * 128:512 + (j + 1) * 128], xT[:, j, cs], C.ident_bf[:, :], [xT.p(j), C.ident_bf], [xbk])
        cp(P, "act", xtm[:, :], xbk[:, 512:1024], [xbk], [xtm])
        tt(P, "dve", xdt[:, :].rearrange("p (h d) -> p h d", d=64), xtm[:, :].rearrange("p (h d) -> p h d", d=64),
           bc(dt[:, c, :], 64), ALU.mult, [xtm, dt], [xdt])
        yield
        rhsB, Bm, tuA, MA = AO["rhsB"], AO["Bm"], AO["tuA"], AO["MA"]
        v3 = lambda t_: t_[:, :].rearrange("p (h j) -> p h j", j=128)
        tri3 = tri.unsqueeze(1).to_broadcast([128, 8, 128])
        negu3 = cf[:, CF_NEGU:CF_NEGU + 128].unsqueeze(1).to_broadcast([128, 8, 128])
        tt(P, "dve", v3(rhsB), tri3, bc(acol[:, c, :], 128), ALU.mult, [cf, acol], [rhsB])
        tt(P, "pool", v3(Bm), negu3, bc(acs[:, c, :], 128), ALU.subtract, [cf, acs], [Bm])
        yield
        for hf_ in range(2):
            bank = C.pf[hf_]
            mm(P, bank[:, :], onesf[:, :], rhsB[:, hf_ * 512:(hf_ + 1) * 512], True, True, [onesf, rhsB], [bank])
            cp(P, "act", rbl[:, hf_ * 4:(hf_ + 1) * 4], bank[:, :].rearrange("p (h j) -> p h j", j=128)[:, :, 127], [bank], [rbl])
            tt(P, "dve", tuA[:, hf_ * 512:(hf_ + 1) * 512], bank[:, :], Bm[:, hf_ * 512:(hf_ + 1) * 512], ALU.add, [bank, Bm], [tuA])
            yield
        actf(P, tuA[:, :], tuA[:, :], AF.Exp, [tuA], [tuA])
        cbt3 = CBT[:, :].unsqueeze(1).to_broadcast([128, 8, 128])
        tt(P, "dve", v3(MA), v3(tuA), cbt3, ALU.mult, [tuA, CBT], [MA])
        yield
        for hd in range(8):
            mm(P, yps[:, hd * 64:(hd + 1) * 64], MA[:, hd * 128:(hd + 1) * 128], xdt[:, hd * 64:(hd + 1) * 64], True, True, [MA, xdt], [yps])
        yield
        proj_tm(C, wz, 512, c, zs[:, :], zs, bank=C.acc[3], func=AF.Silu)

    def partB(c, H):
        cs = slice(c * 128, (c + 1) * 128)
        Btm, xtm, xdt, rbl, zs, yps = H["Btm"], H["xtm"], H["xdt"], H["rbl"], H["zs"], H["yps"]
        ytm, t2, ssq, ybf, wcol, elast, xdtw = BO["ytm"], BO["t2"], BO["ssq"], BO["ybf"], BO["wcol"], BO["elast"], BO["xdtw"]
        yoff = upd = C.acc[2]
        mm(P, yoff[:, :], CT[:, cs], Sbf[:, :], True, True, [CT, Sbf], [yoff])
        tt(P, "dve", ytm[:, :].rearrange("p (h d) -> p h d", d=64), yoff[:, :].rearrange("p (h d) -> p h d", d=64),
           bc(eacs[:, c, :], 64), ALU.mult, [yoff, eacs], [ytm])
        tt(P, "dve", ytm[:, :], ytm[:, :], yps[:, :], ALU.add, [ytm, yps], [ytm])
        yield
        tt(P, "dve", wcol[:, :], rbl[:, :], acs[:, c, :], ALU.subtract, [rbl, acs], [wcol])
        actf(P, wcol[:, :], wcol[:, :], AF.Exp, [wcol], [wcol])
        actf(P, elast[:, :], rbl[:, :], AF.Exp, [rbl], [elast])
        tt(P, "dve", xdtw[:, :].rearrange("p (h d) -> p h d", d=64), xdt[:, :].rearrange("p (h d) -> p h d", d=64),
           bc(wcol[:, :], 64), ALU.mult, [xdt, wcol], [xdtw])
        mm(P, upd[:, :], Btm[:, :], xdtw[:, :], True, True, [Btm, xdtw], [upd])
        tt(P, "dve", Sst[:, :].rearrange("p (h d) -> p h d", d=64), Sst[:, :].rearrange("p (h d) -> p h d", d=64),
           bc(elast[:, :], 64), ALU.mult, [Sst, elast], [Sst])
        tt(P, "dve", Sst[:, :], Sst[:, :], upd[:, :], ALU.add, [Sst, upd], [Sst])
        cp(P, "act", Sbf[:, :], Sst[:, :], [Sst], [Sbf])
        yield
        tt(P, "pool", t2[:, :].rearrange("p (h d) -> p h d", d=64), xtm[:, :].rearrange("p (h d) -> p h d", d=64),
           bc(Drow, 64), ALU.mult, [xtm, sm], [t2])
        tt(P, "dve", ytm[:, :], ytm[:, :], t2[:, :], ALU.add, [ytm, t2], [ytm])
        if "dbg_y0" in io:
            P.dma("sp", io["dbg_y0"][cs, :], ytm[:, :], reads=[ytm], writes=[io["dbg_y0"]], sem="st_dbg", ring=2)
        tt(P, "dve", ytm[:, :], ytm[:, :], zs[:, :], ALU.mult, [ytm, zs], [ytm])
        tt(P, "pool", t2[:, :], ytm[:, :], ytm[:, :], ALU.mult, [ytm], [t2])
        yield
        P.op("dve", lambda e, o=ssq[:, 0:1], i=t2[:, :]: e.tensor_reduce(out=o, in_=i, axis=AX.X, op=ALU.add), [t2], [ssq])
        actf(P, ssq[:, 1:2], ssq[:, 0:1], AF.Sqrt, [ssq, C.epsc], [ssq], bias=C.epsc[:, 0:1], scale=1.0 / 512.0)
        recip(P, ssq[:, 1:2], ssq[:, 1:2], [ssq], [ssq])
        stt(P, ybf[:, :], ytm[:, :], ssq[:, 1:2], grow, ALU.mult, ALU.mult, [ytm, ssq, sm], [ybf])
        yield
        ybk = C.pb[(c + 1) % 2]
        for j in range(4):
            tr(P, ybk[:, 512 + j * 128:512 + (j + 1) * 128], ybf[:, j * 128:(j + 1) * 128], C.ident_bf[:, :], [ybf, C.ident_bf], [ybk])
        cp(P, "act", oT[:, :, cs], ybk[:, 512:1024].rearrange("p (a b) -> p a b", a=4), [ybk], [oT])
        yield

    C.fbanks = [C.pf[0], C.pf[1]]
    interleave([partA(0, HS[0])])
    for c in range(16):
        interleave([partB(c, HS[c % 2])] + ([partA(c + 1, HS[(c + 1) % 2])] if c + 1 < 16 else []))
    C.fbanks = [C.pf[0], C.pf[1], C.acc[2], C.acc[3]]
    og = io["o_half"].mine if isinstance(io["o_half"], DynSlot) else io["o_half"]
    for j in range(4):
        P.dma("sp", og[512 + j * 128:512 + (j + 1) * 128, :], oT[:, j, :], reads=[oT.p(j)], writes=[io["o_half"]], sem="st_o", ring=2)
    P.pop()


def prep_mix0(inp, b, hh):
    d = {}
    W = inp["ab_w_in"][0]

    def fm(c0):
        return W[:, c0:c0 + 128].reshape(16, 128, 128).transpose(1, 0, 2)

    def tmw(cols):
        return np.ascontiguousarray(W[:, cols].reshape(16, 128, len(cols)).transpose(1, 0, 2))
    tl = []
    for hl in range(4):
        h = 4 * hh + hl
        tl += [fm(h * 128), fm(1024 + h * 128), fm(2048 + h * 128), fm(3072 + h * 128)]
    for j in range(4):
        tl.append(fm(5136 + hh * 512 + j * 128))
    tl.append(fm(6160 + hh * 128))
    tl.append(fm(6416 + hh * 128))
    d["wfm_m0"] = np.ascontiguousarray(np.stack(tl))
    d["wg_m0"] = tmw([4096 + 4 * hh + i for i in range(4)] + [4104 + 4 * hh + i for i in range(4)])
    d["wz_m0"] = tmw(list(range(4112 + hh * 512, 4112 + hh * 512 + 512)))
    d["wdt_m0"] = tmw(list(range(6672 + 8 * hh, 6672 + 8 * hh + 8)))
    sm = np.zeros((128, SM0["W"]), np.float32)
    gcw = inp["gdn_conv_w"][0]
    for hl in range(4):
        h = 4 * hh + hl
        for wi in range(3):
            c0 = wi * 1024 + h * 128
            sm[:, SM0["gcw"] + (hl * 3 + wi) * 4:SM0["gcw"] + (hl * 3 + wi) * 4 + 4] = gcw[:, c0:c0 + 128].T
    sm[:, SM0["gnorm"]] = inp["gdn_norm"][0]
    mcw = inp["m2_conv_w"][0]
    mcb = inp["m2_conv_b"][0]
    offs = [hh * 512 + j * 128 for j in range(4)] + [1024 + hh * 128, 1280 + hh * 128]
    for j, c0 in enumerate(offs):
        sm[:, SM0["scw"] + j * 4:SM0["scw"] + j * 4 + 4] = mcw[:, c0:c0 + 128].T
        sm[:, SM0["scb"] + j] = mcb[c0:c0 + 128]
    sm[:, SM0["gdtb"]:SM0["gdtb"] + 64] = np.tile(inp["gdn_dt_bias"][0][4 * hh:4 * hh + 4], 16)[None, :]
    sm[:, SM0["galog"]:SM0["galog"] + 64] = np.tile(inp["gdn_a_log"][0][4 * hh:4 * hh + 4], 16)[None, :]
    sm[:, SM0["sdtb"]:SM0["sdtb"] + 128] = np.tile(inp["m2_dt_bias"][0][8 * hh:8 * hh + 8], 16)[None, :]
    sm[:, SM0["salog"]:SM0["salog"] + 128] = np.tile(inp["m2_a_log"][0][8 * hh:8 * hh + 8], 16)[None, :]
    sm[:, SM0["sD"]:SM0["sD"] + 8] = inp["m2_d"][0][8 * hh:8 * hh + 8][None, :]
    sm[:, SM0["snorm"]:SM0["snorm"] + 512] = inp["m2_norm"][0][512 * hh:512 * hh + 512][None, :]
    sm[:, SM0["nmix"]:SM0["nmix"] + 16] = cols128(inp["norm_mix"][0])
    d["sm_m0"] = sm
    return d


def decl_mix0(P):
    io = {}
    io["wfm"] = P.dram("wfm_m0", [22, 128, 16, 128], F32, kind="ExternalInput")
    io["wg"] = P.dram("wg_m0", [128, 16, 8], F32, kind="ExternalInput")
    io["wz"] = P.dram("wz_m0", [128, 16, 512], F32, kind="ExternalInput")
    io["wdt"] = P.dram("wdt_m0", [128, 16, 8], F32, kind="ExternalInput")
    io["sm"] = P.dram("sm_m0", [128, SM0["W"]], F32, kind="ExternalInput")
    return io


SM1 = dict(lb0=0, lb1=4, hnorm=8, qn=9, kn=10, nmix=11, W=27)


def mixer1(C, ios, xs, hn_gath=None):
    P = C.P
    P.push()
    mix_common(C)
    first = True
    if hn_gath is not None:
        for r in range(2):
            P.dma("sp", C.hnT[:, :, r * TM:(r + 1) * TM], hn_gath[r].rearrange("(k p) t -> p k t", p=128), reads=[hn_gath], writes=[C.hnT], sem="ld_hn", ring=2)
        first = False
    for io in ios:
        P.push()
        sm = P.sb("sm1", [128, SM1["W"]], F32)
        P.dma("sp", sm[:, :], io["sm"][:, :], writes=[sm], sem="ld_sm")
        tiles = [(io["wfm"][i], 16, i) for i in range(20)]
        ws = WS(P, "wmix", tiles, nslot=3)
        if first:
            load_hn(C, xs, sm[:, SM1["nmix"]:SM1["nmix"] + 16], sm)
            first = False
        hgrn(C, io, sm, ws)
        moba(C, io, sm, ws)
        P.pop()
    P.pop()


def hgrn(C, io, sm, ws):
    P = C.P
    cf = C.cf
    P.push()
    whi = P.sb("whi", [128, KC, 512], BF16)
    P.dma("pool", whi[:, :, :], io["whi"][:, :, :], writes=[whi], sem="ld_wz")
    lb = P.sb("lb", [128, 8], F32)
    tt(P, "dve", lb[:, 0:4], sm[:, SM1["lb1"]:SM1["lb1"] + 4], sm[:, SM1["lb0"]:SM1["lb0"] + 4], ALU.subtract, [sm], [lb])
    actf(P, lb[:, 0:4], lb[:, 0:4], AF.Sigmoid, [lb], [lb])
    ts(P, "dve", lb[:, 4:8], lb[:, 0:4], -1.0, 1.0, ALU.mult, ALU.add, [lb], [lb])
    ones = P.sb("onesf", [128, S], F32)
    memset(P, "pool", ones[:, :], 1.0, [ones])
    qT = P.sb("qT", [128, S], F32)
    fT = P.sb("fT", [128, S], F32)
    kT = P.sb("kT", [128, S], F32)
    bG = P.sb("bG", [128, S], F32)
    y = P.sb("y", [128, S], F32)
    vi = P.sb("vi", [128, 16, 128], BF16)
    oraw = P.sb("oraw", [128, S], F32)
    ob = P.sb("ob", [128, S], BF16)
    Sst = P.sb("Sst", [128, 128], F32)
    Sbf = P.sb("Sbf", [128, 128], BF16)
    psets = [dict(n3=P.sb("nb", [128, 4], F32), ex=[P.sb("ex", [128, 128], F32) for _ in range(4)], qe=P.sb("qe", [128, 128], BF16),
                  ke=P.sb("ke", [128, 128], BF16), kdT=P.sb("kdT", [128, 128], BF16)) for _ in range(4)]
    qsA = P.sb("qsA", [128, 16, 128], BF16)
    ATA = P.sb("ATA", [128, 16, 128], BF16)
    kdA = P.sb("kdA", [128, 16, 128], BF16)
    edec = P.sb("edec", [128, 16], F32)
    og = io["o_half"].mine if isinstance(io["o_half"], DynSlot) else io["o_half"]
    ogd = io["o_half"]
    import os
    for hl in range(4):
        proj_plain(C, ws.next(hl * 3 + 0), qT)
        actf(P, qT[:, :], qT[:, :], AF.Silu, [qT], [qT])
        proj_plain(C, ws.next(hl * 3 + 1), fT)
        actf(P, fT[:, :], fT[:, :], AF.Sigmoid, [fT], [fT])
        ts(P, "dve", fT[:, :], fT[:, :], lb[:, 4 + hl:5 + hl], lb[:, hl:hl + 1], ALU.mult, ALU.add, [fT, lb], [fT])
        ts(P, "dve", kT[:, :], fT[:, :], -1.0, 1.0, ALU.mult, ALU.add, [fT], [kT])
        actf(P, fT[:, :], fT[:, :], AF.Ln, [fT], [fT])
        P.op("dve", lambda e: e.tensor_tensor_scan(out=bG[:, :], data0=ones[:, :], data1=fT[:, :], initial=0.0, op0=ALU.mult, op1=ALU.add), [ones, fT], [bG])
        for g4 in range(4):
            proj_tm4(C, whi[:, :, hl * 128:(hl + 1) * 128], g4, vi, [vi.p(4 * g4 + j) for j in range(4)], whi, C.acc[g4 % 4])
        def pre(c, B):
            cs = slice(c * 128, (c + 1) * 128)
            mid, end, prev = c * 128 + 63, c * 128 + 127, c * 128 - 1
            n3, ex, qe, ke, kdT = B["n3"], B["ex"], B["qe"], B["ke"], B["kdT"]
            ts(P, "dve", n3[:, 0:1], bG[:, mid:mid + 1], -1.0, None, ALU.mult, None, [bG], [n3])
            if c > 0:
                ts(P, "dve", n3[:, 1:2], bG[:, prev:prev + 1], -1.0, None, ALU.mult, None, [bG], [n3])
            else:
                memset(P, "dve", n3[:, 1:2], 0.0, [n3])
            yield
            actf(P, ex[0][:, :], bG[:, cs], AF.Exp, [bG, n3], [ex[0]], bias=n3[:, 0:1])
            actf(P, ex[1][:, :], bG[:, cs], AF.Exp, [bG], [ex[1]], bias=bG[:, mid:mid + 1], scale=-1.0)
            actf(P, ex[2][:, :], bG[:, cs], AF.Exp, [bG, n3], [ex[2]], bias=n3[:, 1:2])
            actf(P, ex[3][:, :], bG[:, cs], AF.Exp, [bG], [ex[3]], bias=bG[:, end:end + 1], scale=-1.0)
            actf(P, edec[:, c:c + 1], bG[:, end:end + 1], AF.Exp, [bG, n3], [edec.p(c)], bias=n3[:, 1:2])
            yield
            tt(P, "dve", qe[:, :], qT[:, cs], ex[0][:, :], ALU.mult, [qT, ex[0]], [qe])
            tt(P, "dve", ke[:, :], kT[:, cs], ex[1][:, :], ALU.mult, [kT, ex[1]], [ke])
            tt(P, "dve", qsA[:, c, :], qT[:, cs], ex[2][:, :], ALU.mult, [qT, ex[2]], [qsA.p(c)])
            tt(P, "pool", kdT[:, :], kT[:, cs], ex[3][:, :], ALU.mult, [kT, ex[3]], [kdT])
            yield
            a_, ad = fs(C)
            mm(P, a_, ke[:, :], qe[:, :], True, True, [ke, qe], [ad])
            tt(P, "dve", ATA[:, c, :], a_, cf[:, CF_MU:CF_MU + 128], ALU.mult, [ad, cf], [ATA.p(c)])
            yield
            k_, kdd = bs(C)
            tr(P, k_, kdT[:, :], C.ident_bf[:, :], [kdT, C.ident_bf], [kdd])
            cp(P, "act", kdA[:, c, :], k_, [kdd], [kdA.p(c)])

        for c0 in range(0, 16, 4):
            interleave([pre(c0 + i, psets[i]) for i in range(4)])
        memset(P, "pool", Sst[:, :], 0.0, [Sst])
        memset(P, "pool", Sbf[:, :], 0.0, [Sbf])
        for c in range(16):
            cs = slice(c * 128, (c + 1) * 128)
            ob_, obd = fs(C)
            mm(P, ob_, vi[:, c, :], ATA[:, c, :], True, False, [vi.p(c), ATA.p(c)], [obd])
            mm(P, ob_, Sbf[:, :], qsA[:, c, :], False, True, [Sbf, qsA.p(c)], [obd])
            cp(P, "act", oraw[:, cs], ob_, [obd], [oraw.p(c)])
            s1, s1d = fs(C)
            mm(P, s1, kdA[:, c, :], vi[:, c, :], True, True, [kdA.p(c), vi.p(c)], [s1d])
            stt(P, Sst[:, :], Sst[:, :], edec[:, c:c + 1], s1, ALU.mult, ALU.add, [Sst, edec.p(c), s1d], [Sst])
            cp(P, "act", Sbf[:, :], Sst[:, :], [Sst], [Sbf])
        if "dbg_hraw" in io:
            P.dma("sp", io["dbg_hraw"][hl * 128:(hl + 1) * 128, :], oraw[:, :], reads=[oraw], writes=[io["dbg_hraw"]], sem="st_dbg", ring=2)
        proj_plain(C, ws.next(hl * 3 + 2), y)
        actf(P, y[:, :], y[:, :], AF.Silu, [y], [y])
        pnorm(C, oraw, S, sm[:, SM1["hnorm"]:SM1["hnorm"] + 1], sm, oraw[:, :], oraw)
        tt(P, "dve", ob[:, :], oraw[:, :], y[:, :], ALU.mult, [oraw, y], [ob])
        P.dma("sp", og[hl * 128:(hl + 1) * 128, :], ob[:, :], reads=[ob], writes=[ogd], sem="st_o", ring=2)
    P.pop()


def moba(C, io, sm, ws):
    P = C.P
    cf = C.cf
    P.push()
    wmv = P.sb("wmv", [128, KC, 512], BF16)
    cosF = P.sb("cosF", [128, S], F32)
    sinF = P.sb("sinF", [128, S], F32)
    tf = P.sb("tf", [128, S], F32)
    npi = P.sb("npi", [128, 1], F32)
    memset(P, "pool", npi[:, :], -float(np.pi), [npi])
    selb = P.sb("selb", [128, 8, 128], BF16)
    cp(P, "dve", selb[:, :, :], cf[:, CF_SEL:CF_SEL + 1024].rearrange("p (n k) -> p n k", k=128), [cf], [selb])
    maskA = P.sb("maskA", [128, 256], BF16)
    cp(P, "dve", maskA[:, 0:128], cf[:, CF_MU:CF_MU + 128], [cf], [maskA])
    memset(P, "pool", maskA[:, 128:256], 1.0, [maskA])
    y = P.sb("y", [128, S], F32)
    qf = P.sb("qf", [128, S], F32)
    qbs = [P.sb("qb", [128, S], BF16) for _ in range(2)]
    kbs = [P.sb("kb", [128, S], BF16) for _ in range(2)]
    vtms = [P.sb("vtm", [128, 16, 128], BF16) for _ in range(2)]
    nselTs = [P.sb("nselT", [128, S], BF16) for _ in range(2)]
    kmean = P.sb("kmean", [128, 8], F32)
    gate = P.sb("gate", [128, 16, 8], F32)
    mx8 = P.sb("mx8", [128, 8], F32)
    nsel = P.sb("nsel", [128, 128], F32)
    memset(P, "pool", nsel[:, :], 0.0, [nsel])
    pT = [P.sb("pT", [128, 256], BF16) for _ in range(3)]
    rden = P.sb("rden", [128, 256], F32)
    ob = P.sb("ob", [128, S], BF16)
    posi = qf[:, :].bitcast(I32)
    ti = wmv[:, 0:8, :].bitcast(I32).rearrange("p a b -> p (a b)")
    P.dma("sp", posi, io["pos"][0:1, :].partition_broadcast(128), writes=[qf], sem="ld_pos")
    cp(P, "dve", y[:, :], posi, [qf], [y])
    ts(P, "dve", y[:, :], y[:, :], cf[:, CF_INV:CF_INV + 1], 1.0 / (2.0 * np.pi), ALU.mult, ALU.mult, [y, cf], [y])
    for dst, off in ((sinF, 0.0), (cosF, 0.25)):
        ts(P, "dve", tf[:, :], y[:, :], off, None, ALU.add, None, [y], [tf])
        cp(P, "dve", ti, tf[:, :], [tf], [wmv])
        cp(P, "dve", dst[:, :], ti, [wmv], [dst])
        tt(P, "dve", tf[:, :], tf[:, :], dst[:, :], ALU.subtract, [tf, dst], [tf])
        ts(P, "dve", dst[:, :], tf[:, :], 0.0, None, ALU.is_lt, None, [tf], [dst])
        tt(P, "dve", tf[:, :], tf[:, :], dst[:, :], ALU.add, [tf, dst], [tf])
        actf(P, dst[:, :], tf[:, :], AF.Sin, [tf, npi], [dst], bias=npi[:, 0:1], scale=2.0 * float(np.pi))
        ts(P, "dve", dst[:, :], dst[:, :], -1.0, None, ALU.mult, None, [dst], [dst])
    P.dma("pool", wmv[:, :, :], io["wmv"][:, :, :], writes=[wmv], sem="ld_wz")
    og = io["o_half"].mine if isinstance(io["o_half"], DynSlot) else io["o_half"]
    ogd = io["o_half"]
    sc = 128.0 ** -0.5
    pfb = [C.pf[0], C.pf[1]]
    C.fbanks = pfb
    NH = 4
    MBW = 3

    def prep(hl, D):
        qb, kb, vtm, nselT = qbs[D], kbs[D], vtms[D], nselTs[D]
        for which in range(2):
            yield from proj_plain_g(C, ws.next(12 + hl * 2 + which), y, fixed_banks=pfb)
            gcol = sm[:, SM1["qn"] + which:SM1["qn"] + which + 1]
            pnorm(C, y, S, gcol, sm, y[:, :], y)
            yield
            dstf = qf if which == 0 else y
            for (t0, w) in blocks_of(S):
                r_, bank = fs(C)
                mm(P, bank[:, 0:w], cf[:, CF_ROT:CF_ROT + 128], y[:, t0:t0 + w], True, True, [cf, y], [bank])
                tt(P, "dve", tf[:, t0:t0 + w], bank[:, 0:w], sinF[:, t0:t0 + w], ALU.mult, [bank, sinF], [tf.p(t0)])
                yield
            tt(P, "dve", dstf[:, :], y[:, :], cosF[:, :], ALU.mult, [y, cosF], [dstf])
            tt(P, "dve", dstf[:, :], dstf[:, :], tf[:, :], ALU.add, [dstf, tf], [dstf])
            cp(P, "act", (qb if which == 0 else kb)[:, :], dstf[:, :], [dstf], [qb if which == 0 else kb])
            yield
        P.op("dve", lambda e: e.tensor_reduce(out=kmean[:, :], in_=y[:, :].rearrange("p (n k) -> p n k", k=256), axis=AX.X, op=ALU.add), [y], [kmean])
        ts(P, "dve", kmean[:, :], kmean[:, :], 1.0 / 256.0, None, ALU.mult, None, [kmean], [kmean])
        for g4 in range(4):
            proj_tm4(C, wmv[:, :, hl * 128:(hl + 1) * 128], g4, vtm, [vtm.p(4 * g4 + j) for j in range(4)], wmv, pfb[g4 % 2])
            yield
        memset(P, "pool", gate[:, :, :], -1.0e30, [gate])
        for t in range(2, 16):
            jq = t // 2
            g_, gd = fs(C)
            mm(P, g_[:, 0:8], qf[:, t * 128:(t + 1) * 128], kmean[:, :], True, True, [qf, kmean], [gd])
            cp(P, "act", gate[:, t, 0:jq], g_[:, 0:jq], [gd], [gate.p(t)])
            P.op("dve", lambda e, o=mx8[:, :], i=gate[:, t, :]: e.max(out=o, in_=i), [gate.p(t)], [mx8])
            ts(P, "dve", nsel[:, 0:8], gate[:, t, :], mx8[:, 2:3], 1.0, ALU.is_ge, ALU.subtract, [gate.p(t), mx8], [nsel])
            n_, nd = fs(C)
            mm(P, n_[:, :], nsel[:, :], cf[:, CF_ID:CF_ID + 128], True, True, [nsel, cf], [nd])
            cp(P, "act", nselT[:, t * 128:(t + 1) * 128], n_[:, :], [nd], [nselT.p(t)])
            yield

    def attend(hl, D):
        qb, kb, vtm, nselT = qbs[D], kbs[D], vtms[D], nselTs[D]
        steps = []
        for jq in range(8):
            kts = [(kt, "past", kt // 2) for kt in range(2 * jq)] + [(2 * jq, "own0", jq), (2 * jq + 1, "own1", jq)]
            for idx, (kt, kind, n) in enumerate(kts):
                steps.append((jq, kt, kind, n, idx == 0, idx == len(kts) - 1))

        def score(i):
            jq, kt, kind, n, first, last = steps[i]
            q0 = jq * 256
            sb_ = C.acc[i % 2]
            p_ = pT[i % 3]
            c0, w = (128, 128) if kind == "own1" else (0, 256)
            if kind == "past":
                mm(P, sb_[:, 0:256], kb[:, kt * 128:(kt + 1) * 128], qb[:, q0:q0 + 256], True, False, [kb, qb], [sb_])
                mm(P, sb_[:, 0:256], selb[:, n, :], nselT[:, q0:q0 + 256], False, True, [selb, nselT.p(2 * jq), nselT.p(2 * jq + 1)], [sb_])
                actf(P, p_[:, 0:256], sb_[:, 0:256], AF.Exp, [sb_], [p_], scale=sc)
            else:
                mm(P, sb_[:, 0:w], kb[:, kt * 128:(kt + 1) * 128], qb[:, q0 + c0:q0 + c0 + w], True, True, [kb, qb], [sb_])
                actf(P, p_[:, 0:w], sb_[:, 0:w], AF.Exp, [sb_], [p_], scale=sc)
                mk = maskA[:, 0:256] if kind == "own0" else maskA[:, 0:128]
                tt(P, "pool", p_[:, 0:w], p_[:, 0:w], mk, ALU.mult, [p_, maskA], [p_])

        def pv(i):
            jq, kt, kind, n, first, last = steps[i]
            q0 = jq * 256
            oT, den = C.acc[2], C.acc[3]
            p_ = pT[i % 3]
            c0, w = (128, 128) if kind == "own1" else (0, 256)
            mm(P, oT[:, c0:c0 + w], vtm[:, kt, :], p_[:, 0:w], first, last, [vtm.p(kt), p_], [oT])
            mm(P, den[:, c0:c0 + w], C.ones_bf[:, :], p_[:, 0:w], first, last, [C.ones_bf, p_], [den])
            if last:
                actf(P, rden[:, :], den[:, 0:256], AF.Ln, [den], [rden])
                actf(P, rden[:, :], rden[:, :], AF.Exp, [rden], [rden], scale=-1.0)
                tt(P, "dve", ob[:, q0:q0 + 256], oT[:, 0:256], rden[:, :], ALU.mult, [oT, rden], [ob.p(jq)])

        score(0)
        for i in range(len(steps)):
            if i + 1 < len(steps):
                score(i + 1)
            pv(i)
            yield
        P.dma("sp", og[512 + hl * 128:512 + (hl + 1) * 128, :], ob[:, :], reads=[ob], writes=[ogd], sem="st_o", ring=2)

    interleave([prep(0, 0)])
    for hl in range(NH):
        interleave([attend(hl, hl % 2)] + ([prep(hl + 1, (hl + 1) % 2)] if hl + 1 < NH else []), weights=[MBW, 1])
    C.fbanks = [C.pf[0], C.pf[1], C.acc[2], C.acc[3]]
    P.pop()


def prep_mix1(inp, b, hh):
    d = {}
    W = inp["cd_w_in"][0]

    def fm(c0):
        return W[:, c0:c0 + 128].reshape(16, 128, 128).transpose(1, 0, 2)

    def tmw(cols):
        return np.ascontiguousarray(W[:, cols].reshape(16, 128, len(cols)).transpose(1, 0, 2))
    tl = []
    for hl in range(4):
        h = 4 * hh + hl
        tl += [fm(h * 128), fm(1024 + h * 128), fm(3072 + h * 128)]
    for hl in range(4):
        h = 4 * hh + hl
        tl += [fm(4096 + h * 128), fm(5120 + h * 128)]
    d["wfm_m1"] = np.ascontiguousarray(np.stack(tl))
    d["whi_m1"] = tmw(list(range(2048 + hh * 512, 2048 + hh * 512 + 512)))
    d["wmv_m1"] = tmw(list(range(6144 + hh * 512, 6144 + hh * 512 + 512)))
    sm = np.zeros((128, SM1["W"]), np.float32)
    lb = inp["hgrn_lb"]
    sm[:, SM1["lb0"]:SM1["lb0"] + 4] = lb[0][hh * 512:hh * 512 + 512].reshape(4, 128).T
    sm[:, SM1["lb1"]:SM1["lb1"] + 4] = lb[1][hh * 512:hh * 512 + 512].reshape(4, 128).T
    sm[:, SM1["hnorm"]] = inp["hgrn_norm"][0]
    sm[:, SM1["qn"]] = inp["moba_qnorm"][0]
    sm[:, SM1["kn"]] = inp["moba_knorm"][0]
    sm[:, SM1["nmix"]:SM1["nmix"] + 16] = cols128(inp["norm_mix"][1])
    d["sm_m1"] = sm
    d["pos_m1"] = np.ascontiguousarray(inp["positions"][b:b + 1]).astype(np.int32)
    return d


def decl_mix1(P):
    io = {}
    io["wfm"] = P.dram("wfm_m1", [20, 128, 16, 128], F32, kind="ExternalInput")
    io["whi"] = P.dram("whi_m1", [128, 16, 512], F32, kind="ExternalInput")
    io["wmv"] = P.dram("wmv_m1", [128, 16, 512], F32, kind="ExternalInput")
    io["sm"] = P.dram("sm_m1", [128, SM1["W"]], F32, kind="ExternalInput")
    io["pos"] = P.dram("pos_m1", [1, S], I32, kind="ExternalInput")
    return io


NCORES = 8


def pair_sync(C, k, data):
    P = C.P
    fl, nonce = C.fl, C.nonce
    P.dma2("sp", fl.h[0, k:k + 1, :], fl.h[1, k:k + 1, :], nonce[0:1, k:k + 1], reads=list(data), writes=[fl], sem="flag")

    def spin(e):
        hr = P.hhreg(e)
        with e.register("spin_r%d" % k) as r, e.register("spin_x%d" % k) as r2:
            e.load(r2, nonce[0:1, k:k + 1])
            e.reg_mov(r, 1)
            with e.While(r):
                with e.If_eq(hr, 0):
                    e.load(r, fl.h[1, k:k + 1, :])
                with e.Else():
                    e.load(r, fl.h[0, k:k + 1, :])
                e.reg_sub(r, r, r2)
    P.custom("sp", spin, reads=[fl])


def build_all():
    nc = bass.Bass("TRN2", target_bir_lowering=False)
    P = Prog(nc)
    io = {}
    io["consts"] = P.dram("consts", [128, CF_W], F32, kind="ExternalInput")
    io["flags"] = P.dram("flags", [128, 2], F32, kind="ExternalInput")
    nonce = P.dram("nonce", [1, 8], I32, kind="ExternalInput")
    xT = P.dram("xT", [D, S], F32, kind="ExternalInput")
    memT = P.dram("memT", [D, 256], F32, kind="ExternalInput")
    outT = P.dram("outT", [D, TM], F32, kind="ExternalOutput")
    h1 = P.dram("h1", [D, TM], F32)

    def shared(name, shape, dt):
        return DynSlot(P, name, nc.dram_tensor(name, list(shape), dt, kind="Internal", addr_space="Shared").ap())
    o0 = shared("o0_sh", [2, 1024, S], BF16)
    o1 = shared("o1_sh", [2, 1024, S], BF16)
    hn1 = shared("hn1_sh", [2, D, TM], BF16)
    tail = shared("tail_sh", [2, D, 2], F32)
    fl = shared("flag_sh", [2, 8, 1], I32)
    m0 = decl_mix0(P)
    m1 = decl_mix1(P)
    rows = [decl_row(P, L, None) for L in range(2)]
    hin0 = P.dram("h_in_r0", [D, TM], F32, kind="ExternalInput")
    hhalo0 = P.dram("h_halo_r0", [D, 2], F32, kind="ExternalInput")
    C = make_ctx(P, io)
    C.fl, C.nonce = fl, nonce
    oloc = P.dram("o_loc", [1024, S], BF16)
    m0["o_half"] = oloc
    mixer0(C, [m0], xT)
    def xchg0():
        P.dma2("sp", o0.h[0], o0.h[1], oloc[:, :], reads=[oloc], writes=[o0], sem="cp_o")
        pair_sync(C, 0, [o0])
    rows[0].update(memT=memT, h_in=hin0, h_halo=hhalo0, o_gath=o0, h_out=h1, h_tail=tail, hn1_out=hn1)
    row_phase8(C, 0, rows[0], mid=xchg0)
    pair_sync(C, 1, [hn1, tail])
    m1["o_half"] = oloc
    mixer1(C, [m1], None, hn_gath=hn1)
    def xchg1():
        P.dma2("sp", o1.h[0], o1.h[1], oloc[:, :], reads=[oloc], writes=[o1], sem="cp_o")
        pair_sync(C, 2, [o1])
    rows[1].update(memT=memT, h_in=h1, h_halo=T("tail0", tail.h[0]), o_gath=o1, h_out=outT)
    row_phase8(C, 1, rows[1], mid=xchg1)
    P.finish()
    return nc


def kernel(**inp):
    inp = {k: np.asarray(v) for k, v in inp.items()}
    cores = list(range(NCORES))
    consts = make_consts()
    base = (int.from_bytes(os.urandom(4), "little") & 0x3FFFFFF0) | 0x10
    nonce = (base + np.arange(8)).astype(np.int32).reshape(1, 8)
    shared = {"consts": consts, "nonce": nonce}
    for L in range(2):
        shared.update(prep_row(inp, L, 0, 0))
    xT = [np.ascontiguousarray(inp["x"][b].T) for b in range(4)]
    memT = [np.ascontiguousarray(inp["mem"][b].T) for b in range(4)]
    maps = []
    for c in cores:
        b, hh = c // 2, c % 2
        d = dict(shared)
        fl = np.zeros((128, 2), np.float32)
        fl[:, 0] = 1 - hh
        fl[:, 1] = hh
        d["flags"] = fl
        d["xT"] = xT[b]
        d["memT"] = memT[b]
        d["h_in_r0"] = np.ascontiguousarray(xT[b][:, hh * TM:(hh + 1) * TM])
        d["h_halo_r0"] = np.ascontiguousarray(xT[b][:, TM - 2:TM])
        d.update(prep_mix0(inp, b, hh))
        d.update(prep_mix1(inp, b, hh))
        maps.append(d)
    r = run_bass_kernel_spmd(build_all(), maps, core_ids=cores).results
    out = np.empty((4, S, D), np.float32)
    for c in cores:
        b, hh = c // 2, c % 2
        out[b, hh * TM:(hh + 1) * TM, :] = np.asarray(r[c]["outT"]).T
    return out
```
